# Optimizing a Trainium2 kernel written in Bass

```python
import math
import jax
import jax.numpy as jnp
from jax import lax
import numpy as np

D_MODEL = 1024
BATCH = 1
SEQ = 16384
DEPTH = 4

N_MIXERS = 4
GRID_W = 64
EPS = 1e-6
D_FF = 4 * D_MODEL
ROPE_THETA = 10000.0

NA_HEADS = 16
NA_HEAD_DIM = D_MODEL // NA_HEADS
NA_WIN_ROWS = 8
NA_WIN_COLS = 16
NA_Q_ROWS = 2
NA_COL_BLOCK = NA_WIN_COLS
NA_COL_BAND = 2 * NA_WIN_COLS

GDN_HEADS = 8
GDN_DK = D_MODEL // GDN_HEADS
GDN_DV = D_MODEL // GDN_HEADS
GDN_CONV = 5
GDN_CHUNK = 64

MLA_HEADS = 16
MLA_Q_RANK = 3 * D_MODEL // 4
MLA_KV_RANK = D_MODEL // 4
MLA_NOPE = 64
MLA_ROPE = 32
MLA_V = 64
MLA_Q_BLOCK = 128

RET_HEADS = 8
RET_DK = D_MODEL // RET_HEADS
RET_DV = 2 * D_MODEL // RET_HEADS
RET_CHUNK = 128

kernel_name = 'hybrid_bidir_na_gdn_mla_retention'


def rmsnorm(x, w):
    xf = x.astype(jnp.float32)
    y = xf * lax.rsqrt(jnp.mean(xf * xf, axis=-1, keepdims=True) + EPS)
    return (y * w.astype(jnp.float32)).astype(x.dtype)


def _l2norm(x):
    xf = x.astype(jnp.float32)
    return xf * lax.rsqrt(jnp.sum(xf * xf, axis=-1, keepdims=True) + EPS)


def rotary(x, pos):
    half = x.shape[-1] // 2
    inv = 1.0 / (ROPE_THETA ** (jnp.arange(half, dtype=jnp.float32) / half))
    ang = pos.astype(jnp.float32)[:, None] * inv[None, :]
    cos, sin = jnp.cos(ang), jnp.sin(ang)
    xf = x.astype(jnp.float32)
    x1, x2 = xf[..., :half], xf[..., half:]
    return jnp.concatenate([x1 * cos - x2 * sin, x2 * cos + x1 * sin], axis=-1).astype(x.dtype)


def centred_depthwise_conv(x, w):
    c, kw = w.shape
    rhs = jnp.transpose(w)[:, None, :].astype(x.dtype)
    return lax.conv_general_dilated(x, rhs, window_strides=(1,), padding=[(kw // 2, kw // 2)],
                                    dimension_numbers=('NWC', 'WIO', 'NWC'), feature_group_count=c)


def sq_relu_mlp(h, w1, w2):
    a = jax.nn.relu(h @ w1)
    return (a * a) @ w2


def neighbourhood_attention(h, w_qkv, rpb, w_o):
    B, S, _ = h.shape
    H, dh = NA_HEADS, NA_HEAD_DIM
    rows = S // GRID_W
    kr = min(NA_WIN_ROWS, rows)
    nbr = min(NA_Q_ROWS + kr - 1, rows)
    n_qblk = rows // NA_Q_ROWS
    n_cblk = GRID_W // NA_COL_BLOCK
    scale = dh ** -0.5
    qkv = (h @ w_qkv).reshape(B, rows, GRID_W, 3, H, dh)
    q, k, v = qkv[:, :, :, 0], qkv[:, :, :, 1], qkv[:, :, :, 2]
    cb = np.arange(n_cblk)
    kc0 = np.clip(cb * NA_COL_BLOCK - NA_WIN_COLS // 2, 0, GRID_W - NA_COL_BAND)
    kcols = kc0[:, None] + np.arange(NA_COL_BAND)[None, :]
    qcols = cb[:, None] * NA_COL_BLOCK + np.arange(NA_COL_BLOCK)[None, :]
    qc0 = np.clip(qcols - NA_WIN_COLS // 2, 0, GRID_W - NA_WIN_COLS)
    col_ok = (kcols[:, None, :] >= qc0[:, :, None]) & (kcols[:, None, :] < qc0[:, :, None] + NA_WIN_COLS)
    dcol = np.clip(kcols[:, None, :] - qcols[:, :, None] + NA_WIN_COLS - 1, 0, 2 * NA_WIN_COLS - 2)

    def row_block(i):
        r0 = i * NA_Q_ROWS
        rb = jnp.clip(r0 - kr // 2, 0, rows - nbr)
        qb = lax.dynamic_slice_in_dim(q, r0, NA_Q_ROWS, axis=1).reshape(
            B, NA_Q_ROWS, n_cblk, NA_COL_BLOCK, H, dh)
        kb = lax.dynamic_slice_in_dim(k, rb, nbr, axis=1)[:, :, kcols]
        vb = lax.dynamic_slice_in_dim(v, rb, nbr, axis=1)[:, :, kcols]
        qrows = r0 + jnp.arange(NA_Q_ROWS)
        krows = rb + jnp.arange(nbr)
        qr0 = jnp.clip(qrows - kr // 2, 0, rows - kr)
        row_ok = (krows[None, :] >= qr0[:, None]) & (krows[None, :] < qr0[:, None] + kr)
        drow = jnp.clip(krows[None, :] - qrows[:, None] + NA_WIN_ROWS - 1, 0, 2 * NA_WIN_ROWS - 2)
        bias = rpb[:, drow[None, :, None, :, None], dcol[:, None, :, None, :]].astype(jnp.float32)
        mask = row_ok[None, :, None, :, None] & col_ok[:, None, :, None, :]
        s = jnp.einsum('brjchd,bnjkhd->bhjrcnk', qb, kb).astype(jnp.float32) * scale + bias
        p = jax.nn.softmax(jnp.where(mask, s, -jnp.inf), axis=(-2, -1)).astype(vb.dtype)
        o = jnp.einsum('bhjrcnk,bnjkhd->brjchd', p, vb)
        return o.reshape(B, NA_Q_ROWS * GRID_W, H * dh)

    o = lax.map(row_block, jnp.arange(n_qblk))
    o = jnp.moveaxis(o, 0, 1).reshape(B, S, H * dh)
    return (o @ w_o).astype(h.dtype)


def gated_delta_rule_chunked(q, k, v, g, beta):
    f32 = jnp.float32
    B, H, S, dk = k.shape
    dv = v.shape[-1]
    C = GDN_CHUNK
    N = S // C
    q, k, v = [t.astype(f32).reshape(B, H, N, C, t.shape[-1]) for t in (q, k, v)]
    g = g.astype(f32).reshape(B, H, N, C)
    beta = beta.astype(f32).reshape(B, H, N, C)
    G = jnp.cumsum(g, axis=-1)
    incl = np.tril(np.ones((C, C), dtype=bool))
    strict = np.tril(np.ones((C, C), dtype=bool), -1)
    decay = jnp.where(incl, jnp.exp(jnp.where(incl, G[..., :, None] - G[..., None, :], 0.0)), 0.0)
    kb = k * beta[..., None]
    lmat = jnp.where(strict, jnp.einsum('bhncd,bhnmd->bhncm', kb, k) * decay, 0.0)
    tmat = lmat + jnp.eye(C, dtype=f32)
    rhs = jnp.concatenate([v * beta[..., None], kb * jnp.exp(G)[..., None]], axis=-1)
    sol = lax.linalg.triangular_solve(tmat, rhs, left_side=True, lower=True)
    u, w = sol[..., :dv], sol[..., dv:]
    a_intra = jnp.where(incl, jnp.einsum('bhncd,bhnmd->bhncm', q, k) * decay, 0.0)
    q_dec = q * jnp.exp(G)[..., None]
    k_dec = k * jnp.exp(G[..., -1:] - G)[..., None]
    g_last = jnp.exp(G[..., -1])

    def step(state, xs):
        u_c, w_c, qd_c, kd_c, a_c, gl_c = xs
        v_new = u_c - jnp.einsum('bhck,bhkv->bhcv', w_c, state)
        o_c = jnp.einsum('bhck,bhkv->bhcv', qd_c, state) + jnp.einsum('bhcm,bhmv->bhcv', a_c, v_new)
        state = state * gl_c[..., None, None] + jnp.einsum('bhck,bhcv->bhkv', kd_c, v_new)
        return state, o_c

    xs = tuple(jnp.moveaxis(t, 2, 0) for t in (u, w, q_dec, k_dec, a_intra, g_last))
    _, o = lax.scan(step, jnp.zeros((B, H, dk, dv), f32), xs)
    return jnp.moveaxis(o, 0, 2).reshape(B, H, S, dv)


def gated_deltanet(h, w_in, conv_w, a_log_f, a_log_b, dt_bias_f, dt_bias_b, o_norm, w_o):
    f32 = jnp.float32
    B, S, _ = h.shape
    H, dk, dv = GDN_HEADS, GDN_DK, GDN_DV
    n_qkv = 2 * H * dk + H * dv
    proj = h @ w_in
    qkv = jax.nn.silu(centred_depthwise_conv(proj[..., :n_qkv], conv_w))
    z = proj[..., n_qkv:n_qkv + H * dv].reshape(B, S, H, dv)
    b_f, b_b, a_f, a_b = jnp.split(proj[..., n_qkv + H * dv:].astype(f32), 4, axis=-1)

    def heads(t, d):
        return t.reshape(B, S, H, d).transpose(0, 2, 1, 3)

    q = _l2norm(heads(qkv[..., :H * dk], dk)) * dk ** -0.5
    k = _l2norm(heads(qkv[..., H * dk:2 * H * dk], dk))
    v = heads(qkv[..., 2 * H * dk:], dv)

    def direction(a, b, a_log, dt_bias, reverse):
        g = (-jnp.exp(a_log.astype(f32)) * jax.nn.softplus(a + dt_bias.astype(f32))).transpose(0, 2, 1)
        beta = jax.nn.sigmoid(b).transpose(0, 2, 1)
        if reverse:
            qq, kk, vv, gg, bb = [jnp.flip(t, axis=2) for t in (q, k, v, g, beta)]
            return jnp.flip(gated_delta_rule_chunked(qq, kk, vv, gg, bb), axis=2)
        return gated_delta_rule_chunked(q, k, v, g, beta)

    o = direction(a_f, b_f, a_log_f, dt_bias_f, False) + direction(a_b, b_b, a_log_b, dt_bias_b, True)
    o = rmsnorm(o.transpose(0, 2, 1, 3), o_norm) * jax.nn.silu(z.astype(f32))
    return (o.reshape(B, S, H * dv) @ w_o).astype(h.dtype)


def latent_attention(h, w_in, q_norm, w_uq, kv_norm, w_ukv, w_o, pos):
    B, S, _ = h.shape
    H = MLA_HEADS
    dqk = MLA_NOPE + MLA_ROPE
    proj = h @ w_in
    c_q = rmsnorm(proj[..., :MLA_Q_RANK], q_norm)
    c_kv = rmsnorm(proj[..., MLA_Q_RANK:MLA_Q_RANK + MLA_KV_RANK], kv_norm)
    k_rope = rotary(proj[..., MLA_Q_RANK + MLA_KV_RANK:], pos)
    q = (c_q @ w_uq).reshape(B, S, H, dqk).transpose(0, 2, 1, 3)
    q = jnp.concatenate([q[..., :MLA_NOPE], rotary(q[..., MLA_NOPE:], pos)], axis=-1) * dqk ** -0.5
    kv = (c_kv @ w_ukv).reshape(B, S, H, MLA_NOPE + MLA_V).transpose(0, 2, 1, 3)
    k = jnp.concatenate([kv[..., :MLA_NOPE],
                         jnp.broadcast_to(k_rope[:, None], (B, H, S, MLA_ROPE)).astype(kv.dtype)], axis=-1)
    v = kv[..., MLA_NOPE:]
    nb = S // MLA_Q_BLOCK
    qb = q.reshape(B, H, nb, MLA_Q_BLOCK, dqk).transpose(2, 0, 1, 3, 4)

    def attend(q_blk):
        s = jnp.einsum('bhqd,bhkd->bhqk', q_blk, k).astype(jnp.float32)
        p = jax.nn.softmax(s, axis=-1).astype(v.dtype)
        return jnp.einsum('bhqk,bhkd->bhqd', p, v)

    o = lax.map(attend, qb)
    o = o.transpose(1, 0, 3, 2, 4).reshape(B, S, H * MLA_V)
    return (o @ w_o).astype(h.dtype)


def retention_chunked(q, k, v, log_gamma, include_diag):
    f32 = jnp.float32
    B, H, S, dk = q.shape
    dv = v.shape[-1]
    C = RET_CHUNK
    N = S // C
    q, k, v = [jnp.moveaxis(t.astype(f32).reshape(B, H, N, C, t.shape[-1]), 2, 0) for t in (q, k, v)]
    idx = np.arange(C)
    diff = idx[:, None] - idx[None, :]
    lower = diff >= 0 if include_diag else diff > 0
    lg = log_gamma.astype(f32)
    dmask = jnp.where(lower, jnp.exp(jnp.where(lower, diff, 0).astype(f32)[None] * lg[:, None, None]), 0.0)
    idxf = jnp.arange(C, dtype=f32)
    q_dec = jnp.exp((idxf[None, :] + 1.0) * lg[:, None])
    k_dec = jnp.exp((C - 1.0 - idxf[None, :]) * lg[:, None])
    g_chunk = jnp.exp(C * lg)

    def step(state, xs):
        q_c, k_c, v_c = xs
        inner = jnp.einsum('bhcd,bhmd->bhcm', q_c, k_c) * dmask
        o_c = (jnp.einsum('bhcm,bhmv->bhcv', inner, v_c)
               + jnp.einsum('bhcd,bhdv->bhcv', q_c * q_dec[..., None], state))
        state = state * g_chunk[:, None, None] + jnp.einsum('bhmd,bhmv->bhdv', k_c * k_dec[..., None], v_c)
        return state, o_c

    _, o = lax.scan(step, jnp.zeros((B, H, dk, dv), f32), (q, k, v))
    return jnp.moveaxis(o, 0, 2).reshape(B, H, S, dv)


def retention(h, w_in, gn_w, w_o, pos):
    f32 = jnp.float32
    B, S, _ = h.shape
    H, dk, dv = RET_HEADS, RET_DK, RET_DV
    proj = h @ w_in
    q, k, v, gate = jnp.split(proj, [H * dk, 2 * H * dk, 2 * H * dk + H * dv], axis=-1)

    def heads(t, d):
        return t.reshape(B, S, H, d).transpose(0, 2, 1, 3)

    q = rotary(heads(q, dk), pos)
    k = rotary(heads(k, dk), pos) * dk ** -0.5
    v = heads(v, dv)
    log_gamma_f = jnp.log1p(-jnp.exp2(-5.0 - jnp.arange(H, dtype=f32)))
    log_gamma_b = log_gamma_f[::-1]
    o_f = retention_chunked(q, k, v, log_gamma_f, True)
    o_b = jnp.flip(retention_chunked(jnp.flip(q, axis=2), jnp.flip(k, axis=2), jnp.flip(v, axis=2),
                                     log_gamma_b, False), axis=2)
    o = (o_f + o_b).transpose(0, 2, 1, 3)
    mu = jnp.mean(o, axis=-1, keepdims=True)
    var = jnp.mean(jnp.square(o - mu), axis=-1, keepdims=True)
    o = (o - mu) * lax.rsqrt(var + EPS) * gn_w.astype(f32).reshape(H, dv)
    y = jax.nn.silu(gate.astype(f32)) * o.reshape(B, S, H * dv)
    return (y @ w_o).astype(h.dtype)


def setup_inputs(seed: int = 0) -> dict:
    key = jax.random.key(seed)
    ks = list(jax.random.split(key, 32))
    f32 = jnp.float32
    n_na, n_gdn, n_mla, n_ret = [len(range(m, DEPTH, N_MIXERS)) for m in range(N_MIXERS)]

    def dense(k, n, fan_in, fan_out):
        return jax.random.normal(k, (n, fan_in, fan_out), f32) * fan_in ** -0.5

    def gain(k, shape):
        return 1.0 + 0.02 * jax.random.normal(k, shape, f32)

    def a_log(k, n):
        return jnp.log(jax.random.uniform(k, (n, GDN_HEADS), f32, 1.0, 16.0))

    def dt_bias(k, n):
        dt = jnp.exp(jax.random.uniform(k, (n, GDN_HEADS), f32, math.log(1e-3), math.log(1e-1)))
        return dt + jnp.log(-jnp.expm1(-dt))

    gdn_qkv = 2 * GDN_HEADS * GDN_DK + GDN_HEADS * GDN_DV
    return {
        'x': jax.random.normal(ks[0], (BATCH, SEQ, D_MODEL), f32),
        'na_norm': gain(ks[1], (n_na, D_MODEL)),
        'na_w_qkv': dense(ks[2], n_na, D_MODEL, 3 * NA_HEADS * NA_HEAD_DIM),
        'na_rpb': 0.05 * jax.random.normal(ks[3], (n_na, NA_HEADS, 2 * NA_WIN_ROWS - 1, 2 * NA_WIN_COLS - 1), f32),
        'na_w_o': dense(ks[4], n_na, NA_HEADS * NA_HEAD_DIM, D_MODEL),
        'gdn_norm': gain(ks[5], (n_gdn, D_MODEL)),
        'gdn_w_in': dense(ks[6], n_gdn, D_MODEL, gdn_qkv + GDN_HEADS * GDN_DV + 4 * GDN_HEADS),
        'gdn_conv': jax.random.normal(ks[7], (n_gdn, gdn_qkv, GDN_CONV), f32) * GDN_CONV ** -0.5,
        'gdn_a_log_f': a_log(ks[8], n_gdn),
        'gdn_a_log_b': a_log(ks[9], n_gdn),
        'gdn_dt_bias_f': dt_bias(ks[10], n_gdn),
        'gdn_dt_bias_b': dt_bias(ks[11], n_gdn),
        'gdn_o_norm': gain(ks[12], (n_gdn, GDN_DV)),
        'gdn_w_o': dense(ks[13], n_gdn, GDN_HEADS * GDN_DV, D_MODEL),
        'mla_norm': gain(ks[14], (n_mla, D_MODEL)),
        'mla_w_in': dense(ks[15], n_mla, D_MODEL, MLA_Q_RANK + MLA_KV_RANK + MLA_ROPE),
        'mla_q_norm': gain(ks[16], (n_mla, MLA_Q_RANK)),
        'mla_w_uq': dense(ks[17], n_mla, MLA_Q_RANK, MLA_HEADS * (MLA_NOPE + MLA_ROPE)),
        'mla_kv_norm': gain(ks[18], (n_mla, MLA_KV_RANK)),
        'mla_w_ukv': dense(ks[19], n_mla, MLA_KV_RANK, MLA_HEADS * (MLA_NOPE + MLA_V)),
        'mla_w_o': dense(ks[20], n_mla, MLA_HEADS * MLA_V, D_MODEL),
        'ret_norm': gain(ks[21], (n_ret, D_MODEL)),
        'ret_w_in': dense(ks[22], n_ret, D_MODEL, 2 * RET_HEADS * RET_DK + 2 * RET_HEADS * RET_DV),
        'ret_gn': gain(ks[23], (n_ret, RET_HEADS * RET_DV)),
        'ret_w_o': dense(ks[24], n_ret, RET_HEADS * RET_DV, D_MODEL),
        'mlp_norm': gain(ks[25], (DEPTH, D_MODEL)),
        'mlp_w1': dense(ks[26], DEPTH, D_MODEL, D_FF),
        'mlp_w2': dense(ks[27], DEPTH, D_FF, D_MODEL),
        'final_norm': gain(ks[28], (D_MODEL,)),
    }


def reference(x, na_norm, na_w_qkv, na_rpb, na_w_o,
              gdn_norm, gdn_w_in, gdn_conv, gdn_a_log_f, gdn_a_log_b, gdn_dt_bias_f, gdn_dt_bias_b,
              gdn_o_norm, gdn_w_o,
              mla_norm, mla_w_in, mla_q_norm, mla_w_uq, mla_kv_norm, mla_w_ukv, mla_w_o,
              ret_norm, ret_w_in, ret_gn, ret_w_o,
              mlp_norm, mlp_w1, mlp_w2, final_norm):
    S = x.shape[1]
    pos = jnp.arange(S, dtype=jnp.int32)
    h = x
    for i in range(DEPTH):
        m, j = i % N_MIXERS, i // N_MIXERS
        if m == 0:
            h = h + neighbourhood_attention(rmsnorm(h, na_norm[j]), na_w_qkv[j], na_rpb[j], na_w_o[j])
        elif m == 1:
            h = h + gated_deltanet(rmsnorm(h, gdn_norm[j]), gdn_w_in[j], gdn_conv[j], gdn_a_log_f[j],
                                   gdn_a_log_b[j], gdn_dt_bias_f[j], gdn_dt_bias_b[j], gdn_o_norm[j], gdn_w_o[j])
        elif m == 2:
            h = h + latent_attention(rmsnorm(h, mla_norm[j]), mla_w_in[j], mla_q_norm[j], mla_w_uq[j],
                                     mla_kv_norm[j], mla_w_ukv[j], mla_w_o[j], pos)
        else:
            h = h + retention(rmsnorm(h, ret_norm[j]), ret_w_in[j], ret_gn[j], ret_w_o[j], pos)
        h = h + sq_relu_mlp(rmsnorm(h, mlp_norm[i]), mlp_w1[i], mlp_w2[i]).astype(h.dtype)
    return rmsnorm(h, final_norm)
```

```python
import numpy as np
from contextlib import ExitStack
import concourse.bass as bass
import concourse.mybir as mybir
from concourse.bass_utils import run_bass_kernel_spmd

F32 = mybir.dt.float32
BF16 = mybir.dt.bfloat16
AF = mybir.ActivationFunctionType
ALU = mybir.AluOpType
AX = mybir.AxisListType

NCORES = 8
EPS = 1e-6


class Dep:
    __slots__ = ("name", "w", "r", "dsem", "dram")

    def __init__(self, name="", dram=False):
        self.name = name
        self.w = []
        self.r = []
        self.dsem = None
        self.dram = dram


class Prog:
    ENG = ("tensor", "vector", "scalar", "gpsimd", "sync")

    def __init__(self, nc, es, ndsem=90):
        self.nc = nc
        self.es = es
        self.eng = {"tensor": nc.tensor, "vector": nc.vector, "scalar": nc.scalar,
                    "gpsimd": nc.gpsimd, "sync": nc.sync}
        self.sem = {}
        self.cnt = {}
        self.seen = {e: {} for e in self.ENG}
        for e in self.ENG:
            self.sem[e] = es.enter_context(nc.semaphore("s_" + e))
            self.cnt[e] = 0
        self.free_dsem = []
        for i in range(ndsem):
            k = "d%d" % i
            self.sem[k] = es.enter_context(nc.semaphore(k))
            self.cnt[k] = 0
            self.free_dsem.append(k)
        self.stage_dsem = []
        self.nins = 0

    def sbuf(self, st, name, shape, dtype):
        t = st.enter_context(self.nc.sbuf_tensor(name, list(shape), dtype))
        return t, Dep(name)

    def psum(self, st, name, shape, dtype=F32):
        t = st.enter_context(self.nc.psum_tensor(name, list(shape), dtype))
        return t, Dep(name)

    def _dsem(self, d):
        if d.dsem is None:
            d.dsem = self.free_dsem.pop()
            self.stage_dsem.append((d, d.dsem))
        return d.dsem

    def _collect(self, eng, reads, writes, same_engine_sync=True):
        need = {}
        for d in reads:
            for (k, v) in d.w:
                if need.get(k, 0) < v:
                    need[k] = v
        for d in writes:
            if d.dram:
                continue
            for (k, v) in d.w:
                if need.get(k, 0) < v:
                    need[k] = v
            for (k, v) in d.r:
                if need.get(k, 0) < v:
                    need[k] = v
        seen = self.seen[eng]
        e = self.eng[eng]
        for k, v in need.items():
            if k == eng and not same_engine_sync:
                continue
            if seen.get(k, 0) >= v:
                continue
            seen[k] = v
            e.wait_ge(self.sem[k], v)
            self.nins += 1

    def _finish(self, comp, reads, writes):
        for d in writes:
            if d.dram:
                d.w = [c for c in d.w if c[0] != comp[0]] + [comp]
                continue
            d.w = [comp]
            d.r = []
        for d in reads:
            if d in writes:
                continue
            d.r = [c for c in d.r if c[0] != comp[0]] + [comp]

    def op(self, eng, fn, reads=(), writes=()):
        self._collect(eng, reads, writes, same_engine_sync=(eng != "tensor"))
        self.cnt[eng] += 1
        comp = (eng, self.cnt[eng])
        ins = fn(self.eng[eng])
        ins.then_inc(self.sem[eng], 1)
        self.nins += 1
        self._finish(comp, reads, writes)
        return comp

    def dma(self, fns, reads=(), writes=(), semdep=None, queue="sync"):
        if semdep is None:
            cand = [d for d in list(writes) + list(reads) if not d.dram]
            semdep = cand[0]
        key = self._dsem(semdep)
        self._collect(queue, reads, writes, same_engine_sync=False)
        e = self.eng[queue]
        for f in fns:
            f(e).then_inc(self.sem[key], 16)
            self.nins += 1
        self.cnt[key] += 16 * len(fns)
        comp = (key, self.cnt[key])
        self._finish(comp, reads, writes)
        return comp

    def barrier(self):
        for en in self.ENG:
            e = self.eng[en]
            seen = self.seen[en]
            for k, v in self.cnt.items():
                if k == en or v == 0:
                    continue
                if seen.get(k, 0) >= v:
                    continue
                seen[k] = v
                e.wait_ge(self.sem[k], v)
                self.nins += 1
        for (d, k) in self.stage_dsem:
            d.dsem = None
            self.free_dsem.append(k)
        self.stage_dsem = []


def dram(nc, name, shape, dtype, kind="Internal"):
    return nc.dram_tensor(name, list(shape), dtype, kind=kind).ap()


def fm(ap, p=128):
    return ap.rearrange("(c p) t -> p c t", p=p)


def load_weight_bf16(P, st, W_d, K, N, scale_d=None, name="w", piece=2048):
    KC = K // 128
    wb, wb_d = P.sbuf(st, name + "_b", [128, KC, N], BF16)
    deps = [Dep(name + "_b%d" % k) for k in range(KC)]
    sc = None
    if scale_d is not None:
        sc, sc_d = P.sbuf(st, name + "_sc", [128, KC], F32)
        P.dma([lambda e: e.dma_start(out=sc[:], in_=scale_d[:, :])], writes=[sc_d])
    stg = [P.sbuf(st, name + "_stg%d" % i, [128, piece], F32) for i in range(2)]
    it = 0
    for k in range(KC):
        for n0 in range(0, N, piece):
            n1 = min(N, n0 + piece)
            s, s_d = stg[it % 2]
            P.dma([lambda e, s=s, k=k, n0=n0, n1=n1: e.dma_start(out=s[:, 0:n1 - n0], in_=W_d[k * 128:(k + 1) * 128, n0:n1])],
                  writes=[s_d])
            eng = "vector" if it % 2 == 0 else "gpsimd"
            if sc is not None:
                P.op(eng, lambda e, s=s, k=k, n0=n0, n1=n1: e.tensor_scalar(
                    out=wb[:, k, n0:n1], in0=s[:, 0:n1 - n0], scalar1=sc[:, k:k + 1], scalar2=None, op0=ALU.mult),
                    reads=[s_d, sc_d], writes=[deps[k]])
            else:
                P.op(eng, lambda e, s=s, k=k, n0=n0, n1=n1: e.tensor_copy(out=wb[:, k, n0:n1], in_=s[:, 0:n1 - n0]),
                     reads=[s_d], writes=[deps[k]])
            it += 1
    return wb, deps


class Consts:
    def __init__(self, P, st):
        self.ones_bf, self.ones_bf_d = P.sbuf(st, "c_ones_bf", [128, 128], BF16)
        P.op("vector", lambda e: e.memset(self.ones_bf[:], 1.0), writes=[self.ones_bf_d])
        self.ones_f, self.ones_f_d = P.sbuf(st, "c_ones_f", [128, 128], F32)
        P.op("vector", lambda e: e.memset(self.ones_f[:], 1.0), writes=[self.ones_f_d])


def rms_tile(P, C, xt, xt_d, KC, T, D, sq, sq_d, ps, ps_d, rstd, rstd_d, xn, xn_d, eps=EPS):
    P.op("scalar", lambda e: e.activation(out=sq[:, 0:KC, 0:T], in_=xt[:, 0:KC, 0:T], func=AF.Square),
         reads=[xt_d], writes=[sq_d])
    for k in range(KC):
        P.op("tensor", lambda e, k=k: e.matmul(ps[:, 0:T], lhsT=C.ones_bf[:], rhs=sq[:, k, 0:T],
                                                  start=(k == 0), stop=(k == KC - 1)),
             reads=[sq_d, C.ones_bf_d], writes=[ps_d])
    P.op("scalar", lambda e: e.activation(out=rstd[:, 0:T], in_=ps[:, 0:T], func=AF.Sqrt, scale=1.0 / D, bias=eps),
         reads=[ps_d], writes=[rstd_d])
    P.op("vector", lambda e: e.reciprocal(out=rstd[:, 0:T], in_=rstd[:, 0:T]), reads=[rstd_d], writes=[rstd_d])
    for k in range(KC):
        eng = "vector" if k % 2 == 0 else "gpsimd"
        P.op(eng, lambda e, k=k: e.tensor_tensor(out=xn[:, k, 0:T], in0=xt[:, k, 0:T], in1=rstd[:, 0:T], op=ALU.mult),
             reads=[xt_d, rstd_d], writes=[xn_d])


def stage_linear(P, C, name, src_d, K, T, W_d, N, sinks, nscale_d=None, norm=False, src_dtype=F32,
                 tile_T=512, D_norm=None):
    nc = P.nc
    KC = K // 128
    with ExitStack() as st:
        wb, wdeps = load_weight_bf16(P, st, W_d, K, N, scale_d=nscale_d, name=name + "_w")
        xts = [P.sbuf(st, name + "_xt%d" % i, [128, KC, tile_T], src_dtype) for i in range(2)]
        need_cast = norm or (src_dtype != BF16)
        if need_cast:
            xns = [P.sbuf(st, name + "_xn%d" % i, [128, KC, tile_T], BF16) for i in range(2)]
        if norm:
            sq, sq_d = P.sbuf(st, name + "_sq", [128, KC, tile_T], BF16)
            rstd, rstd_d = P.sbuf(st, name + "_rstd", [128, tile_T], F32)
            ps_ss, ps_ss_d = P.psum(st, name + "_psss", [128, 512])
        pss = [P.psum(st, name + "_ps%d" % i, [128, 512]) for i in range(4)]
        psi = [0]
        outs = []
        for si, s in enumerate(sinks):
            ncs = (s["n1"] - s["n0"]) // 128
            if s["kind"] == "fm":
                o = [P.sbuf(st, name + "_o%d_%d" % (si, i), [128, ncs, tile_T], s["dtype"]) for i in range(2)]
                r = None
                if s.get("epi") == "resadd":
                    r = [P.sbuf(st, name + "_r%d_%d" % (si, i), [128, ncs, tile_T], F32) for i in range(2)]
                tmp = None
                if s.get("epi") == "relu2":
                    tmp = [P.sbuf(st, name + "_t%d_%d" % (si, i), [128, tile_T], F32) for i in range(2)]
                outs.append((o, r, tmp))
            elif s.get("aug"):
                o = [P.sbuf(st, name + "_o%d_%d" % (si, i), [128, (s["n1"] - s["n0"]) // 64, 65], s["dtype"])
                     for i in range(2)]
                for (oo, oo_d) in o:
                    P.op("vector", lambda e, oo=oo: e.memset(oo[:], 1.0), writes=[oo_d])
                outs.append((o, None, None))
            else:
                o = [P.sbuf(st, name + "_o%d_%d" % (si, i), [128, s["n1"] - s["n0"]], s["dtype"]) for i in range(2)]
                outs.append((o, None, None))
        ntiles = T // tile_T
        srcv = fm(src_d)

        def load(ti):
            xt, xt_d = xts[ti % 2]
            P.dma([lambda e: e.dma_start(out=xt[:], in_=srcv[:, :, ti * tile_T:(ti + 1) * tile_T])], writes=[xt_d])
            for si, s in enumerate(sinks):
                if s.get("epi") == "resadd":
                    r, r_d = outs[si][1][ti % 2]
                    P.dma([lambda e, r=r, s=s: e.dma_start(
                        out=r[:], in_=fm(s["res"])[:, :, ti * tile_T:(ti + 1) * tile_T])], writes=[r_d])

        load(0)
        cp = 0
        tmcount = 0
        for ti in range(ntiles):
            if ti + 1 < ntiles:
                load(ti + 1)
            xt, xt_d = xts[ti % 2]
            if norm:
                xn, xn_d = xns[ti % 2]
                rms_tile(P, C, xt, xt_d, KC, tile_T, D_norm or K, sq, sq_d, ps_ss, ps_ss_d, rstd, rstd_d, xn, xn_d)
            elif need_cast:
                xn, xn_d = xns[ti % 2]
                for k in range(KC):
                    eng = ("vector", "gpsimd")[k % 2]
                    P.op(eng, lambda e, k=k: e.tensor_copy(out=xn[:, k, :], in_=xt[:, k, :]), reads=[xt_d], writes=[xn_d])
            else:
                xn, xn_d = xt, xt_d
            for si, s in enumerate(sinks):
                n0, n1 = s["n0"], s["n1"]
                if s["kind"] == "fm":
                    (ob, rb, tb) = outs[si]
                    o, o_d = ob[ti % 2]
                    ncs = (n1 - n0) // 128
                    for c in range(ncs):
                        ps, ps_d = pss[psi[0] % 4]
                        psi[0] += 1
                        for k in range(KC):
                            P.op("tensor", lambda e, k=k, c=c, ps=ps: e.matmul(
                                ps[:, 0:tile_T], lhsT=wb[:, k, n0 + c * 128:n0 + (c + 1) * 128], rhs=xn[:, k, :],
                                start=(k == 0), stop=(k == KC - 1)), reads=[wdeps[k], xn_d], writes=[ps_d])
                        epi = s.get("epi", "copy")
                        if epi == "copy":
                            if cp % 2 == 0:
                                P.op("scalar", lambda e, c=c, ps=ps: e.copy(out=o[:, c, :], in_=ps[:, 0:tile_T]),
                                     reads=[ps_d], writes=[o_d])
                            else:
                                P.op("vector", lambda e, c=c, ps=ps: e.tensor_copy(out=o[:, c, :], in_=ps[:, 0:tile_T]),
                                     reads=[ps_d], writes=[o_d])
                            cp += 1
                        elif epi == "relu2":
                            t, t_d = tb[c % 2]
                            P.op("scalar", lambda e, ps=ps, t=t: e.activation(out=t[:], in_=ps[:, 0:tile_T], func=AF.Relu),
                                 reads=[ps_d], writes=[t_d])
                            eng = ("vector", "gpsimd")[c % 2]
                            P.op(eng, lambda e, c=c, t=t: e.tensor_tensor(out=o[:, c, :], in0=t[:], in1=t[:], op=ALU.mult),
                                 reads=[t_d], writes=[o_d])
                        elif epi == "resadd":
                            r, r_d = rb[ti % 2]
                            P.op("vector", lambda e, c=c, ps=ps, r=r: e.tensor_tensor(
                                out=o[:, c, :], in0=ps[:, 0:tile_T], in1=r[:, c, :], op=ALU.add),
                                reads=[ps_d, r_d], writes=[o_d])
                    P.dma([lambda e, o=o: e.dma_start(out=fm(s["dst"])[:, :, ti * tile_T:(ti + 1) * tile_T], in_=o[:])],
                          reads=[o_d], writes=[s["dst_dep"]])
                else:
                    (ob, _, _) = outs[si]
                    for tk in range(tile_T // 128):
                        o, o_d = ob[tmcount % 2]
                        tmcount += 1
                        for nb in range(n0, n1, 512):
                            ne = min(n1, nb + 512)
                            ps, ps_d = pss[psi[0] % 4]
                            psi[0] += 1
                            for k in range(KC):
                                P.op("tensor", lambda e, k=k, ps=ps, nb=nb, ne=ne, tk=tk: e.matmul(
                                    ps[:, 0:ne - nb], lhsT=xn[:, k, tk * 128:(tk + 1) * 128], rhs=wb[:, k, nb:ne],
                                    start=(k == 0), stop=(k == KC - 1)), reads=[wdeps[k], xn_d], writes=[ps_d])
                            if s.get("aug"):
                                oview = o[:, (nb - n0) // 64:(ne - n0) // 64, 0:64]
                                pview = ps[:, 0:ne - nb].rearrange("p (h d) -> p h d", d=64)
                            else:
                                oview = o[:, nb - n0:ne - n0]
                                pview = ps[:, 0:ne - nb]
                            if cp % 2 == 0:
                                P.op("scalar", lambda e: e.copy(out=oview, in_=pview), reads=[ps_d], writes=[o_d])
                            else:
                                P.op("vector", lambda e: e.tensor_copy(out=oview, in_=pview), reads=[ps_d], writes=[o_d])
                            cp += 1
                        r0 = ti * tile_T + tk * 128
                        oflat = o[:].rearrange("p h d -> p (h d)") if s.get("aug") else o[:]
                        P.dma([lambda e: e.dma_start(out=s["dst"][r0:r0 + 128, :], in_=oflat)],
                              reads=[o_d], writes=[s["dst_dep"]])
        P.barrier()


TLOC = 2048
HALO = 256
TEXT = TLOC + 2 * HALO
NA_H = 16
NA_DH = 64


def na_variant(i):
    return {0: 0, 1: 1, 14: 3, 15: 4}.get(i, 2)


def stage_na_attention(P, C, QT_d, KT_d, V_d, bias_d, oT_d, oT_dep):
    with ExitStack() as st:
        ident, ident_d = P.sbuf(st, "na_ident", [128, 128], BF16)
        idf, idf_d = P.sbuf(st, "na_idf", [128, 128], F32)
        P.op("gpsimd", lambda e: e.memset(idf[:], 1.0), writes=[idf_d])
        P.op("gpsimd", lambda e: e.affine_select(out=idf[:], in_=idf[:], pattern=[[-1, 128]], compare_op=ALU.is_equal,
                                                  fill=0.0, base=0, channel_multiplier=1), reads=[idf_d], writes=[idf_d])
        P.op("vector", lambda e: e.tensor_copy(out=ident[:], in_=idf[:]), reads=[idf_d], writes=[ident_d])
        kts = [P.sbuf(st, "na_kt%d" % i, [128, 8, 640], BF16) for i in range(2)]
        qt, qt_d = P.sbuf(st, "na_qt", [128, 8, TLOC], BF16)
        vts = [P.sbuf(st, "na_vt%d" % i, [128, 5, 16, 65], BF16) for i in range(2)]
        bts = [P.sbuf(st, "na_bt%d" % i, [128, 5, 128], F32) for i in range(3)]
        sbs = [P.sbuf(st, "na_sb%d" % i, [128, 5, 128], F32) for i in range(2)]
        pts = [P.sbuf(st, "na_pt%d" % i, [128, 5, 128], BF16) for i in range(2)]
        rcs = [P.sbuf(st, "na_rc%d" % i, [128, 1], F32) for i in range(2)]
        otm = [P.sbuf(st, "na_otm%d" % i, [128, 16, 64], BF16) for i in range(2)]
        oT, oT_sd = P.sbuf(st, "na_oT", [128, 8, TLOC], BF16)
        ps_s = [P.psum(st, "na_pss%d" % i, [128, 8, 128]) for i in range(2)]
        ps_o = [P.psum(st, "na_pso%d" % i, [128, 512]) for i in range(2)]
        ps_t = [P.psum(st, "na_pst%d" % i, [128, 8, 128], BF16) for i in range(1)]
        KTv, QTv = fm(KT_d), fm(QT_d)
        Vv = V_d.rearrange("(t p) f -> p t f", p=128)
        nblk = TLOC // 128
        P.dma([lambda e: e.dma_start(out=qt[:], in_=QTv[:, :, HALO:HALO + TLOC])], writes=[qt_d])

        def load_blk(i):
            kt, kt_d = kts[i % 2]
            vt, vt_d = vts[i % 2]
            P.dma([lambda e: e.dma_start(out=kt[:], in_=KTv[:, :, 128 * i:128 * i + 640])], writes=[kt_d])
            P.dma([lambda e: e.dma_start(out=vt[:].rearrange("p t h d -> p t (h d)"), in_=Vv[:, i:i + 5, :])],
                  writes=[vt_d])

        def load_bias(it):
            i, h = divmod(it, NA_H)
            bt, bt_d = bts[it % 3]
            P.dma([lambda e: e.dma_start(out=bt[:], in_=bias_d[na_variant(i), h])], writes=[bt_d])

        load_blk(0)
        load_bias(0)
        load_bias(1)
        it = 0
        for i in range(nblk):
            if i + 1 < nblk:
                load_blk(i + 1)
            kt, kt_d = kts[i % 2]
            vt, vt_d = vts[i % 2]
            ot, ot_d = otm[i % 2]
            for h in range(NA_H):
                if it + 2 < nblk * NA_H:
                    load_bias(it + 2)
                bt, bt_d = bts[it % 3]
                pss, pss_d = ps_s[it % 2]
                pso, pso_d = ps_o[it % 2]
                sb, sb_d = sbs[it % 2]
                pt, pt_d = pts[it % 2]
                rc, rc_d = rcs[it % 2]
                c, po = h // 2, (h % 2) * 64
                for t in range(5):
                    P.op("tensor", lambda e, t=t: e.matmul(pss[:, t, :], lhsT=kt[po:po + 64, c, t * 128:(t + 1) * 128],
                                                           rhs=qt[po:po + 64, c, 128 * i:128 * i + 128], start=True, stop=True),
                         reads=[kt_d, qt_d], writes=[pss_d])
                P.op("vector", lambda e: e.scalar_tensor_tensor(out=sb[:], in0=pss[:, 0:5, :], scalar=NA_DH ** -0.5,
                                                                in1=bt[:], op0=ALU.mult, op1=ALU.add),
                     reads=[pss_d, bt_d], writes=[sb_d])
                P.op("scalar", lambda e: e.activation(out=pt[:], in_=sb[:], func=AF.Exp), reads=[sb_d], writes=[pt_d])
                for t in range(5):
                    P.op("tensor", lambda e, t=t: e.matmul(pso[:, 0:65], lhsT=pt[:, t, :], rhs=vt[:, t, h, :],
                                                           start=(t == 0), stop=(t == 4)),
                         reads=[pt_d, vt_d], writes=[pso_d])
                P.op("vector", lambda e: e.reciprocal(out=rc[:], in_=pso[:, 64:65]), reads=[pso_d], writes=[rc_d])
                P.op("vector", lambda e: e.tensor_scalar(out=ot[:, h, :], in0=pso[:, 0:64], scalar1=rc[:, 0:1],
                                                         scalar2=None, op0=ALU.mult),
                     reads=[pso_d, rc_d], writes=[ot_d])
                it += 1
            pst, pst_d = ps_t[0]
            for cc in range(8):
                P.op("tensor", lambda e, cc=cc: e.transpose(pst[:, cc, :], ot[:, 2 * cc:2 * cc + 2, :], ident[:]),
                     reads=[ot_d, ident_d], writes=[pst_d])
            P.op("scalar", lambda e: e.copy(out=oT[:, :, 128 * i:128 * i + 128], in_=pst[:]), reads=[pst_d],
                 writes=[oT_sd])
        P.dma([lambda e: e.dma_start(out=fm(oT_d)[:, :, :], in_=oT[:])], reads=[oT_sd], writes=[oT_dep])
        P.barrier()


def stage_mlp(P, C, name, hT_d, T, w1_d, w2_d, nscale_d, aT_d, out_d, out_dep, tile_T=512):
    aT_dep = Dep(name + "_aT", dram=True)
    stage_linear(P, C, name + "a", hT_d, 1024, T, w1_d, 4096,
                 [dict(kind="fm", n0=0, n1=4096, dst=aT_d, dst_dep=aT_dep, epi="relu2", dtype=BF16)],
                 nscale_d=nscale_d, norm=True, tile_T=tile_T)
    stage_linear(P, C, name + "b", aT_d, 4096, T, w2_d, 1024,
                 [dict(kind="fm", n0=0, n1=1024, dst=out_d, dst_dep=out_dep, epi="resadd", res=hT_d, dtype=F32)],
                 src_dtype=BF16, tile_T=256)


def build_launch_a():
    nc = bass.Bass("TRN2", target_bir_lowering=False)
    xT = dram(nc, "xT", [1024, TEXT], F32, "ExternalInput")
    na_norm = dram(nc, "na_norm", [128, 8], F32, "ExternalInput")
    w_qkv = dram(nc, "w_qkv", [1024, 3072], F32, "ExternalInput")
    w_o = dram(nc, "w_o", [1024, 1024], F32, "ExternalInput")
    bias = dram(nc, "bias", [5, 16, 128, 5, 128], F32, "ExternalInput")
    mlp_norm = dram(nc, "mlp_norm", [128, 8], F32, "ExternalInput")
    w1 = dram(nc, "w1", [1024, 4096], F32, "ExternalInput")
    w2 = dram(nc, "w2", [4096, 1024], F32, "ExternalInput")
    h1T = dram(nc, "h1T", [1024, TLOC], F32, "ExternalOutput")
    gdn_norm = dram(nc, "gdn_norm", [128, 8], F32, "ExternalInput")
    gdn_w_in = dram(nc, "gdn_w_in", [1024, 4128], F32, "ExternalInput")
    g_qkvT = dram(nc, "g_qkvT", [3072, TLOC], F32, "ExternalOutput")
    g_ztm = dram(nc, "g_ztm", [TLOC, 1024], F32, "ExternalOutput")
    g_gates = dram(nc, "g_gates", [TLOC, 32], F32, "ExternalOutput")
    QT = dram(nc, "QT", [1024, TEXT], BF16)
    KT = dram(nc, "KT", [1024, TEXT], BF16)
    V = dram(nc, "V", [TEXT, 16 * 65], BF16)
    oT = dram(nc, "oT", [1024, TLOC], BF16)
    hmT = dram(nc, "hmT", [1024, TLOC], F32)
    aT = dram(nc, "aT", [4096, TLOC], BF16)
    with ExitStack() as es:
        P = Prog(nc, es)
        C = Consts(P, es)
        dd = lambda n: Dep(n, dram=True)
        stage_linear(P, C, "qkv", xT, 1024, TEXT, w_qkv, 3072,
                     [dict(kind="fm", n0=0, n1=1024, dst=QT, dst_dep=dd("QT"), dtype=BF16),
                      dict(kind="fm", n0=1024, n1=2048, dst=KT, dst_dep=dd("KT"), dtype=BF16),
                      dict(kind="tm", n0=2048, n1=3072, dst=V, dst_dep=dd("V"), dtype=BF16, aug=True)],
                     nscale_d=na_norm, norm=True)
        stage_na_attention(P, C, QT, KT, V, bias, oT, dd("oT"))
        stage_linear(P, C, "wo", oT, 1024, TLOC, w_o, 1024,
                     [dict(kind="fm", n0=0, n1=1024, dst=hmT, dst_dep=dd("hmT"), epi="resadd",
                           res=xT[:, HALO:HALO + TLOC], dtype=F32)], src_dtype=BF16)
        stage_mlp(P, C, "mlp0", hmT, TLOC, w1, w2, mlp_norm, aT, h1T, dd("h1T"))
        stage_linear(P, C, "gin", h1T, 1024, TLOC, gdn_w_in, 4128,
                     [dict(kind="fm", n0=0, n1=1024, dst=g_qkvT[0:1024, :], dst_dep=dd("gq"), dtype=F32),
                      dict(kind="fm", n0=1024, n1=2048, dst=g_qkvT[1024:2048, :], dst_dep=dd("gk"), dtype=F32),
                      dict(kind="fm", n0=2048, n1=3072, dst=g_qkvT[2048:3072, :], dst_dep=dd("gv"), dtype=F32),
                      dict(kind="tm", n0=3072, n1=4096, dst=g_ztm, dst_dep=dd("gz"), dtype=F32),
                      dict(kind="tm", n0=4096, n1=4128, dst=g_gates, dst_dep=dd("gg"), dtype=F32)],
                     nscale_d=gdn_norm, norm=True, tile_T=256)
        print("launch A instructions:", P.nins)
    return nc


def na_host_prep(x, na_rpb):
    xg = x.reshape(256, 64, 1024)
    rpb = na_rpb
    xTs, biases = [], []
    for c in range(NCORES):
        R0 = 32 * c
        slot_row = np.arange(40) + R0 - 4
        if c == 0:
            slot_row[0:4] = [6, 7, 6, 7]
        if c == NCORES - 1:
            slot_row[36:40] = [248, 249, 248, 249]
        xe = xg[slot_row].reshape(TEXT, 1024)
        xTs.append(np.ascontiguousarray(xe.T))
        tab = np.full((5, 16, 640, 128), -30000.0, np.float32)
        for vi, i in enumerate([0, 1, 2, 14, 15]):
            slots = np.arange(2 * i, 2 * i + 10)
            krow = slot_row[slots]
            first = np.array([list(krow).index(r) == j for j, r in enumerate(krow)])
            kr = np.repeat(krow, 64)
            kvalid = np.repeat(first, 64)
            kc = np.tile(np.arange(64), 10)
            qr = np.repeat(R0 + 2 * i + np.arange(2), 64)
            qc = np.tile(np.arange(64), 2)
            qr0 = np.clip(qr - 4, 0, 248)
            qc0 = np.clip(qc - 8, 0, 48)
            ok = (kr[:, None] >= qr0[None, :]) & (kr[:, None] < qr0[None, :] + 8) & kvalid[:, None] \
                & (kc[:, None] >= qc0[None, :]) & (kc[:, None] < qc0[None, :] + 16)
            drow = np.clip(kr[:, None] - qr[None, :] + 7, 0, 14)
            dcol = np.clip(kc[:, None] - qc[None, :] + 15, 0, 30)
            g = rpb[:, drow, dcol]
            tab[vi] = np.where(ok[None], g, np.float32(-30000.0))
        biases.append(np.ascontiguousarray(tab.reshape(5, 16, 5, 128, 128).transpose(0, 1, 3, 2, 4)))
    return xTs, biases


def pc(v):
    return np.ascontiguousarray(np.asarray(v).reshape(-1, 128).T)


S_ALL = 16384
GCH = 64
NCH = S_ALL // GCH
GRP = 8


def gdn_consts_host():
    j = np.arange(64)
    NEG = -30000.0
    I128 = np.eye(128, dtype=np.float32)

    def pad(m):
        out = np.zeros((128, 64), np.float32)
        out[:64] = m
        return out
    tri_f = (j[:, None] <= j[None, :])
    tri_b = (j[:, None] >= j[None, :])
    us_f = (j[:, None] > j[None, :])
    us_b = (j[:, None] < j[None, :])
    negs_f = np.where(j[:, None] > j[None, :], 0, NEG)
    negiT_f = np.where(j[None, :] >= j[:, None], 0, NEG)
    negs_b = np.where(j[:, None] < j[None, :], 0, NEG)
    negiT_b = np.where(j[None, :] <= j[:, None], 0, NEG)
    blocks = [tri_f, tri_b, us_f, us_b, negs_f, negiT_f, negs_b, negiT_b]
    return np.ascontiguousarray(np.concatenate([I128] + [pad(np.asarray(b, np.float32)) for b in blocks], axis=1))


class Rot:
    def __init__(self, P, st, name, shape, dtype, n):
        self.bufs = [P.sbuf(st, "%s%d" % (name, i), shape, dtype) for i in range(n)]
        self.i = 0

    def next(self):
        b = self.bufs[self.i % len(self.bufs)]
        self.i += 1
        return b


class PsPool:
    def __init__(self, P, st, name, nbanks=8):
        self.tiles = []
        for b in range(nbanks):
            t, d = P.psum(st, "%s%d" % (name, b), [128, 512])
            self.tiles.append((t, d))
        self.i = 0

    def next(self, p, f):
        t, d = self.tiles[self.i % len(self.tiles)]
        self.i += 1
        return t[0:p, 0:f], d


def gdn_stage_prep(P, qkv_pre, conv_w, gconst_sb, qT_d, kT_d, ktm_d, vtm_d):
    gc, gc_d = gconst_sb
    with ExitStack() as st:
        cw, cw_d = P.sbuf(st, "gp_cw", [128, 3, 5], F32)
        P.dma([lambda e: e.dma_start(out=cw[:], in_=conv_w[:, :, :])], writes=[cw_d])
        onesf, onesf_d = P.sbuf(st, "gp_ones", [128, 128], F32)
        P.op("vector", lambda e: e.memset(onesf[:], 1.0), writes=[onesf_d])
        xin = Rot(P, st, "gp_x", [128, 3, 516], F32, 2)
        acc = Rot(P, st, "gp_acc", [128, 512], F32, 2)
        yb = Rot(P, st, "gp_y", [128, 512], F32, 3)
        sqb = Rot(P, st, "gp_sq", [128, 512], F32, 2)
        rsb = Rot(P, st, "gp_rs", [128, 512], F32, 2)
        ynb = Rot(P, st, "gp_yn", [128, 512], F32, 4)
        tmb = Rot(P, st, "gp_tm", [128, 4, 128], F32, 4)
        pss = [P.psum(st, "gp_ps%d" % i, [128, 512]) for i in range(2)]
        pst = [P.psum(st, "gp_pt%d" % i, [128, 512]) for i in range(4)]
        nt = S_ALL // 512
        qd, kd, ktd, vtd = Dep("qT", True), Dep("kT", True), Dep("ktm", True), Dep("vtm", True)
        ki = 0
        for ti in range(nt):
            t0 = ti * 512
            x, x_d = xin.next()
            lo, hi = max(t0 - 2, 0), min(t0 + 514, S_ALL)
            if ti == 0:
                P.op("gpsimd", lambda e: e.memset(x[:, :, 0:2], 0.0), writes=[x_d])
            if ti == nt - 1:
                P.op("gpsimd", lambda e: e.memset(x[:, :, 514:516], 0.0), writes=[x_d])
            c0 = lo - (t0 - 2)
            P.dma([lambda e: e.dma_start(out=x[:, :, c0:c0 + (hi - lo)], in_=qkv_pre[:, :, lo:hi])], writes=[x_d])
            for j in range(3):
                a, a_d = acc.next()
                P.op("vector", lambda e: e.tensor_scalar(out=a[:], in0=x[:, j, 0:512], scalar1=cw[:, j, 0:1], scalar2=None,
                                                         op0=ALU.mult), reads=[x_d, cw_d], writes=[a_d])
                for tap in range(1, 5):
                    P.op("vector", lambda e, tap=tap: e.scalar_tensor_tensor(
                        out=a[:], in0=x[:, j, tap:tap + 512], scalar=cw[:, j, tap:tap + 1], in1=a[:],
                        op0=ALU.mult, op1=ALU.add), reads=[x_d, cw_d, a_d], writes=[a_d])
                y, y_d = yb.next()
                P.op("scalar", lambda e: e.activation(out=y[:], in_=a[:], func=AF.Silu), reads=[a_d], writes=[y_d])
                if j < 2:
                    sq, sq_d = sqb.next()
                    P.op("scalar", lambda e: e.activation(out=sq[:], in_=y[:], func=AF.Square), reads=[y_d], writes=[sq_d])
                    ps, ps_d = pss[ki % 2]
                    ki += 1
                    P.op("tensor", lambda e: e.matmul(ps[:], lhsT=onesf[:], rhs=sq[:], start=True, stop=True),
                         reads=[onesf_d, sq_d], writes=[ps_d])
                    rs, rs_d = rsb.next()
                    sc = 128.0 if j == 0 else 1.0
                    P.op("scalar", lambda e: e.activation(out=rs[:], in_=ps[:], func=AF.Sqrt, scale=sc, bias=sc * EPS),
                         reads=[ps_d], writes=[rs_d])
                    P.op("vector", lambda e: e.reciprocal(out=rs[:], in_=rs[:]), reads=[rs_d], writes=[rs_d])
                    yn, yn_d = ynb.next()
                    P.op("gpsimd", lambda e: e.tensor_tensor(out=yn[:], in0=y[:], in1=rs[:], op=ALU.mult),
                         reads=[y_d, rs_d], writes=[yn_d])
                    dst, dstd = (qT_d, qd) if j == 0 else (kT_d, kd)
                    P.dma([lambda e: e.dma_start(out=dst[:, t0:t0 + 512], in_=yn[:])], reads=[yn_d], writes=[dstd])
                else:
                    yn, yn_d = y, y_d
                if j >= 1:
                    tm, tm_d = tmb.next()
                    for b in range(4):
                        pt, pt_d = pst[(ti * 8 + j * 4 + b) % 4]
                        P.op("tensor", lambda e, b=b: e.transpose(pt[:, 0:128], yn[:, b * 128:(b + 1) * 128], gc[:, 0:128]),
                             reads=[yn_d, gc_d], writes=[pt_d])
                        if b % 2 == 0:
                            P.op("scalar", lambda e, b=b: e.copy(out=tm[:, b, :], in_=pt[:, 0:128]), reads=[pt_d], writes=[tm_d])
                        else:
                            P.op("vector", lambda e, b=b: e.tensor_copy(out=tm[:, b, :], in_=pt[:, 0:128]), reads=[pt_d],
                                 writes=[tm_d])
                    dst, dstd = (ktm_d, ktd) if j == 1 else (vtm_d, vtd)
                    P.dma([lambda e: e.dma_start(out=dst[t0:t0 + 512, :].rearrange("(b p) d -> p b d", p=128), in_=tm[:])],
                          reads=[tm_d], writes=[dstd])
        P.barrier()


def gdn_stage_gates(P, st, gates_d, scal_d, gconst_sb):
    gc, gc_d = gconst_sb
    out = {}
    pers = []
    for d in range(2):
        pers.append(dict(
            g=P.sbuf(st, "gg_g%d" % d, [64, NCH], F32), beta=P.sbuf(st, "gg_beta%d" % d, [64, NCH], F32),
            eG=P.sbuf(st, "gg_eG%d" % d, [64, NCH], F32), beG=P.sbuf(st, "gg_beG%d" % d, [64, NCH], F32),
            kdec=P.sbuf(st, "gg_kdec%d" % d, [64, NCH], F32), gl=P.sbuf(st, "gg_gl%d" % d, [128, NCH], F32)))
    with ExitStack() as tmp:
        gt, gt_d = P.sbuf(tmp, "gg_gt", [64, NCH, 4], F32)
        sc, sc_d = P.sbuf(tmp, "gg_sc", [64, 4], F32)
        P.dma([lambda e: e.dma_start(out=gt[:], in_=gates_d[:, :, :])], writes=[gt_d])
        P.dma([lambda e: e.dma_start(out=sc[:], in_=scal_d[:, :])], writes=[sc_d])
        onesf, onesf_d = P.sbuf(tmp, "gg_ones", [64, 128], F32)
        P.op("vector", lambda e: e.memset(onesf[:], 1.0), writes=[onesf_d])
        t1, t1_d = P.sbuf(tmp, "gg_t1", [64, NCH], F32)
        t2, t2_d = P.sbuf(tmp, "gg_t2", [64, NCH], F32)
        t3, t3_d = P.sbuf(tmp, "gg_t3", [64, NCH], F32)
        Gs, Gs_d = P.sbuf(tmp, "gg_G", [64, NCH], F32)
        Gt, Gt_d = P.sbuf(tmp, "gg_Gt", [64, NCH], F32)
        ac, ac_d = P.sbuf(tmp, "gg_ac", [64, 1], F32)
        psG, psG_d = P.psum(tmp, "gg_psG", [128, 512])
        psT, psT_d = P.psum(tmp, "gg_psT", [128, 512])
        for d in range(2):
            (g, g_d), (beta, beta_d), (eG, eG_d) = pers[d]["g"], pers[d]["beta"], pers[d]["eG"]
            (beG, beG_d), (kdec, kdec_d), (gl, gl_d) = pers[d]["beG"], pers[d]["kdec"], pers[d]["gl"]
            P.op("vector", lambda e: e.tensor_scalar(out=t1[:], in0=gt[:, :, 2 + d], scalar1=sc[:, 2 + d:3 + d], scalar2=None,
                                                     op0=ALU.add), reads=[gt_d, sc_d], writes=[t1_d])
            P.op("scalar", lambda e: e.activation(out=t2[:], in_=t1[:], func=AF.Abs), reads=[t1_d], writes=[t2_d])
            P.op("scalar", lambda e: e.activation(out=t2[:], in_=t2[:], func=AF.Exp, scale=-1.0), reads=[t2_d], writes=[t2_d])
            P.op("scalar", lambda e: e.activation(out=t2[:], in_=t2[:], func=AF.Ln, bias=1.0), reads=[t2_d], writes=[t2_d])
            P.op("vector", lambda e: e.tensor_scalar(out=t3[:], in0=t1[:], scalar1=0.0, scalar2=None, op0=ALU.max),
                 reads=[t1_d], writes=[t3_d])
            P.op("vector", lambda e: e.tensor_tensor(out=t3[:], in0=t3[:], in1=t2[:], op=ALU.add), reads=[t3_d, t2_d],
                 writes=[t3_d])
            P.op("scalar", lambda e: e.activation(out=ac[:], in_=sc[:, d:d + 1], func=AF.Exp), reads=[sc_d], writes=[ac_d])
            P.op("vector", lambda e: e.tensor_scalar(out=g[:], in0=t3[:], scalar1=ac[:, 0:1], scalar2=-1.0, op0=ALU.mult,
                                                     op1=ALU.mult), reads=[t3_d, ac_d], writes=[g_d])
            P.op("scalar", lambda e: e.activation(out=beta[:], in_=gt[:, :, d], func=AF.Sigmoid), reads=[gt_d], writes=[beta_d])
            tri = gc[0:64, 128 + 64 * d:128 + 64 * d + 64]
            P.op("tensor", lambda e: e.matmul(psG[0:64, 0:NCH], lhsT=tri, rhs=g[:], start=True, stop=True),
                 reads=[gc_d, g_d], writes=[psG_d])
            P.op("tensor", lambda e: e.matmul(psT[:, 0:NCH], lhsT=onesf[:], rhs=g[:], start=True, stop=True),
                 reads=[onesf_d, g_d], writes=[psT_d])
            P.op("vector", lambda e: e.tensor_copy(out=Gs[:], in_=psG[0:64, 0:NCH]), reads=[psG_d], writes=[Gs_d])
            P.op("scalar", lambda e: e.activation(out=gl[:], in_=psT[:, 0:NCH], func=AF.Exp), reads=[psT_d], writes=[gl_d])
            P.op("vector", lambda e: e.tensor_tensor(out=Gt[:], in0=psT[0:64, 0:NCH], in1=Gs[:], op=ALU.subtract),
                 reads=[Gs_d], writes=[Gt_d, psT_d])
            P.op("scalar", lambda e: e.activation(out=kdec[:], in_=Gt[:], func=AF.Exp), reads=[Gt_d], writes=[kdec_d])
            P.op("scalar", lambda e: e.activation(out=eG[:], in_=Gs[:], func=AF.Exp), reads=[Gs_d], writes=[eG_d])
            P.op("vector", lambda e: e.tensor_tensor(out=beG[:], in0=beta[:], in1=eG[:], op=ALU.mult), reads=[beta_d, eG_d],
                 writes=[beG_d])
            out[d] = dict(g=(g, g_d), beta=(beta, beta_d), eG=(eG, eG_d), beG=(beG, beG_d), kdec=(kdec, kdec_d), gl=(gl, gl_d))
        P.barrier()
    return out


def gdn_stage_scan(P, G, gconst_sb, qT_d, kT_d, ktm_d, vtm_d, of_d, z_d, onorm_d, og_d, dbg=None):
    gc, gc_d = gconst_sb
    I64 = gc[0:64, 0:64]
    with ExitStack() as st:
        pp = PsPool(P, st, "gs_ps")
        S, S_d = P.sbuf(st, "gs_S", [128, 128], F32)
        onr, onr_d = P.sbuf(st, "gs_onr", [64, GRP, 128], F32)
        P.dma([lambda e: e.dma_start(out=onr[:], in_=onorm_d[:, :, :])], writes=[onr_d])
        ktg = Rot(P, st, "gs_ktg", [128, GRP * 64], F32, 2)
        qtg = Rot(P, st, "gs_qtg", [128, GRP * 64], F32, 2)
        kg = Rot(P, st, "gs_kg", [64, GRP, 128], F32, 2)
        vg = Rot(P, st, "gs_vg", [64, GRP, 128], F32, 2)
        og = Rot(P, st, "gs_og", [64, GRP, 128], F32, 2)
        ofg = Rot(P, st, "gs_ofg", [64, GRP, 128], F32, 2)
        zg = Rot(P, st, "gs_zg", [64, GRP, 128], F32, 2)
        sqg = Rot(P, st, "gs_sqg", [64, GRP, 128], F32, 1)
        ssg = Rot(P, st, "gs_ssg", [64, GRP], F32, 2)
        gtri = Rot(P, st, "gs_gtri", [64, 64], F32, 2)
        Dm = Rot(P, st, "gs_Dm", [64, 64], F32, 2)
        DTm = Rot(P, st, "gs_DTm", [64, 64], F32, 2)
        Lb = Rot(P, st, "gs_L", [64, 64], F32, 2)
        Nb = Rot(P, st, "gs_N", [64, 64], F32, 2)
        Ztb = Rot(P, st, "gs_Zt", [64, 64], F32, 2)
        Pb = Rot(P, st, "gs_P", [64, 64], F32, 4)
        Ptb = Rot(P, st, "gs_Pt", [64, 64], F32, 4)
        rwb = Rot(P, st, "gs_rw", [64, 128], F32, 2)
        vbb = Rot(P, st, "gs_vb", [64, 128], F32, 2)
        ATb = Rot(P, st, "gs_AT", [64, 64], F32, 3)
        kdb = Rot(P, st, "gs_kd", [64, 128], F32, 3)
        wTb = Rot(P, st, "gs_wT", [128, 64], F32, 3)
        ub = Rot(P, st, "gs_u", [64, 128], F32, 3)
        vnb = Rot(P, st, "gs_vn", [64, 128], F32, 2)
        tb = Rot(P, st, "gs_t", [64, 128], F32, 2)
        ofd, ogd = Dep("of", True), Dep("og", True)
        cpi = [0]

        def evac(out_ap, out_d, ps, ps_d):
            if cpi[0] % 2 == 0:
                P.op("scalar", lambda e: e.copy(out=out_ap, in_=ps), reads=[ps_d], writes=[out_d])
            else:
                P.op("vector", lambda e: e.tensor_copy(out=out_ap, in_=ps), reads=[ps_d], writes=[out_d])
            cpi[0] += 1

        for d in range(2 if dbg is None else 1):
            gd = G[d]
            (g, g_d), (beta, beta_d), (eG, eG_d) = gd["g"], gd["beta"], gd["eG"]
            (beG, beG_d), (kdec, kdec_d), (gl, gl_d) = gd["beG"], gd["kdec"], gd["gl"]
            tri = gc[0:64, 128 + 64 * d:192 + 64 * d]
            us = gc[0:64, 256 + 64 * d:320 + 64 * d]
            negs = gc[0:64, 384 + 128 * d:448 + 128 * d]
            negiT = gc[0:64, 448 + 128 * d:512 + 128 * d]
            P.op("vector", lambda e: e.memset(S[:], 0.0), writes=[S_d])
            order = list(range(NCH)) if d == 0 else list(range(NCH - 1, -1, -1))
            groups = [order[i:i + GRP] for i in range(0, NCH, GRP)]
            if dbg is not None:
                groups = groups[:dbg]
            state = {}

            def load_group(gi):
                chunks = groups[gi]
                c0 = min(chunks)
                t0 = c0 * 64
                kt, kt_d = ktg.next()
                qt, qt_d = qtg.next()
                k, k_d = kg.next()
                v, v_d = vg.next()
                P.dma([lambda e: e.dma_start(out=kt[:], in_=kT_d[:, t0:t0 + GRP * 64])], writes=[kt_d])
                P.dma([lambda e: e.dma_start(out=qt[:], in_=qT_d[:, t0:t0 + GRP * 64])], writes=[qt_d])
                P.dma([lambda e: e.dma_start(out=k[:], in_=ktm_d[t0:t0 + GRP * 64, :].rearrange("(n p) d -> p n d", p=64))],
                      writes=[k_d])
                P.dma([lambda e: e.dma_start(out=v[:], in_=vtm_d[t0:t0 + GRP * 64, :].rearrange("(n p) d -> p n d", p=64))],
                      writes=[v_d])
                r = dict(c0=c0, kt=(kt, kt_d), qt=(qt, qt_d), k=(k, k_d), v=(v, v_d), o=og.next())
                if d == 1:
                    of_, of_dd = ofg.next()
                    z_, z_dd = zg.next()
                    P.dma([lambda e: e.dma_start(out=of_[:], in_=of_d[t0:t0 + GRP * 64, :].rearrange("(n p) d -> p n d", p=64))],
                          writes=[of_dd])
                    P.dma([lambda e: e.dma_start(out=z_[:], in_=z_d[t0:t0 + GRP * 64, :].rearrange("(n p) d -> p n d", p=64))],
                          writes=[z_dd])
                    r["of"] = (of_, of_dd)
                    r["z"] = (z_, z_dd)
                state[gi] = r

            def prep(gi, n):
                r = state[gi]
                ln = n - r["c0"]
                kt, kt_d = r["kt"]
                qt, qt_d = r["qt"]
                k, k_d = r["k"]
                v, v_d = r["v"]
                ktn = kt[:, ln * 64:(ln + 1) * 64]
                qtn = qt[:, ln * 64:(ln + 1) * 64]
                gt_, gt_dd = gtri.next()
                P.op("gpsimd", lambda e: e.tensor_scalar(out=gt_[:], in0=tri, scalar1=g[:, n:n + 1], scalar2=None, op0=ALU.mult),
                     reads=[gc_d, g_d], writes=[gt_dd])
                ps1, ps1_d = pp.next(64, 64)
                P.op("tensor", lambda e: e.matmul(ps1, lhsT=gt_[:], rhs=us, start=True, stop=False), reads=[gt_dd, gc_d],
                     writes=[ps1_d])
                P.op("tensor", lambda e: e.matmul(ps1, lhsT=I64, rhs=negs, start=False, stop=True), reads=[gc_d], writes=[ps1_d])
                ps2, ps2_d = pp.next(64, 64)
                P.op("tensor", lambda e: e.matmul(ps2, lhsT=us, rhs=gt_[:], start=True, stop=False), reads=[gt_dd, gc_d],
                     writes=[ps2_d])
                P.op("tensor", lambda e: e.matmul(ps2, lhsT=I64, rhs=negiT, start=False, stop=True), reads=[gc_d], writes=[ps2_d])
                dm, dm_d = Dm.next()
                dtm, dtm_d = DTm.next()
                P.op("scalar", lambda e: e.activation(out=dm[:], in_=ps1, func=AF.Exp), reads=[ps1_d], writes=[dm_d])
                P.op("scalar", lambda e: e.activation(out=dtm[:], in_=ps2, func=AF.Exp), reads=[ps2_d], writes=[dtm_d])
                pkk, pkk_d = pp.next(64, 64)
                P.op("tensor", lambda e: e.matmul(pkk, lhsT=ktn, rhs=ktn, start=True, stop=True), reads=[kt_d], writes=[pkk_d])
                pkq, pkq_d = pp.next(64, 64)
                P.op("tensor", lambda e: e.matmul(pkq, lhsT=ktn, rhs=qtn, start=True, stop=True), reads=[kt_d, qt_d],
                     writes=[pkq_d])
                L, L_d = Lb.next()
                P.op("vector", lambda e: e.scalar_tensor_tensor(out=L[:], in0=pkk, scalar=beta[:, n:n + 1], in1=dm[:],
                                                                op0=ALU.mult, op1=ALU.mult),
                     reads=[pkk_d, beta_d, dm_d], writes=[L_d])
                AT, AT_d = ATb.next()
                P.op("vector", lambda e: e.tensor_tensor(out=AT[:], in0=pkq, in1=dtm[:], op=ALU.mult), reads=[pkq_d, dtm_d],
                     writes=[AT_d])
                pn, pn_d = pp.next(64, 64)
                P.op("tensor", lambda e: e.transpose(pn, L[:], I64), reads=[L_d, gc_d], writes=[pn_d])
                N, N_d = Nb.next()
                evac(N[:], N_d, pn, pn_d)
                Zt, Zt_d = Ztb.next()
                P.op("gpsimd", lambda e: e.tensor_tensor(out=Zt[:], in0=I64, in1=N[:], op=ALU.subtract), reads=[gc_d, N_d],
                     writes=[Zt_d])
                Pc, Pc_d, Ptc, Ptc_d = L, L_d, N, N_d
                for lvl in range(1, 6):
                    pP, pP_d = pp.next(64, 64)
                    P.op("tensor", lambda e: e.matmul(pP, lhsT=Ptc[:], rhs=Pc[:], start=True, stop=True), reads=[Ptc_d, Pc_d],
                         writes=[pP_d])
                    Pn, Pn_d = Pb.next()
                    evac(Pn[:], Pn_d, pP, pP_d)
                    if lvl < 5:
                        pPt, pPt_d = pp.next(64, 64)
                        P.op("tensor", lambda e: e.matmul(pPt, lhsT=Pc[:], rhs=Ptc[:], start=True, stop=True),
                             reads=[Ptc_d, Pc_d], writes=[pPt_d])
                        Ptn, Ptn_d = Ptb.next()
                        evac(Ptn[:], Ptn_d, pPt, pPt_d)
                    pZ, pZ_d = pp.next(64, 64)
                    P.op("tensor", lambda e: e.matmul(pZ, lhsT=Pn[:], rhs=Zt[:], start=True, stop=True), reads=[Pn_d, Zt_d],
                         writes=[pZ_d])
                    P.op("vector", lambda e: e.tensor_tensor(out=Zt[:], in0=pZ, in1=Zt[:], op=ALU.add), reads=[pZ_d, Zt_d],
                         writes=[Zt_d])
                    Pc, Pc_d = Pn, Pn_d
                    if lvl < 5:
                        Ptc, Ptc_d = Ptn, Ptn_d
                rw, rw_d = rwb.next()
                vb, vb_d = vbb.next()
                kd_, kd_d = kdb.next()
                P.op("gpsimd", lambda e: e.tensor_scalar(out=rw[:], in0=k[:, ln, :], scalar1=beG[:, n:n + 1], scalar2=None,
                                                         op0=ALU.mult), reads=[k_d, beG_d], writes=[rw_d])
                P.op("gpsimd", lambda e: e.tensor_scalar(out=vb[:], in0=v[:, ln, :], scalar1=beta[:, n:n + 1], scalar2=None,
                                                         op0=ALU.mult), reads=[v_d, beta_d], writes=[vb_d])
                P.op("gpsimd", lambda e: e.tensor_scalar(out=kd_[:], in0=k[:, ln, :], scalar1=kdec[:, n:n + 1], scalar2=None,
                                                         op0=ALU.mult), reads=[k_d, kdec_d], writes=[kd_d])
                pw, pw_d = pp.next(128, 64)
                P.op("tensor", lambda e: e.matmul(pw, lhsT=rw[:], rhs=Zt[:], start=True, stop=True), reads=[rw_d, Zt_d],
                     writes=[pw_d])
                wT, wT_d = wTb.next()
                evac(wT[:], wT_d, pw, pw_d)
                pu, pu_d = pp.next(64, 128)
                P.op("tensor", lambda e: e.matmul(pu, lhsT=Zt[:], rhs=vb[:], start=True, stop=True), reads=[Zt_d, vb_d],
                     writes=[pu_d])
                u, u_d = ub.next()
                evac(u[:], u_d, pu, pu_d)
                return dict(wT=(wT, wT_d), u=(u, u_d), AT=(AT, AT_d), kd=(kd_, kd_d), qtn=qtn, qt_d=qt_d, ln=ln)

            def scan(gi, n, pr):
                r = state[gi]
                ln = pr["ln"]
                wT, wT_d = pr["wT"]
                u, u_d = pr["u"]
                AT, AT_d = pr["AT"]
                kd_, kd_d = pr["kd"]
                o, o_d = r["o"]
                pws, pws_d = pp.next(64, 128)
                P.op("tensor", lambda e: e.matmul(pws, lhsT=wT[:], rhs=S[:], start=True, stop=True), reads=[wT_d, S_d],
                     writes=[pws_d])
                vn, vn_d = vnb.next()
                P.op("vector", lambda e: e.tensor_tensor(out=vn[:], in0=u[:], in1=pws, op=ALU.subtract), reads=[u_d, pws_d],
                     writes=[vn_d])
                pqs, pqs_d = pp.next(64, 128)
                P.op("tensor", lambda e: e.matmul(pqs, lhsT=pr["qtn"], rhs=S[:], start=True, stop=True),
                     reads=[pr["qt_d"], S_d], writes=[pqs_d])
                pkv, pkv_d = pp.next(128, 128)
                P.op("tensor", lambda e: e.matmul(pkv, lhsT=kd_[:], rhs=vn[:], start=True, stop=True), reads=[kd_d, vn_d],
                     writes=[pkv_d])
                pav, pav_d = pp.next(64, 128)
                P.op("tensor", lambda e: e.matmul(pav, lhsT=AT[:], rhs=vn[:], start=True, stop=True), reads=[AT_d, vn_d],
                     writes=[pav_d])
                P.op("vector", lambda e: e.scalar_tensor_tensor(out=S[:], in0=S[:], scalar=gl[:, n:n + 1], in1=pkv,
                                                                op0=ALU.mult, op1=ALU.add),
                     reads=[S_d, gl_d, pkv_d], writes=[S_d])
                t_, t_d = tb.next()
                P.op("scalar", lambda e: e.copy(out=t_[:], in_=pav), reads=[pav_d], writes=[t_d])
                P.op("vector", lambda e: e.scalar_tensor_tensor(out=o[:, ln, :], in0=pqs, scalar=eG[:, n:n + 1], in1=t_[:],
                                                                op0=ALU.mult, op1=ALU.add),
                     reads=[pqs_d, eG_d, t_d], writes=[o_d])

            def finish_group(gi):
                r = state.pop(gi)
                t0 = r["c0"] * 64
                o, o_d = r["o"]
                if d == 0:
                    P.dma([lambda e: e.dma_start(out=of_d[t0:t0 + GRP * 64, :].rearrange("(n p) d -> p n d", p=64), in_=o[:])],
                          reads=[o_d], writes=[ofd])
                    return
                of_, of_dd = r["of"]
                z_, z_dd = r["z"]
                sq, sq_d = sqg.next()
                ss, ss_d = ssg.next()
                P.op("gpsimd", lambda e: e.tensor_tensor(out=o[:], in0=o[:], in1=of_[:], op=ALU.add), reads=[o_d, of_dd],
                     writes=[o_d])
                P.op("scalar", lambda e: e.activation(out=sq[:], in_=o[:], func=AF.Square), reads=[o_d], writes=[sq_d])
                P.op("vector", lambda e: e.tensor_reduce(out=ss[:], in_=sq[:], axis=AX.X, op=ALU.add), reads=[sq_d],
                     writes=[ss_d])
                P.op("scalar", lambda e: e.activation(out=ss[:], in_=ss[:], func=AF.Sqrt, scale=1.0 / 128, bias=EPS),
                     reads=[ss_d], writes=[ss_d])
                P.op("vector", lambda e: e.reciprocal(out=ss[:], in_=ss[:]), reads=[ss_d], writes=[ss_d])
                P.op("scalar", lambda e: e.activation(out=z_[:], in_=z_[:], func=AF.Silu), reads=[z_dd], writes=[z_dd])
                P.op("gpsimd", lambda e: e.tensor_tensor(out=z_[:], in0=z_[:], in1=onr[:], op=ALU.mult), reads=[z_dd, onr_d],
                     writes=[z_dd])
                for c in range(GRP):
                    P.op("vector", lambda e, c=c: e.scalar_tensor_tensor(out=o[:, c, :], in0=o[:, c, :], scalar=ss[:, c:c + 1],
                                                                         in1=z_[:, c, :], op0=ALU.mult, op1=ALU.mult),
                         reads=[o_d, ss_d, z_dd], writes=[o_d])
                P.dma([lambda e: e.dma_start(out=og_d[t0:t0 + GRP * 64, :].rearrange("(n p) d -> p n d", p=64), in_=o[:])],
                      reads=[o_d], writes=[ogd])

            flat = [(gi, n) for gi, ch in enumerate(groups) for n in ch]
            load_group(0)
            pending = None
            for idx, (gi, n) in enumerate(flat):
                pr = prep(gi, n)
                if pending is not None:
                    pgi, pn, ppr = pending
                    scan(pgi, pn, ppr)
                    if pn == groups[pgi][-1]:
                        finish_group(pgi)
                if n == groups[gi][0] and gi + 1 < len(groups):
                    load_group(gi + 1)
                pending = (gi, n, pr)
            pgi, pn, ppr = pending
            scan(pgi, pn, ppr)
            finish_group(pgi)
            P.barrier()
        P.barrier()


def build_launch_b(dbg=None):
    nc = bass.Bass("TRN2", target_bir_lowering=False)
    ik = "Internal" if dbg is None else "ExternalOutput"
    qkv_pre = dram(nc, "qkv_pre", [128, 3, S_ALL], F32, "ExternalInput")
    z_tm = dram(nc, "z_tm", [S_ALL, 128], F32, "ExternalInput")
    gates = dram(nc, "gates", [64, NCH, 4], F32, "ExternalInput")
    conv_w = dram(nc, "conv_w", [128, 3, 5], F32, "ExternalInput")
    scal = dram(nc, "scal", [64, 4], F32, "ExternalInput")
    onorm = dram(nc, "onorm", [64, GRP, 128], F32, "ExternalInput")
    gconst = dram(nc, "gconst", [128, 640], F32, "ExternalInput")
    og = dram(nc, "og", [S_ALL, 128], F32, "ExternalOutput")
    qT = dram(nc, "g_qT", [128, S_ALL], F32, ik)
    kT = dram(nc, "g_kT", [128, S_ALL], F32, ik)
    ktm = dram(nc, "g_ktm", [S_ALL, 128], F32, ik)
    vtm = dram(nc, "g_vtm", [S_ALL, 128], F32, ik)
    of = dram(nc, "g_of", [S_ALL, 128], F32, ik)
    with ExitStack() as es:
        P = Prog(nc, es)
        gc = P.sbuf(es, "gconst_sb", [128, 640], F32)
        P.dma([lambda e: e.dma_start(out=gc[0][:], in_=gconst[:, :])], writes=[gc[1]])
        gdn_stage_prep(P, qkv_pre, conv_w, gc, qT, kT, ktm, vtm)
        G = gdn_stage_gates(P, es, gates, scal, gc)
        gdn_stage_scan(P, G, gc, qT, kT, ktm, vtm, of, z_tm, onorm, og, dbg=dbg)
        print("launch B instructions:", P.nins)
    return nc


def stage_mla_prep(P, C, cqT_d, ckvT_d, krT_d, pkrT_d, cos_d, ssin_d, cqn_d, ckvn_d, krr_d, T):
    with ExitStack() as st:
        xq = Rot(P, st, "mp_xq", [128, 6, 512], F32, 2)
        xkv = Rot(P, st, "mp_xkv", [128, 2, 512], F32, 2)
        xr = Rot(P, st, "mp_xr", [32, 4, 512], F32, 2)
        sq, sq_d = P.sbuf(st, "mp_sq", [128, 6, 512], BF16)
        rstd, rstd_d = P.sbuf(st, "mp_rstd", [128, 512], F32)
        oq = Rot(P, st, "mp_oq", [128, 6, 512], BF16, 2)
        okv = Rot(P, st, "mp_okv", [128, 2, 512], BF16, 2)
        okr = Rot(P, st, "mp_okr", [32, 512], BF16, 2)
        t1, t1_d = P.sbuf(st, "mp_t1", [32, 512], F32)
        t2, t2_d = P.sbuf(st, "mp_t2", [32, 512], F32)
        ps, ps_d = P.psum(st, "mp_ps", [128, 512])
        dq, dkv, dkr = Dep("cqn", True), Dep("ckvn", True), Dep("krr", True)
        for ti in range(T // 512):
            sl = slice(ti * 512, (ti + 1) * 512)
            a, a_d = xq.next()
            b, b_d = xkv.next()
            r, r_d = xr.next()
            P.dma([lambda e: e.dma_start(out=a[:], in_=fm(cqT_d)[:, :, sl])], writes=[a_d])
            P.dma([lambda e: e.dma_start(out=b[:], in_=fm(ckvT_d)[:, :, sl])], writes=[b_d])
            P.dma([lambda e: e.dma_start(out=r[:, 0, :], in_=krT_d[0:32, sl]),
                   lambda e: e.dma_start(out=r[:, 1, :], in_=pkrT_d[0:32, sl]),
                   lambda e: e.dma_start(out=r[:, 2, :], in_=cos_d[0:32, sl]),
                   lambda e: e.dma_start(out=r[:, 3, :], in_=ssin_d[0:32, sl])], writes=[r_d])
            o1, o1_d = oq.next()
            o2, o2_d = okv.next()
            o3, o3_d = okr.next()
            rms_tile(P, C, a, a_d, 6, 512, 768, sq, sq_d, ps, ps_d, rstd, rstd_d, o1, o1_d)
            P.dma([lambda e: e.dma_start(out=fm(cqn_d)[:, :, sl], in_=o1[:])], reads=[o1_d], writes=[dq])
            rms_tile(P, C, b, b_d, 2, 512, 256, sq, sq_d, ps, ps_d, rstd, rstd_d, o2, o2_d)
            P.dma([lambda e: e.dma_start(out=fm(ckvn_d)[:, :, sl], in_=o2[:])], reads=[o2_d], writes=[dkv])
            P.op("vector", lambda e: e.tensor_tensor(out=t1[:], in0=r[:, 0, :], in1=r[:, 2, :], op=ALU.mult), reads=[r_d],
                 writes=[t1_d])
            P.op("gpsimd", lambda e: e.tensor_tensor(out=t2[:], in0=r[:, 1, :], in1=r[:, 3, :], op=ALU.mult), reads=[r_d],
                 writes=[t2_d])
            P.op("vector", lambda e: e.tensor_tensor(out=o3[:], in0=t1[:], in1=t2[:], op=ALU.add), reads=[t1_d, t2_d],
                 writes=[o3_d])
            P.dma([lambda e: e.dma_start(out=krr_d[:, sl], in_=o3[:])], reads=[o3_d], writes=[dkr])
        P.barrier()


MLA_SCALE = 96 ** -0.5


def stage_mla_attention(P, C, cqn_d, ckvn_all_d, krr_all_d, wuq_d, qnorm_d, wukv_d, kvnorm_d, cosq_d, ssinq_d, sel_d, oT_d):
    NKT = S_ALL // 128
    with ExitStack() as st:
        wq, wq_deps = load_weight_bf16(P, st, wuq_d, 768, 2048, scale_d=qnorm_d, name="ma_wq", piece=1024)
        wkv, wkv_deps = load_weight_bf16(P, st, wukv_d, 256, 2048, scale_d=kvnorm_d, name="ma_wkv", piece=1024)
        cqnR = Rot(P, st, "ma_cqn", [128, 6, 512], BF16, 2)
        tabR = Rot(P, st, "ma_tab", [128, 2, 512], F32, 2)
        sel, sel_sd = P.sbuf(st, "ma_sel", [128, 256], F32)
        P.dma([lambda e: e.dma_start(out=sel[:], in_=sel_d[:, :])], writes=[sel_sd])
        KT, KT_d = P.sbuf(st, "ma_KT", [96, S_ALL], BF16)
        P.dma([lambda e: e.dma_start(out=KT[64:96, :], in_=krr_all_d[:, :])], writes=[KT_d])
        Vt, Vt_d = P.sbuf(st, "ma_V", [128, NKT, 128], BF16)
        QT, QT_d = P.sbuf(st, "ma_QT", [96, TLOC], BF16)
        oT, oT_sd = P.sbuf(st, "ma_oT", [128, 8, TLOC], BF16)
        lat = Rot(P, st, "ma_lat", [128, 2, 1024], BF16, 2)
        t1, t1_d = P.sbuf(st, "ma_t1", [96, 512], F32)
        t2, t2_d = P.sbuf(st, "ma_t2", [96, 512], F32)
        pT = Rot(P, st, "ma_pT", [128, 512], BF16, 3)
        osb = Rot(P, st, "ma_osb", [128, 512], F32, 2)
        rcp = Rot(P, st, "ma_rcp", [128, 512], F32, 2)
        ps_s = [P.psum(st, "ma_pss%d" % i, [128, 512]) for i in range(3)]
        ps_o = [P.psum(st, "ma_pso%d" % i, [128, 512]) for i in range(2)]
        ps_g = [P.psum(st, "ma_psg%d" % i, [128, 512]) for i in range(3)]
        gi = [0]
        si = [0]
        for h in range(16):
            c, po = h // 2, (h % 2) * 64
            cb = h * 128
            P.op("gpsimd", lambda e: e.memset(Vt[:], 0.0), writes=[Vt_d])
            onescol = 64 if h % 2 == 0 else 0
            P.op("gpsimd", lambda e: e.memset(Vt[:, :, onescol:onescol + 1], 1.0), writes=[Vt_d])
            for kb in range(S_ALL // 1024):
                lt, lt_d = lat.next()
                P.dma([lambda e: e.dma_start(out=lt[:], in_=fm(ckvn_all_d)[:, :, kb * 1024:(kb + 1) * 1024])], writes=[lt_d])
                for half in range(2):
                    pg, pg_d = ps_g[gi[0] % 3]
                    gi[0] += 1
                    for kc in range(2):
                        P.op("tensor", lambda e, kc=kc: e.matmul(pg[0:64, :], lhsT=wkv[:, kc, cb:cb + 64],
                                                                 rhs=lt[:, kc, half * 512:(half + 1) * 512],
                                                                 start=(kc == 0), stop=(kc == 1)),
                             reads=[wkv_deps[kc], lt_d], writes=[pg_d])
                    k0 = kb * 1024 + half * 512
                    P.op("vector", lambda e: e.tensor_copy(out=KT[0:64, k0:k0 + 512], in_=pg[0:64, :]), reads=[pg_d],
                         writes=[KT_d])
                pg, pg_d = ps_g[gi[0] % 3]
                gi[0] += 1
                for kt in range(8):
                    for kc in range(2):
                        P.op("tensor", lambda e, kc=kc, kt=kt: e.matmul(pg[:, kt * 64:(kt + 1) * 64],
                                                                        lhsT=lt[:, kc, kt * 128:(kt + 1) * 128],
                                                                        rhs=wkv[:, kc, cb + 64:cb + 128],
                                                                        start=(kc == 0), stop=(kc == 1)),
                             reads=[wkv_deps[kc], lt_d], writes=[pg_d])
                P.op("scalar", lambda e: e.copy(out=Vt[:, kb * 8:(kb + 1) * 8, po:po + 64],
                                                in_=pg[:].rearrange("p (t d) -> p t d", d=64)),
                     reads=[pg_d], writes=[Vt_d])
            for qt in range(TLOC // 512):
                qs = slice(qt * 512, (qt + 1) * 512)
                cqn, cqn_sd = cqnR.next()
                tab, tab_d = tabR.next()
                P.dma([lambda e: e.dma_start(out=cqn[:], in_=fm(cqn_d)[:, :, qs])], writes=[cqn_sd])
                P.dma([lambda e: e.dma_start(out=tab[:, 0, :], in_=cosq_d[:, qs]),
                       lambda e: e.dma_start(out=tab[:, 1, :], in_=ssinq_d[:, qs])], writes=[tab_d])
                p1, p1_d = ps_g[gi[0] % 3]
                gi[0] += 1
                for kc in range(6):
                    P.op("tensor", lambda e, kc=kc: e.matmul(p1[0:96, :], lhsT=wq[:, kc, cb:cb + 96], rhs=cqn[:, kc, :],
                                                             start=(kc == 0), stop=(kc == 5)),
                         reads=[wq_deps[kc], cqn_sd], writes=[p1_d])
                p2, p2_d = ps_g[gi[0] % 3]
                gi[0] += 1
                for kc in range(6):
                    P.op("tensor", lambda e, kc=kc: e.matmul(p2[0:96, :], lhsT=wq[:, kc, cb + 32:cb + 128], rhs=cqn[:, kc, :],
                                                             start=(kc == 0), stop=(kc == 5)),
                         reads=[wq_deps[kc], cqn_sd], writes=[p2_d])
                P.op("scalar", lambda e: e.copy(out=QT[0:64, qs], in_=p1[0:64, :]), reads=[], writes=[QT_d, p1_d])
                P.op("vector", lambda e: e.tensor_tensor(out=t1[64:96, :], in0=p1[64:96, :], in1=tab[64:96, 0, :], op=ALU.mult),
                     reads=[p1_d, tab_d], writes=[t1_d])
                P.op("vector", lambda e: e.tensor_tensor(out=t2[64:96, :], in0=p2[64:96, :], in1=tab[64:96, 1, :], op=ALU.mult),
                     reads=[p2_d, tab_d], writes=[t2_d])
                P.op("vector", lambda e: e.tensor_tensor(out=QT[64:96, qs], in0=t1[64:96, :], in1=t2[64:96, :], op=ALU.add),
                     reads=[t1_d, t2_d], writes=[QT_d])
            for qt in range(TLOC // 512):
                qs = slice(qt * 512, (qt + 1) * 512)
                po_, po_d = ps_o[(h * 4 + qt) % 2]
                for kt in range(NKT):
                    pss, pss_d = ps_s[si[0] % 3]
                    si[0] += 1
                    P.op("tensor", lambda e: e.matmul(pss[:], lhsT=KT[0:96, kt * 128:(kt + 1) * 128], rhs=QT[0:96, qs],
                                                      start=True, stop=True), reads=[KT_d, QT_d], writes=[pss_d])
                    p_, p_d = pT.next()
                    P.op("scalar", lambda e: e.activation(out=p_[:], in_=pss[:], func=AF.Exp, scale=MLA_SCALE),
                         reads=[pss_d], writes=[p_d])
                    P.op("tensor", lambda e: e.matmul(po_[:], lhsT=Vt[:, kt, :], rhs=p_[:], start=(kt == 0),
                                                      stop=(kt == NKT - 1)), reads=[Vt_d, p_d], writes=[po_d])
                ob, ob_d = osb.next()
                P.op("vector", lambda e: e.tensor_copy(out=ob[:], in_=po_[:]), reads=[po_d], writes=[ob_d])
                pd, pd_d = ps_g[gi[0] % 3]
                gi[0] += 1
                so = (h % 2) * 128
                P.op("tensor", lambda e: e.matmul(pd[:], lhsT=sel[:, so:so + 128], rhs=ob[:], start=True, stop=True),
                     reads=[sel_sd, ob_d], writes=[pd_d])
                rc, rc_d = rcp.next()
                P.op("vector", lambda e: e.reciprocal(out=rc[:], in_=pd[:]), reads=[pd_d], writes=[rc_d])
                P.op("vector", lambda e: e.tensor_tensor(out=oT[po:po + 64, c, qs], in0=ob[po:po + 64, :],
                                                         in1=rc[po:po + 64, :], op=ALU.mult),
                     reads=[ob_d, rc_d], writes=[oT_sd])
        P.dma([lambda e: e.dma_start(out=fm(oT_d)[:, :, :], in_=oT[:])], reads=[oT_sd], writes=[Dep("maoT", True)])
        P.barrier()


RCH = 128
RNCH = S_ALL // RCH
RGRP = 4


def ret_consts_host(h):
    f32 = np.float32
    lgs = np.log1p(-np.exp2(-5.0 - np.arange(8, dtype=f32))).astype(f32)
    lg_f, lg_b = lgs[h], lgs[7 - h]
    i = np.arange(128, dtype=f32)
    m, c = i[:, None], i[None, :]
    dmT_f = np.where(c >= m, np.exp(np.where(c >= m, c - m, 0) * lg_f), 0).astype(f32)
    dmT_b = np.where(m > c, np.exp(np.where(m > c, m - c, 0) * lg_b), 0).astype(f32)
    qdec_f = np.broadcast_to(np.exp((i + 1) * lg_f)[None, :], (128, 128)).astype(f32)
    qdec_b = np.broadcast_to(np.exp((128 - i) * lg_b)[None, :], (128, 128)).astype(f32)
    cols = np.zeros((128, 4), f32)
    cols[:, 0] = np.exp((127 - i) * lg_f)
    cols[:, 1] = np.exp(i * lg_b)
    cols[:, 2] = np.exp(128 * lg_f)
    cols[:, 3] = np.exp(128 * lg_b)
    return np.ascontiguousarray(np.concatenate([np.eye(128, dtype=f32), dmT_f, dmT_b, qdec_f, qdec_b, cols], axis=1))


def stage_retention(P, q_d, k_d, v_d, gate_d, cos_d, sin_d, rc_d, gnw_d, of_d, y_d):
    with ExitStack() as st:
        rc, rc_dd = P.sbuf(st, "rt_rc", [128, 644], F32)
        P.dma([lambda e: e.dma_start(out=rc[:], in_=rc_d[:, :])], writes=[rc_dd])
        gnw, gnw_dd = P.sbuf(st, "rt_gnw", [128, RGRP, 256], F32)
        P.dma([lambda e: e.dma_start(out=gnw[:], in_=gnw_d[:, :, :])], writes=[gnw_dd])
        ident = rc[:, 0:128]
        S, S_d = P.sbuf(st, "rt_S", [128, 256], F32)
        pp = PsPool(P, st, "rt_ps")
        qg = Rot(P, st, "rt_qg", [128, RGRP, 128], F32, 2)
        kg = Rot(P, st, "rt_kg", [128, RGRP, 128], F32, 2)
        vg = Rot(P, st, "rt_vg", [128, RGRP, 256], F32, 2)
        cg = Rot(P, st, "rt_cg", [128, RGRP, 64], F32, 2)
        sg = Rot(P, st, "rt_sg", [128, RGRP, 64], F32, 2)
        qr = Rot(P, st, "rt_qr", [128, RGRP, 128], F32, 2)
        kr = Rot(P, st, "rt_kr", [128, RGRP, 128], F32, 2)
        ta = Rot(P, st, "rt_ta", [128, RGRP, 64], F32, 2)
        tb_ = Rot(P, st, "rt_tb", [128, RGRP, 64], F32, 2)
        og = Rot(P, st, "rt_og", [128, RGRP, 256], F32, 2)
        ofg = Rot(P, st, "rt_ofg", [128, RGRP, 256], F32, 2)
        gg = Rot(P, st, "rt_gg", [128, RGRP, 256], F32, 2)
        sqg, sqg_d = P.sbuf(st, "rt_sqg", [128, RGRP, 256], F32)
        stat = Rot(P, st, "rt_stat", [128, RGRP], F32, 4)
        Qtb = Rot(P, st, "rt_Qt", [128, 128], F32, 3)
        Ktb = Rot(P, st, "rt_Kt", [128, 128], F32, 3)
        ATb = Rot(P, st, "rt_AT", [128, 128], F32, 3)
        Qdb = Rot(P, st, "rt_Qd", [128, 128], F32, 3)
        Kdb = Rot(P, st, "rt_Kd", [128, 128], F32, 3)
        ofd, yd = Dep("rof", True), Dep("ry", True)
        cpi = [0]

        def evac(out_ap, out_d, ps, ps_d):
            if cpi[0] % 2 == 0:
                P.op("scalar", lambda e: e.copy(out=out_ap, in_=ps), reads=[ps_d], writes=[out_d])
            else:
                P.op("vector", lambda e: e.tensor_copy(out=out_ap, in_=ps), reads=[ps_d], writes=[out_d])
            cpi[0] += 1

        def rows(dt, t0, w):
            return dt[t0:t0 + RGRP * 128, :].rearrange("(n p) d -> p n d", p=128)

        for d in range(2):
            dmT = rc[:, 128 + 128 * d:256 + 128 * d]
            qdec = rc[:, 384 + 128 * d:512 + 128 * d]
            kdec = rc[:, 640 + d:641 + d]
            gcol = rc[:, 642 + d:643 + d]
            P.op("vector", lambda e: e.memset(S[:], 0.0), writes=[S_d])
            order = list(range(RNCH)) if d == 0 else list(range(RNCH - 1, -1, -1))
            groups = [order[i:i + RGRP] for i in range(0, RNCH, RGRP)]
            state = {}

            def load_group(gi):
                c0 = min(groups[gi])
                t0 = c0 * 128
                q, q_dd = qg.next()
                k, k_dd = kg.next()
                v, v_dd = vg.next()
                cs, cs_dd = cg.next()
                sn, sn_dd = sg.next()
                P.dma([lambda e: e.dma_start(out=q[:], in_=rows(q_d, t0, 128))], writes=[q_dd])
                P.dma([lambda e: e.dma_start(out=k[:], in_=rows(k_d, t0, 128))], writes=[k_dd])
                P.dma([lambda e: e.dma_start(out=v[:], in_=rows(v_d, t0, 256))], writes=[v_dd])
                P.dma([lambda e: e.dma_start(out=cs[:], in_=rows(cos_d, t0, 64))], writes=[cs_dd])
                P.dma([lambda e: e.dma_start(out=sn[:], in_=rows(sin_d, t0, 64))], writes=[sn_dd])
                r = dict(c0=c0, v=(v, v_dd), o=og.next())
                if d == 1:
                    of_, of_dd = ofg.next()
                    g_, g_dd = gg.next()
                    P.dma([lambda e: e.dma_start(out=of_[:], in_=rows(of_d, t0, 256))], writes=[of_dd])
                    P.dma([lambda e: e.dma_start(out=g_[:], in_=rows(gate_d, t0, 256))], writes=[g_dd])
                    r["of"] = (of_, of_dd)
                    r["g"] = (g_, g_dd)
                outs = []
                for (x, x_dd, rot, scale) in ((q, q_dd, qr, None), (k, k_dd, kr, 128.0 ** -0.5)):
                    y, y_dd = rot.next()
                    a, a_dd = ta.next()
                    b, b_dd = tb_.next()
                    x1, x2 = x[:, :, 0:64], x[:, :, 64:128]
                    P.op("vector", lambda e: e.tensor_tensor(out=a[:], in0=x1, in1=cs[:], op=ALU.mult), reads=[x_dd, cs_dd],
                         writes=[a_dd])
                    P.op("gpsimd", lambda e: e.tensor_tensor(out=b[:], in0=x2, in1=sn[:], op=ALU.mult), reads=[x_dd, sn_dd],
                         writes=[b_dd])
                    P.op("vector", lambda e: e.tensor_tensor(out=y[:, :, 0:64], in0=a[:], in1=b[:], op=ALU.subtract),
                         reads=[a_dd, b_dd], writes=[y_dd])
                    a, a_dd = ta.next()
                    b, b_dd = tb_.next()
                    P.op("gpsimd", lambda e: e.tensor_tensor(out=a[:], in0=x2, in1=cs[:], op=ALU.mult), reads=[x_dd, cs_dd],
                         writes=[a_dd])
                    P.op("vector", lambda e: e.tensor_tensor(out=b[:], in0=x1, in1=sn[:], op=ALU.mult), reads=[x_dd, sn_dd],
                         writes=[b_dd])
                    P.op("gpsimd", lambda e: e.tensor_tensor(out=y[:, :, 64:128], in0=a[:], in1=b[:], op=ALU.add),
                         reads=[a_dd, b_dd], writes=[y_dd])
                    if scale is not None:
                        P.op("gpsimd", lambda e: e.tensor_scalar(out=y[:], in0=y[:], scalar1=scale, scalar2=None, op0=ALU.mult),
                             reads=[y_dd], writes=[y_dd])
                    outs.append((y, y_dd))
                r["q"], r["k"] = outs
                state[gi] = r

            def prep(gi, n):
                r = state[gi]
                j = n - r["c0"]
                q, q_dd = r["q"]
                k, k_dd = r["k"]
                pq, pq_d = pp.next(128, 128)
                P.op("tensor", lambda e: e.transpose(pq, q[:, j, :], ident), reads=[q_dd, rc_dd], writes=[pq_d])
                Qt, Qt_d = Qtb.next()
                evac(Qt[:], Qt_d, pq, pq_d)
                pk, pk_d = pp.next(128, 128)
                P.op("tensor", lambda e: e.transpose(pk, k[:, j, :], ident), reads=[k_dd, rc_dd], writes=[pk_d])
                Kt, Kt_d = Ktb.next()
                evac(Kt[:], Kt_d, pk, pk_d)
                pin, pin_d = pp.next(128, 128)
                P.op("tensor", lambda e: e.matmul(pin, lhsT=Kt[:], rhs=Qt[:], start=True, stop=True), reads=[Kt_d, Qt_d],
                     writes=[pin_d])
                AT, AT_d = ATb.next()
                P.op("vector", lambda e: e.tensor_tensor(out=AT[:], in0=pin, in1=dmT, op=ALU.mult), reads=[pin_d, rc_dd],
                     writes=[AT_d])
                Qd, Qd_d = Qdb.next()
                P.op("gpsimd", lambda e: e.tensor_tensor(out=Qd[:], in0=Qt[:], in1=qdec, op=ALU.mult), reads=[Qt_d, rc_dd],
                     writes=[Qd_d])
                Kd, Kd_d = Kdb.next()
                P.op("gpsimd", lambda e: e.tensor_scalar(out=Kd[:], in0=k[:, j, :], scalar1=kdec, scalar2=None, op0=ALU.mult),
                     reads=[k_dd, rc_dd], writes=[Kd_d])
                return dict(j=j, AT=(AT, AT_d), Qd=(Qd, Qd_d), Kd=(Kd, Kd_d))

            def scan(gi, n, pr):
                r = state[gi]
                j = pr["j"]
                AT, AT_d = pr["AT"]
                Qd, Qd_d = pr["Qd"]
                Kd, Kd_d = pr["Kd"]
                v, v_dd = r["v"]
                o, o_d = r["o"]
                po, po_d = pp.next(128, 256)
                P.op("tensor", lambda e: e.matmul(po, lhsT=AT[:], rhs=v[:, j, :], start=True, stop=False), reads=[AT_d, v_dd],
                     writes=[po_d])
                P.op("tensor", lambda e: e.matmul(po, lhsT=Qd[:], rhs=S[:], start=False, stop=True), reads=[Qd_d, S_d],
                     writes=[po_d])
                ps, ps_d = pp.next(128, 256)
                P.op("tensor", lambda e: e.matmul(ps, lhsT=Kd[:], rhs=v[:, j, :], start=True, stop=True), reads=[Kd_d, v_dd],
                     writes=[ps_d])
                P.op("scalar", lambda e: e.copy(out=o[:, j, :], in_=po), reads=[po_d], writes=[o_d])
                P.op("vector", lambda e: e.scalar_tensor_tensor(out=S[:], in0=S[:], scalar=gcol, in1=ps, op0=ALU.mult,
                                                                op1=ALU.add), reads=[S_d, rc_dd, ps_d], writes=[S_d])

            def finish_group(gi):
                r = state.pop(gi)
                t0 = r["c0"] * 128
                o, o_d = r["o"]
                if d == 0:
                    P.dma([lambda e: e.dma_start(out=rows(of_d, t0, 256), in_=o[:])], reads=[o_d], writes=[ofd])
                    return
                of_, of_dd = r["of"]
                g_, g_dd = r["g"]
                mu, mu_d = stat.next()
                var, var_d = stat.next()
                P.op("gpsimd", lambda e: e.tensor_tensor(out=o[:], in0=o[:], in1=of_[:], op=ALU.add), reads=[o_d, of_dd],
                     writes=[o_d])
                P.op("vector", lambda e: e.tensor_reduce(out=mu[:], in_=o[:], axis=AX.X, op=ALU.add), reads=[o_d], writes=[mu_d])
                P.op("vector", lambda e: e.tensor_scalar(out=mu[:], in0=mu[:], scalar1=1.0 / 256, scalar2=None, op0=ALU.mult),
                     reads=[mu_d], writes=[mu_d])
                for c in range(RGRP):
                    P.op("vector", lambda e, c=c: e.tensor_scalar(out=o[:, c, :], in0=o[:, c, :], scalar1=mu[:, c:c + 1],
                                                                  scalar2=None, op0=ALU.subtract),
                         reads=[o_d, mu_d], writes=[o_d])
                P.op("scalar", lambda e: e.activation(out=sqg[:], in_=o[:], func=AF.Square), reads=[o_d], writes=[sqg_d])
                P.op("vector", lambda e: e.tensor_reduce(out=var[:], in_=sqg[:], axis=AX.X, op=ALU.add), reads=[sqg_d],
                     writes=[var_d])
                P.op("scalar", lambda e: e.activation(out=var[:], in_=var[:], func=AF.Sqrt, scale=1.0 / 256, bias=EPS),
                     reads=[var_d], writes=[var_d])
                P.op("vector", lambda e: e.reciprocal(out=var[:], in_=var[:]), reads=[var_d], writes=[var_d])
                P.op("scalar", lambda e: e.activation(out=g_[:], in_=g_[:], func=AF.Silu), reads=[g_dd], writes=[g_dd])
                P.op("gpsimd", lambda e: e.tensor_tensor(out=g_[:], in0=g_[:], in1=gnw[:], op=ALU.mult), reads=[g_dd, gnw_dd],
                     writes=[g_dd])
                for c in range(RGRP):
                    P.op("vector", lambda e, c=c: e.scalar_tensor_tensor(out=o[:, c, :], in0=o[:, c, :], scalar=var[:, c:c + 1],
                                                                         in1=g_[:, c, :], op0=ALU.mult, op1=ALU.mult),
                         reads=[o_d, var_d, g_dd], writes=[o_d])
                P.dma([lambda e: e.dma_start(out=rows(y_d, t0, 256), in_=o[:])], reads=[o_d], writes=[yd])

            flat = [(gi, n) for gi, ch in enumerate(groups) for n in ch]
            load_group(0)
            pending = None
            for idx, (gi, n) in enumerate(flat):
                pr = prep(gi, n)
                if pending is not None:
                    pgi, pn, ppr = pending
                    scan(pgi, pn, ppr)
                    if pn == groups[pgi][-1]:
                        finish_group(pgi)
                if n == groups[gi][0] and gi + 1 < len(groups):
                    load_group(gi + 1)
                pending = (gi, n, pr)
            pgi, pn, ppr = pending
            scan(pgi, pn, ppr)
            finish_group(pgi)
            P.barrier()
        P.barrier()


def stage_final_norm(P, C, hT_d, w_d, out_d, T):
    with ExitStack() as st:
        w, w_dd = P.sbuf(st, "fn_w", [128, 8], F32)
        P.dma([lambda e: e.dma_start(out=w[:], in_=w_d[:, :])], writes=[w_dd])
        xs = Rot(P, st, "fn_x", [128, 8, 512], F32, 2)
        os_ = Rot(P, st, "fn_o", [128, 8, 512], F32, 2)
        sq, sq_d = P.sbuf(st, "fn_sq", [128, 8, 512], BF16)
        rstd, rstd_d = P.sbuf(st, "fn_rstd", [128, 512], F32)
        ps, ps_d = P.psum(st, "fn_ps", [128, 512])
        od = Dep("fnout", True)
        for ti in range(T // 512):
            sl = slice(ti * 512, (ti + 1) * 512)
            x, x_d = xs.next()
            o, o_d = os_.next()
            P.dma([lambda e: e.dma_start(out=x[:], in_=fm(hT_d)[:, :, sl])], writes=[x_d])
            P.op("scalar", lambda e: e.activation(out=sq[:], in_=x[:], func=AF.Square), reads=[x_d], writes=[sq_d])
            for k in range(8):
                P.op("tensor", lambda e, k=k: e.matmul(ps[:], lhsT=C.ones_bf[:], rhs=sq[:, k, :], start=(k == 0), stop=(k == 7)),
                     reads=[sq_d, C.ones_bf_d], writes=[ps_d])
            P.op("scalar", lambda e: e.activation(out=rstd[:], in_=ps[:], func=AF.Sqrt, scale=1.0 / 1024, bias=EPS),
                 reads=[ps_d], writes=[rstd_d])
            P.op("vector", lambda e: e.reciprocal(out=rstd[:], in_=rstd[:]), reads=[rstd_d], writes=[rstd_d])
            for k in range(8):
                P.op("vector", lambda e, k=k: e.scalar_tensor_tensor(out=o[:, k, :], in0=x[:, k, :], scalar=w[:, k:k + 1],
                                                                     in1=rstd[:], op0=ALU.mult, op1=ALU.mult),
                     reads=[x_d, w_dd, rstd_d], writes=[o_d])
            P.dma([lambda e: e.dma_start(out=fm(out_d)[:, :, sl], in_=o[:])], reads=[o_d], writes=[od])
        P.barrier()


def build_launch_c():
    nc = bass.Bass("TRN2", target_bir_lowering=False)
    ei = lambda n, s, dt=F32: dram(nc, n, s, dt, "ExternalInput")
    eo = lambda n, s, dt=F32: dram(nc, n, s, dt, "ExternalOutput")
    h1T = ei("h1T", [1024, TLOC])
    ogT = ei("ogT", [1024, TLOC])
    g_wo = ei("g_wo", [1024, 1024])
    mlp_norm = ei("mlp_norm", [128, 8])
    w1 = ei("w1", [1024, 4096])
    w2 = ei("w2", [4096, 1024])
    mla_norm = ei("mla_norm", [128, 8])
    mla_w_in = ei("mla_w_in", [1024, 1280])
    cos_kr = ei("cos_kr", [32, TLOC])
    ssin_kr = ei("ssin_kr", [32, TLOC])
    h2T = eo("h2T", [1024, TLOC])
    cqn = eo("cqn", [768, TLOC], BF16)
    ckvn = eo("ckvn", [256, TLOC], BF16)
    krr = eo("krr", [32, TLOC], BF16)
    hmT = dram(nc, "c_hmT", [1024, TLOC], F32)
    aT = dram(nc, "c_aT", [4096, TLOC], BF16)
    cqT = dram(nc, "c_cqT", [768, TLOC], F32)
    ckvT = dram(nc, "c_ckvT", [256, TLOC], F32)
    krT = dram(nc, "c_krT", [128, TLOC], F32)
    pkrT = dram(nc, "c_pkrT", [128, TLOC], F32)
    with ExitStack() as es:
        P = Prog(nc, es)
        C = Consts(P, es)
        dd = lambda n: Dep(n, dram=True)
        stage_linear(P, C, "gwo", ogT, 1024, TLOC, g_wo, 1024,
                     [dict(kind="fm", n0=0, n1=1024, dst=hmT, dst_dep=dd("hmT"), epi="resadd", res=h1T, dtype=F32)])
        stage_mlp(P, C, "mlp1", hmT, TLOC, w1, w2, mlp_norm, aT, h2T, dd("h2T"))
        stage_linear(P, C, "min", h2T, 1024, TLOC, mla_w_in, 1280,
                     [dict(kind="fm", n0=0, n1=768, dst=cqT, dst_dep=dd("cq"), dtype=F32),
                      dict(kind="fm", n0=768, n1=1024, dst=ckvT, dst_dep=dd("ckv"), dtype=F32),
                      dict(kind="fm", n0=1024, n1=1152, dst=krT, dst_dep=dd("kr"), dtype=F32),
                      dict(kind="fm", n0=1152, n1=1280, dst=pkrT, dst_dep=dd("pkr"), dtype=F32)],
                     nscale_d=mla_norm, norm=True)
        stage_mla_prep(P, C, cqT, ckvT, krT, pkrT, cos_kr, ssin_kr, cqn, ckvn, krr, TLOC)
        print("launch C instructions:", P.nins)
    return nc


def build_launch_d():
    nc = bass.Bass("TRN2", target_bir_lowering=False)
    ei = lambda n, s, dt=F32: dram(nc, n, s, dt, "ExternalInput")
    eo = lambda n, s, dt=F32: dram(nc, n, s, dt, "ExternalOutput")
    h2T = ei("h2T", [1024, TLOC])
    cqn = ei("cqn", [768, TLOC], BF16)
    ckvn_all = ei("ckvn_all", [256, S_ALL], BF16)
    krr_all = ei("krr_all", [32, S_ALL], BF16)
    wuq = ei("wuq", [768, 2048])
    qnorm = ei("qnorm", [128, 6])
    wukv = ei("wukv", [256, 2048])
    kvnorm = ei("kvnorm", [128, 2])
    cosq = ei("cosq", [128, TLOC])
    ssinq = ei("ssinq", [128, TLOC])
    sel = ei("sel", [128, 256])
    m_wo = ei("m_wo", [1024, 1024])
    mlp_norm = ei("mlp_norm", [128, 8])
    w1 = ei("w1", [1024, 4096])
    w2 = ei("w2", [4096, 1024])
    ret_norm = ei("ret_norm", [128, 8])
    ret_w_in = ei("ret_w_in", [1024, 6144])
    h3T = eo("h3T", [1024, TLOC])
    r_q = eo("r_q", [TLOC, 1024])
    r_k = eo("r_k", [TLOC, 1024])
    r_v = eo("r_v", [TLOC, 2048])
    r_g = eo("r_g", [TLOC, 2048])
    oT = dram(nc, "d_oT", [1024, TLOC], BF16)
    hmT = dram(nc, "d_hmT", [1024, TLOC], F32)
    aT = dram(nc, "d_aT", [4096, TLOC], BF16)
    with ExitStack() as es:
        P = Prog(nc, es)
        C = Consts(P, es)
        dd = lambda n: Dep(n, dram=True)
        stage_mla_attention(P, C, cqn, ckvn_all, krr_all, wuq, qnorm, wukv, kvnorm, cosq, ssinq, sel, oT)
        stage_linear(P, C, "mwo", oT, 1024, TLOC, m_wo, 1024,
                     [dict(kind="fm", n0=0, n1=1024, dst=hmT, dst_dep=dd("hmT"), epi="resadd", res=h2T, dtype=F32)],
                     src_dtype=BF16)
        stage_mlp(P, C, "mlp2", hmT, TLOC, w1, w2, mlp_norm, aT, h3T, dd("h3T"))
        stage_linear(P, C, "rin", h3T, 1024, TLOC, ret_w_in, 6144,
                     [dict(kind="tm", n0=0, n1=1024, dst=r_q, dst_dep=dd("rq"), dtype=F32),
                      dict(kind="tm", n0=1024, n1=2048, dst=r_k, dst_dep=dd("rk"), dtype=F32),
                      dict(kind="tm", n0=2048, n1=4096, dst=r_v, dst_dep=dd("rv"), dtype=F32),
                      dict(kind="tm", n0=4096, n1=6144, dst=r_g, dst_dep=dd("rg"), dtype=F32)],
                     nscale_d=ret_norm, norm=True, tile_T=256)
        print("launch D instructions:", P.nins)
    return nc


def build_launch_e():
    nc = bass.Bass("TRN2", target_bir_lowering=False)
    ei = lambda n, s, dt=F32: dram(nc, n, s, dt, "ExternalInput")
    q = ei("q_tm", [S_ALL, 128])
    k = ei("k_tm", [S_ALL, 128])
    v = ei("v_tm", [S_ALL, 256])
    g = ei("gate_tm", [S_ALL, 256])
    cos = ei("cos_tm", [S_ALL, 64])
    sin = ei("sin_tm", [S_ALL, 64])
    rc = ei("rconst", [128, 644])
    gnw = ei("gnw", [128, RGRP, 256])
    y = dram(nc, "y_tm", [S_ALL, 256], F32, "ExternalOutput")
    of = dram(nc, "e_of", [S_ALL, 256], F32)
    with ExitStack() as es:
        P = Prog(nc, es)
        stage_retention(P, q, k, v, g, cos, sin, rc, gnw, of, y)
        print("launch E instructions:", P.nins)
    return nc


def build_launch_f():
    nc = bass.Bass("TRN2", target_bir_lowering=False)
    ei = lambda n, s, dt=F32: dram(nc, n, s, dt, "ExternalInput")
    yT = ei("yT", [2048, TLOC])
    h3T = ei("h3T", [1024, TLOC])
    r_wo = ei("r_wo", [2048, 1024])
    mlp_norm = ei("mlp_norm", [128, 8])
    w1 = ei("w1", [1024, 4096])
    w2 = ei("w2", [4096, 1024])
    fnorm = ei("fnorm", [128, 8])
    outT = dram(nc, "outT", [1024, TLOC], F32, "ExternalOutput")
    hmT = dram(nc, "f_hmT", [1024, TLOC], F32)
    h4T = dram(nc, "f_h4T", [1024, TLOC], F32)
    aT = dram(nc, "f_aT", [4096, TLOC], BF16)
    with ExitStack() as es:
        P = Prog(nc, es)
        C = Consts(P, es)
        dd = lambda n: Dep(n, dram=True)
        stage_linear(P, C, "rwo", yT, 2048, TLOC, r_wo, 1024,
                     [dict(kind="fm", n0=0, n1=1024, dst=hmT, dst_dep=dd("hmT"), epi="resadd", res=h3T, dtype=F32)],
                     tile_T=256)
        stage_mlp(P, C, "mlp3", hmT, TLOC, w1, w2, mlp_norm, aT, h4T, dd("h4T"))
        stage_final_norm(P, C, h4T, fnorm, outT, TLOC)
        print("launch F instructions:", P.nins)
    return nc


def _run(nc, in_maps):
    res = run_bass_kernel_spmd(nc, in_maps, core_ids=list(range(NCORES)))
    return res.results


def gdn_in_maps(qkvT_full, z_tm_full, gates_full, inp):
    gcs = gdn_consts_host()
    conv = inp['gdn_conv'][0]
    maps = []
    for hd in range(8):
        qkv_pre = np.stack([qkvT_full[j * 1024 + hd * 128: j * 1024 + (hd + 1) * 128] for j in range(3)], axis=1)
        ztm = np.ascontiguousarray(z_tm_full[:, hd * 128:(hd + 1) * 128])
        gt = gates_full[:, [hd, 8 + hd, 16 + hd, 24 + hd]]
        gt = np.ascontiguousarray(gt.reshape(NCH, 64, 4).transpose(1, 0, 2))
        cw = np.stack([conv[j * 1024 + hd * 128: j * 1024 + (hd + 1) * 128] for j in range(3)], axis=1)
        sc = np.array([inp['gdn_a_log_f'][0][hd], inp['gdn_a_log_b'][0][hd], inp['gdn_dt_bias_f'][0][hd],
                       inp['gdn_dt_bias_b'][0][hd]], np.float32)
        sc = np.ascontiguousarray(np.broadcast_to(sc[None], (64, 4)))
        on = np.ascontiguousarray(np.broadcast_to(inp['gdn_o_norm'][0][None, None], (64, GRP, 128)))
        maps.append(dict(qkv_pre=np.ascontiguousarray(qkv_pre), z_tm=ztm, gates=gt, conv_w=np.ascontiguousarray(cw),
                         scal=sc, onorm=on, gconst=gcs))
    return maps


def rope_tables(pos, half):
    inv = (1.0 / (10000.0 ** (np.arange(half, dtype=np.float32) / half))).astype(np.float32)
    ang = pos.astype(np.float32)[:, None] * inv[None, :]
    return np.cos(ang).astype(np.float32), np.sin(ang).astype(np.float32)


def kernel(**inp):
    inp = {k: np.asarray(v) for k, v in inp.items()}
    f32 = np.float32
    cores = range(NCORES)
    xTs, biases = na_host_prep(inp['x'][0], inp['na_rpb'][0])
    ra = _run(build_launch_a(), [dict(
        xT=xTs[c], na_norm=pc(inp['na_norm'][0]), w_qkv=inp['na_w_qkv'][0], w_o=inp['na_w_o'][0], bias=biases[c],
        mlp_norm=pc(inp['mlp_norm'][0]), w1=inp['mlp_w1'][0], w2=inp['mlp_w2'][0],
        gdn_norm=pc(inp['gdn_norm'][0]), gdn_w_in=inp['gdn_w_in'][0]) for c in cores])
    qkvT = np.concatenate([ra[c]["g_qkvT"] for c in cores], axis=1)
    ztm = np.concatenate([ra[c]["g_ztm"] for c in cores], axis=0)
    gates = np.concatenate([ra[c]["g_gates"] for c in cores], axis=0)
    rb = _run(build_launch_b(), gdn_in_maps(qkvT, ztm, gates, inp))
    og = np.stack([rb[c]["og"] for c in cores], axis=1).reshape(S_ALL, 1024)
    pos = np.arange(S_ALL)
    c16, s16 = rope_tables(pos, 16)
    cos32 = np.concatenate([c16, c16], axis=1).T
    ssin32 = np.concatenate([-s16, s16], axis=1).T
    w_in = inp['mla_w_in'][0]
    kr_w = w_in[:, 1024:1056]
    pkr_w = np.concatenate([kr_w[:, 16:32], kr_w[:, 0:16]], axis=1)
    zpad = np.zeros((1024, 96), f32)
    w_in_ext = np.ascontiguousarray(np.concatenate([w_in[:, :1024], kr_w, zpad, pkr_w, zpad], axis=1))
    rc_ = _run(build_launch_c(), [dict(
        h1T=ra[c]["h1T"], ogT=np.ascontiguousarray(og[c * TLOC:(c + 1) * TLOC].T), g_wo=inp['gdn_w_o'][0],
        mlp_norm=pc(inp['mlp_norm'][1]), w1=inp['mlp_w1'][1], w2=inp['mlp_w2'][1],
        mla_norm=pc(inp['mla_norm'][0]), mla_w_in=w_in_ext,
        cos_kr=np.ascontiguousarray(cos32[:, c * TLOC:(c + 1) * TLOC]),
        ssin_kr=np.ascontiguousarray(ssin32[:, c * TLOC:(c + 1) * TLOC])) for c in cores])
    ckvn_all = np.ascontiguousarray(np.concatenate([rc_[c]["ckvn"] for c in cores], axis=1))
    krr_all = np.ascontiguousarray(np.concatenate([rc_[c]["krr"] for c in cores], axis=1))
    wuq = inp['mla_w_uq'][0].reshape(768, 16, 96)
    wuq_ext = np.ascontiguousarray(np.concatenate(
        [wuq, wuq[:, :, 80:96], wuq[:, :, 64:80]], axis=2).reshape(768, 2048))
    cosq = np.zeros((128, S_ALL), f32)
    ssinq = np.zeros((128, S_ALL), f32)
    cosq[64:96] = cos32
    ssinq[64:96] = ssin32
    sel = np.zeros((128, 256), f32)
    sel[64, 0:128] = 1.0
    sel[0, 128:256] = 1.0
    rd = _run(build_launch_d(), [dict(
        h2T=rc_[c]["h2T"], cqn=rc_[c]["cqn"], ckvn_all=ckvn_all, krr_all=krr_all, wuq=wuq_ext,
        qnorm=pc(inp['mla_q_norm'][0]), wukv=inp['mla_w_ukv'][0], kvnorm=pc(inp['mla_kv_norm'][0]),
        cosq=np.ascontiguousarray(cosq[:, c * TLOC:(c + 1) * TLOC]),
        ssinq=np.ascontiguousarray(ssinq[:, c * TLOC:(c + 1) * TLOC]), sel=sel, m_wo=inp['mla_w_o'][0],
        mlp_norm=pc(inp['mlp_norm'][2]), w1=inp['mlp_w1'][2], w2=inp['mlp_w2'][2],
        ret_norm=pc(inp['ret_norm'][0]), ret_w_in=inp['ret_w_in'][0]) for c in cores])
    r_q = np.concatenate([rd[c]["r_q"] for c in cores], axis=0)
    r_k = np.concatenate([rd[c]["r_k"] for c in cores], axis=0)
    r_v = np.concatenate([rd[c]["r_v"] for c in cores], axis=0)
    r_g = np.concatenate([rd[c]["r_g"] for c in cores], axis=0)
    c64, s64 = rope_tables(pos, 64)
    gn = inp['ret_gn'][0]
    re_ = _run(build_launch_e(), [dict(
        q_tm=np.ascontiguousarray(r_q[:, h * 128:(h + 1) * 128]), k_tm=np.ascontiguousarray(r_k[:, h * 128:(h + 1) * 128]),
        v_tm=np.ascontiguousarray(r_v[:, h * 256:(h + 1) * 256]), gate_tm=np.ascontiguousarray(r_g[:, h * 256:(h + 1) * 256]),
        cos_tm=c64, sin_tm=s64, rconst=ret_consts_host(h),
        gnw=np.ascontiguousarray(np.broadcast_to(gn[h * 256:(h + 1) * 256][None, None], (128, RGRP, 256)))) for h in cores])
    y = np.stack([re_[h]["y_tm"] for h in cores], axis=1).reshape(S_ALL, 2048)
    rf = _run(build_launch_f(), [dict(
        yT=np.ascontiguousarray(y[c * TLOC:(c + 1) * TLOC].T), h3T=rd[c]["h3T"], r_wo=inp['ret_w_o'][0],
        mlp_norm=pc(inp['mlp_norm'][3]), w1=inp['mlp_w1'][3], w2=inp['mlp_w2'][3],
        fnorm=pc(inp['final_norm'])) for c in cores])
    out = np.concatenate([rf[c]["outT"].T for c in cores], axis=0)
    return np.ascontiguousarray(out.reshape(1, S_ALL, 1024).astype(f32))
```

```python
import numpy as np
from contextlib import ExitStack
import concourse.bass as bass
import concourse.mybir as mybir
from concourse.bass_utils import run_bass_kernel_spmd

F32 = mybir.dt.float32
BF16 = mybir.dt.bfloat16
AF = mybir.ActivationFunctionType
ALU = mybir.AluOpType
AX = mybir.AxisListType

NCORES = 8
EPS = 1e-6


class Dep:
    __slots__ = ("name", "w", "r", "dsem", "dram")

    def __init__(self, name="", dram=False):
        self.name = name
        self.w = []
        self.r = []
        self.dsem = None
        self.dram = dram


class Prog:
    ENG = ("tensor", "vector", "scalar", "gpsimd", "sync")

    def __init__(self, nc, es, ndsem=90):
        self.nc = nc
        self.es = es
        self.eng = {"tensor": nc.tensor, "vector": nc.vector, "scalar": nc.scalar,
                    "gpsimd": nc.gpsimd, "sync": nc.sync}
        self.sem = {}
        self.cnt = {}
        self.seen = {e: {} for e in self.ENG}
        for e in self.ENG:
            self.sem[e] = es.enter_context(nc.semaphore("s_" + e))
            self.cnt[e] = 0
        self.free_dsem = []
        for i in range(ndsem):
            k = "d%d" % i
            self.sem[k] = es.enter_context(nc.semaphore(k))
            self.cnt[k] = 0
            self.free_dsem.append(k)
        self.stage_dsem = []
        self.nins = 0

    def sbuf(self, st, name, shape, dtype):
        t = st.enter_context(self.nc.sbuf_tensor(name, list(shape), dtype))
        return t, Dep(name)

    def psum(self, st, name, shape, dtype=F32):
        t = st.enter_context(self.nc.psum_tensor(name, list(shape), dtype))
        return t, Dep(name)

    def _dsem(self, d):
        if d.dsem is None:
            d.dsem = self.free_dsem.pop()
            self.stage_dsem.append((d, d.dsem))
        return d.dsem

    def _collect(self, eng, reads, writes, same_engine_sync=True):
        need = {}
        for d in reads:
            for (k, v) in d.w:
                if need.get(k, 0) < v:
                    need[k] = v
        for d in writes:
            if d.dram:
                continue
            for (k, v) in d.w:
                if need.get(k, 0) < v:
                    need[k] = v
            for (k, v) in d.r:
                if need.get(k, 0) < v:
                    need[k] = v
        seen = self.seen[eng]
        e = self.eng[eng]
        for k, v in need.items():
            if k == eng and not same_engine_sync:
                continue
            if seen.get(k, 0) >= v:
                continue
            seen[k] = v
            e.wait_ge(self.sem[k], v)
            self.nins += 1

    def _finish(self, comp, reads, writes):
        for d in writes:
            if d.dram:
                d.w = [c for c in d.w if c[0] != comp[0]] + [comp]
                continue
            d.w = [comp]
            d.r = []
        for d in reads:
            if d in writes:
                continue
            d.r = [c for c in d.r if c[0] != comp[0]] + [comp]

    def op(self, eng, fn, reads=(), writes=()):
        self._collect(eng, reads, writes, same_engine_sync=(eng != "tensor"))
        self.cnt[eng] += 1
        comp = (eng, self.cnt[eng])
        ins = fn(self.eng[eng])
        ins.then_inc(self.sem[eng], 1)
        self.nins += 1
        self._finish(comp, reads, writes)
        return comp

    def dma(self, fns, reads=(), writes=(), semdep=None, queue="sync"):
        if semdep is None:
            cand = [d for d in list(writes) + list(reads) if not d.dram]
            semdep = cand[0]
        key = self._dsem(semdep)
        self._collect(queue, reads, writes, same_engine_sync=False)
        e = self.eng[queue]
        for f in fns:
            f(e).then_inc(self.sem[key], 16)
            self.nins += 1
        self.cnt[key] += 16 * len(fns)
        comp = (key, self.cnt[key])
        self._finish(comp, reads, writes)
        return comp

    def barrier(self):
        for en in self.ENG:
            e = self.eng[en]
            seen = self.seen[en]
            for k, v in self.cnt.items():
                if k == en or v == 0:
                    continue
                if seen.get(k, 0) >= v:
                    continue
                seen[k] = v
                e.wait_ge(self.sem[k], v)
                self.nins += 1
        for (d, k) in self.stage_dsem:
            d.dsem = None
            self.free_dsem.append(k)
        self.stage_dsem = []


def dram(nc, name, shape, dtype, kind="Internal"):
    return nc.dram_tensor(name, list(shape), dtype, kind=kind).ap()


def fm(ap, p=128):
    return ap.rearrange("(c p) t -> p c t", p=p)


def load_weight_bf16(P, st, W_d, K, N, scale_d=None, name="w", piece=2048):
    KC = K // 128
    wb, wb_d = P.sbuf(st, name + "_b", [128, KC, N], BF16)
    deps = [Dep(name + "_b%d" % k) for k in range(KC)]
    sc = None
    if scale_d is not None:
        sc, sc_d = P.sbuf(st, name + "_sc", [128, KC], F32)
        P.dma([lambda e: e.dma_start(out=sc[:], in_=scale_d[:, :])], writes=[sc_d])
    stg = [P.sbuf(st, name + "_stg%d" % i, [128, piece], F32) for i in range(2)]
    it = 0
    for k in range(KC):
        for n0 in range(0, N, piece):
            n1 = min(N, n0 + piece)
            s, s_d = stg[it % 2]
            P.dma([lambda e, s=s, k=k, n0=n0, n1=n1: e.dma_start(out=s[:, 0:n1 - n0], in_=W_d[k * 128:(k + 1) * 128, n0:n1])],
                  writes=[s_d])
            eng = "vector" if it % 2 == 0 else "gpsimd"
            if sc is not None:
                P.op(eng, lambda e, s=s, k=k, n0=n0, n1=n1: e.tensor_scalar(
                    out=wb[:, k, n0:n1], in0=s[:, 0:n1 - n0], scalar1=sc[:, k:k + 1], scalar2=None, op0=ALU.mult),
                    reads=[s_d, sc_d], writes=[deps[k]])
            else:
                P.op(eng, lambda e, s=s, k=k, n0=n0, n1=n1: e.tensor_copy(out=wb[:, k, n0:n1], in_=s[:, 0:n1 - n0]),
                     reads=[s_d], writes=[deps[k]])
            it += 1
    return wb, deps


class Consts:
    def __init__(self, P, st):
        self.ones_bf, self.ones_bf_d = P.sbuf(st, "c_ones_bf", [128, 128], BF16)
        P.op("vector", lambda e: e.memset(self.ones_bf[:], 1.0), writes=[self.ones_bf_d])
        self.ones_f, self.ones_f_d = P.sbuf(st, "c_ones_f", [128, 128], F32)
        P.op("vector", lambda e: e.memset(self.ones_f[:], 1.0), writes=[self.ones_f_d])


def rms_tile(P, C, xt, xt_d, KC, T, D, sq, sq_d, ps, ps_d, rstd, rstd_d, xn, xn_d, eps=EPS):
    P.op("scalar", lambda e: e.activation(out=sq[:, 0:KC, 0:T], in_=xt[:, 0:KC, 0:T], func=AF.Square),
         reads=[xt_d], writes=[sq_d])
    for k in range(KC):
        P.op("tensor", lambda e, k=k: e.matmul(ps[:, 0:T], lhsT=C.ones_bf[:], rhs=sq[:, k, 0:T],
                                                  start=(k == 0), stop=(k == KC - 1)),
             reads=[sq_d, C.ones_bf_d], writes=[ps_d])
    P.op("scalar", lambda e: e.activation(out=rstd[:, 0:T], in_=ps[:, 0:T], func=AF.Sqrt, scale=1.0 / D, bias=eps),
         reads=[ps_d], writes=[rstd_d])
    P.op("vector", lambda e: e.reciprocal(out=rstd[:, 0:T], in_=rstd[:, 0:T]), reads=[rstd_d], writes=[rstd_d])
    for k in range(KC):
        eng = "vector" if k % 2 == 0 else "gpsimd"
        P.op(eng, lambda e, k=k: e.tensor_tensor(out=xn[:, k, 0:T], in0=xt[:, k, 0:T], in1=rstd[:, 0:T], op=ALU.mult),
             reads=[xt_d, rstd_d], writes=[xn_d])


def stage_linear(P, C, name, src_d, K, T, W_d, N, sinks, nscale_d=None, norm=False, src_dtype=F32,
                 tile_T=512, D_norm=None):
    nc = P.nc
    KC = K // 128
    with ExitStack() as st:
        wb, wdeps = load_weight_bf16(P, st, W_d, K, N, scale_d=nscale_d, name=name + "_w")
        xts = [P.sbuf(st, name + "_xt%d" % i, [128, KC, tile_T], src_dtype) for i in range(2)]
        need_cast = norm or (src_dtype != BF16)
        if need_cast:
            xns = [P.sbuf(st, name + "_xn%d" % i, [128, KC, tile_T], BF16) for i in range(2)]
        if norm:
            sq, sq_d = P.sbuf(st, name + "_sq", [128, KC, tile_T], BF16)
            rstd, rstd_d = P.sbuf(st, name + "_rstd", [128, tile_T], F32)
            ps_ss, ps_ss_d = P.psum(st, name + "_psss", [128, 512])
        pss = [P.psum(st, name + "_ps%d" % i, [128, 512]) for i in range(4)]
        psi = [0]
        outs = []
        for si, s in enumerate(sinks):
            ncs = (s["n1"] - s["n0"]) // 128
            if s["kind"] == "fm":
                o = [P.sbuf(st, name + "_o%d_%d" % (si, i), [128, ncs, tile_T], s["dtype"]) for i in range(2)]
                r = None
                if s.get("epi") == "resadd":
                    r = [P.sbuf(st, name + "_r%d_%d" % (si, i), [128, ncs, tile_T], F32) for i in range(2)]
                tmp = None
                if s.get("epi") == "relu2":
                    tmp = [P.sbuf(st, name + "_t%d_%d" % (si, i), [128, tile_T], F32) for i in range(2)]
                outs.append((o, r, tmp))
            elif s.get("aug"):
                o = [P.sbuf(st, name + "_o%d_%d" % (si, i), [128, (s["n1"] - s["n0"]) // 64, 65], s["dtype"])
                     for i in range(2)]
                for (oo, oo_d) in o:
                    P.op("vector", lambda e, oo=oo: e.memset(oo[:], 1.0), writes=[oo_d])
                outs.append((o, None, None))
            else:
                o = [P.sbuf(st, name + "_o%d_%d" % (si, i), [128, s["n1"] - s["n0"]], s["dtype"]) for i in range(2)]
                outs.append((o, None, None))
        ntiles = T // tile_T
        srcv = fm(src_d)

        def load(ti):
            xt, xt_d = xts[ti % 2]
            P.dma([lambda e: e.dma_start(out=xt[:], in_=srcv[:, :, ti * tile_T:(ti + 1) * tile_T])], writes=[xt_d])
            for si, s in enumerate(sinks):
                if s.get("epi") == "resadd":
                    r, r_d = outs[si][1][ti % 2]
                    P.dma([lambda e, r=r, s=s: e.dma_start(
                        out=r[:], in_=fm(s["res"])[:, :, ti * tile_T:(ti + 1) * tile_T])], writes=[r_d])

        load(0)
        cp = 0
        tmcount = 0
        for ti in range(ntiles):
            if ti + 1 < ntiles:
                load(ti + 1)
            xt, xt_d = xts[ti % 2]
            if norm:
                xn, xn_d = xns[ti % 2]
                rms_tile(P, C, xt, xt_d, KC, tile_T, D_norm or K, sq, sq_d, ps_ss, ps_ss_d, rstd, rstd_d, xn, xn_d)
            elif need_cast:
                xn, xn_d = xns[ti % 2]
                for k in range(KC):
                    eng = ("vector", "gpsimd")[k % 2]
                    P.op(eng, lambda e, k=k: e.tensor_copy(out=xn[:, k, :], in_=xt[:, k, :]), reads=[xt_d], writes=[xn_d])
            else:
                xn, xn_d = xt, xt_d
            for si, s in enumerate(sinks):
                n0, n1 = s["n0"], s["n1"]
                if s["kind"] == "fm":
                    (ob, rb, tb) = outs[si]
                    o, o_d = ob[ti % 2]
                    ncs = (n1 - n0) // 128
                    for c in range(ncs):
                        ps, ps_d = pss[psi[0] % 4]
                        psi[0] += 1
                        for k in range(KC):
                            P.op("tensor", lambda e, k=k, c=c, ps=ps: e.matmul(
                                ps[:, 0:tile_T], lhsT=wb[:, k, n0 + c * 128:n0 + (c + 1) * 128], rhs=xn[:, k, :],
                                start=(k == 0), stop=(k == KC - 1)), reads=[wdeps[k], xn_d], writes=[ps_d])
                        epi = s.get("epi", "copy")
                        if epi == "copy":
                            if cp % 2 == 0:
                                P.op("scalar", lambda e, c=c, ps=ps: e.copy(out=o[:, c, :], in_=ps[:, 0:tile_T]),
                                     reads=[ps_d], writes=[o_d])
                            else:
                                P.op("vector", lambda e, c=c, ps=ps: e.tensor_copy(out=o[:, c, :], in_=ps[:, 0:tile_T]),
                                     reads=[ps_d], writes=[o_d])
                            cp += 1
                        elif epi == "relu2":
                            t, t_d = tb[c % 2]
                            P.op("scalar", lambda e, ps=ps, t=t: e.activation(out=t[:], in_=ps[:, 0:tile_T], func=AF.Relu),
                                 reads=[ps_d], writes=[t_d])
                            eng = ("vector", "gpsimd")[c % 2]
                            P.op(eng, lambda e, c=c, t=t: e.tensor_tensor(out=o[:, c, :], in0=t[:], in1=t[:], op=ALU.mult),
                                 reads=[t_d], writes=[o_d])
                        elif epi == "resadd":
                            r, r_d = rb[ti % 2]
                            P.op("vector", lambda e, c=c, ps=ps, r=r: e.tensor_tensor(
                                out=o[:, c, :], in0=ps[:, 0:tile_T], in1=r[:, c, :], op=ALU.add),
                                reads=[ps_d, r_d], writes=[o_d])
                    P.dma([lambda e, o=o: e.dma_start(out=fm(s["dst"])[:, :, ti * tile_T:(ti + 1) * tile_T], in_=o[:])],
                          reads=[o_d], writes=[s["dst_dep"]])
                else:
                    (ob, _, _) = outs[si]
                    for tk in range(tile_T // 128):
                        o, o_d = ob[tmcount % 2]
                        tmcount += 1
                        for nb in range(n0, n1, 512):
                            ne = min(n1, nb + 512)
                            ps, ps_d = pss[psi[0] % 4]
                            psi[0] += 1
                            for k in range(KC):
                                P.op("tensor", lambda e, k=k, ps=ps, nb=nb, ne=ne, tk=tk: e.matmul(
                                    ps[:, 0:ne - nb], lhsT=xn[:, k, tk * 128:(tk + 1) * 128], rhs=wb[:, k, nb:ne],
                                    start=(k == 0), stop=(k == KC - 1)), reads=[wdeps[k], xn_d], writes=[ps_d])
                            if s.get("aug"):
                                oview = o[:, (nb - n0) // 64:(ne - n0) // 64, 0:64]
                                pview = ps[:, 0:ne - nb].rearrange("p (h d) -> p h d", d=64)
                            else:
                                oview = o[:, nb - n0:ne - n0]
                                pview = ps[:, 0:ne - nb]
                            if cp % 2 == 0:
                                P.op("scalar", lambda e: e.copy(out=oview, in_=pview), reads=[ps_d], writes=[o_d])
                            else:
                                P.op("vector", lambda e: e.tensor_copy(out=oview, in_=pview), reads=[ps_d], writes=[o_d])
                            cp += 1
                        r0 = ti * tile_T + tk * 128
                        oflat = o[:].rearrange("p h d -> p (h d)") if s.get("aug") else o[:]
                        P.dma([lambda e: e.dma_start(out=s["dst"][r0:r0 + 128, :], in_=oflat)],
                              reads=[o_d], writes=[s["dst_dep"]])
        P.barrier()


TLOC = 2048
HALO = 256
TEXT = TLOC + 2 * HALO
NA_H = 16
NA_DH = 64


def na_variant(i):
    return {0: 0, 1: 1, 14: 3, 15: 4}.get(i, 2)


def stage_na_attention(P, C, QT_d, KT_d, V_d, bias_d, oT_d, oT_dep):
    with ExitStack() as st:
        ident, ident_d = P.sbuf(st, "na_ident", [128, 128], BF16)
        idf, idf_d = P.sbuf(st, "na_idf", [128, 128], F32)
        P.op("gpsimd", lambda e: e.memset(idf[:], 1.0), writes=[idf_d])
        P.op("gpsimd", lambda e: e.affine_select(out=idf[:], in_=idf[:], pattern=[[-1, 128]], compare_op=ALU.is_equal,
                                                  fill=0.0, base=0, channel_multiplier=1), reads=[idf_d], writes=[idf_d])
        P.op("vector", lambda e: e.tensor_copy(out=ident[:], in_=idf[:]), reads=[idf_d], writes=[ident_d])
        kts = [P.sbuf(st, "na_kt%d" % i, [128, 8, 640], BF16) for i in range(2)]
        qt, qt_d = P.sbuf(st, "na_qt", [128, 8, TLOC], BF16)
        vts = [P.sbuf(st, "na_vt%d" % i, [128, 5, 16, 65], BF16) for i in range(2)]
        bts = [P.sbuf(st, "na_bt%d" % i, [128, 5, 128], F32) for i in range(3)]
        sbs = [P.sbuf(st, "na_sb%d" % i, [128, 5, 128], F32) for i in range(2)]
        pts = [P.sbuf(st, "na_pt%d" % i, [128, 5, 128], BF16) for i in range(2)]
        rcs = [P.sbuf(st, "na_rc%d" % i, [128, 1], F32) for i in range(2)]
        otm = [P.sbuf(st, "na_otm%d" % i, [128, 16, 64], BF16) for i in range(2)]
        oT, oT_sd = P.sbuf(st, "na_oT", [128, 8, TLOC], BF16)
        ps_s = [P.psum(st, "na_pss%d" % i, [128, 8, 128]) for i in range(2)]
        ps_o = [P.psum(st, "na_pso%d" % i, [128, 512]) for i in range(2)]
        ps_t = [P.psum(st, "na_pst%d" % i, [128, 8, 128], BF16) for i in range(1)]
        KTv, QTv = fm(KT_d), fm(QT_d)
        Vv = V_d.rearrange("(t p) f -> p t f", p=128)
        nblk = TLOC // 128
        P.dma([lambda e: e.dma_start(out=qt[:], in_=QTv[:, :, HALO:HALO + TLOC])], writes=[qt_d])

        def load_blk(i):
            kt, kt_d = kts[i % 2]
            vt, vt_d = vts[i % 2]
            P.dma([lambda e: e.dma_start(out=kt[:], in_=KTv[:, :, 128 * i:128 * i + 640])], writes=[kt_d])
            P.dma([lambda e: e.dma_start(out=vt[:].rearrange("p t h d -> p t (h d)"), in_=Vv[:, i:i + 5, :])],
                  writes=[vt_d])

        def load_bias(it):
            i, h = divmod(it, NA_H)
            bt, bt_d = bts[it % 3]
            P.dma([lambda e: e.dma_start(out=bt[:], in_=bias_d[na_variant(i), h])], writes=[bt_d])

        load_blk(0)
        load_bias(0)
        load_bias(1)
        it = 0
        for i in range(nblk):
            if i + 1 < nblk:
                load_blk(i + 1)
            kt, kt_d = kts[i % 2]
            vt, vt_d = vts[i % 2]
            ot, ot_d = otm[i % 2]
            for h in range(NA_H):
                if it + 2 < nblk * NA_H:
                    load_bias(it + 2)
                bt, bt_d = bts[it % 3]
                pss, pss_d = ps_s[it % 2]
                pso, pso_d = ps_o[it % 2]
                sb, sb_d = sbs[it % 2]
                pt, pt_d = pts[it % 2]
                rc, rc_d = rcs[it % 2]
                c, po = h // 2, (h % 2) * 64
                for t in range(5):
                    P.op("tensor", lambda e, t=t: e.matmul(pss[:, t, :], lhsT=kt[po:po + 64, c, t * 128:(t + 1) * 128],
                                                           rhs=qt[po:po + 64, c, 128 * i:128 * i + 128], start=True, stop=True),
                         reads=[kt_d, qt_d], writes=[pss_d])
                P.op("vector", lambda e: e.scalar_tensor_tensor(out=sb[:], in0=pss[:, 0:5, :], scalar=NA_DH ** -0.5,
                                                                in1=bt[:], op0=ALU.mult, op1=ALU.add),
                     reads=[pss_d, bt_d], writes=[sb_d])
                P.op("scalar", lambda e: e.activation(out=pt[:], in_=sb[:], func=AF.Exp), reads=[sb_d], writes=[pt_d])
                for t in range(5):
                    P.op("tensor", lambda e, t=t: e.matmul(pso[:, 0:65], lhsT=pt[:, t, :], rhs=vt[:, t, h, :],
                                                           start=(t == 0), stop=(t == 4)),
                         reads=[pt_d, vt_d], writes=[pso_d])
                P.op("vector", lambda e: e.reciprocal(out=rc[:], in_=pso[:, 64:65]), reads=[pso_d], writes=[rc_d])
                P.op("vector", lambda e: e.tensor_scalar(out=ot[:, h, :], in0=pso[:, 0:64], scalar1=rc[:, 0:1],
                                                         scalar2=None, op0=ALU.mult),
                     reads=[pso_d, rc_d], writes=[ot_d])
                it += 1
            pst, pst_d = ps_t[0]
            for cc in range(8):
                P.op("tensor", lambda e, cc=cc: e.transpose(pst[:, cc, :], ot[:, 2 * cc:2 * cc + 2, :], ident[:]),
                     reads=[ot_d, ident_d], writes=[pst_d])
            P.op("scalar", lambda e: e.copy(out=oT[:, :, 128 * i:128 * i + 128], in_=pst[:]), reads=[pst_d],
                 writes=[oT_sd])
        P.dma([lambda e: e.dma_start(out=fm(oT_d)[:, :, :], in_=oT[:])], reads=[oT_sd], writes=[oT_dep])
        P.barrier()


def stage_mlp(P, C, name, hT_d, T, w1_d, w2_d, nscale_d, aT_d, out_d, out_dep, tile_T=512):
    aT_dep = Dep(name + "_aT", dram=True)
    stage_linear(P, C, name + "a", hT_d, 1024, T, w1_d, 4096,
                 [dict(kind="fm", n0=0, n1=4096, dst=aT_d, dst_dep=aT_dep, epi="relu2", dtype=BF16)],
                 nscale_d=nscale_d, norm=True, tile_T=tile_T)
    stage_linear(P, C, name + "b", aT_d, 4096, T, w2_d, 1024,
                 [dict(kind="fm", n0=0, n1=1024, dst=out_d, dst_dep=out_dep, epi="resadd", res=hT_d, dtype=F32)],
                 src_dtype=BF16, tile_T=256)


def build_launch_a():
    nc = bass.Bass("TRN2", target_bir_lowering=False)
    xT = dram(nc, "xT", [1024, TEXT], F32, "ExternalInput")
    na_norm = dram(nc, "na_norm", [128, 8], F32, "ExternalInput")
    w_qkv = dram(nc, "w_qkv", [1024, 3072], F32, "ExternalInput")
    w_o = dram(nc, "w_o", [1024, 1024], F32, "ExternalInput")
    bias = dram(nc, "bias", [5, 16, 128, 5, 128], F32, "ExternalInput")
    mlp_norm = dram(nc, "mlp_norm", [128, 8], F32, "ExternalInput")
    w1 = dram(nc, "w1", [1024, 4096], F32, "ExternalInput")
    w2 = dram(nc, "w2", [4096, 1024], F32, "ExternalInput")
    h1T = dram(nc, "h1T", [1024, TLOC], F32, "ExternalOutput")
    gdn_norm = dram(nc, "gdn_norm", [128, 8], F32, "ExternalInput")
    gdn_w_in = dram(nc, "gdn_w_in", [1024, 4128], F32, "ExternalInput")
    g_qkvT = dram(nc, "g_qkvT", [3072, TLOC], F32, "ExternalOutput")
    g_ztm = dram(nc, "g_ztm", [TLOC, 1024], F32, "ExternalOutput")
    g_gates = dram(nc, "g_gates", [TLOC, 32], F32, "ExternalOutput")
    QT = dram(nc, "QT", [1024, TEXT], BF16)
    KT = dram(nc, "KT", [1024, TEXT], BF16)
    V = dram(nc, "V", [TEXT, 16 * 65], BF16)
    oT = dram(nc, "oT", [1024, TLOC], BF16)
    hmT = dram(nc, "hmT", [1024, TLOC], F32)
    aT = dram(nc, "aT", [4096, TLOC], BF16)
    with ExitStack() as es:
        P = Prog(nc, es)
        C = Consts(P, es)
        dd = lambda n: Dep(n, dram=True)
        stage_linear(P, C, "qkv", xT, 1024, TEXT, w_qkv, 3072,
                     [dict(kind="fm", n0=0, n1=1024, dst=QT, dst_dep=dd("QT"), dtype=BF16),
                      dict(kind="fm", n0=1024, n1=2048, dst=KT, dst_dep=dd("KT"), dtype=BF16),
                      dict(kind="tm", n0=2048, n1=3072, dst=V, dst_dep=dd("V"), dtype=BF16, aug=True)],
                     nscale_d=na_norm, norm=True)
        stage_na_attention(P, C, QT, KT, V, bias, oT, dd("oT"))
        stage_linear(P, C, "wo", oT, 1024, TLOC, w_o, 1024,
                     [dict(kind="fm", n0=0, n1=1024, dst=hmT, dst_dep=dd("hmT"), epi="resadd",
                           res=xT[:, HALO:HALO + TLOC], dtype=F32)], src_dtype=BF16)
        stage_mlp(P, C, "mlp0", hmT, TLOC, w1, w2, mlp_norm, aT, h1T, dd("h1T"))
        stage_linear(P, C, "gin", h1T, 1024, TLOC, gdn_w_in, 4128,
                     [dict(kind="fm", n0=0, n1=1024, dst=g_qkvT[0:1024, :], dst_dep=dd("gq"), dtype=F32),
                      dict(kind="fm", n0=1024, n1=2048, dst=g_qkvT[1024:2048, :], dst_dep=dd("gk"), dtype=F32),
                      dict(kind="fm", n0=2048, n1=3072, dst=g_qkvT[2048:3072, :], dst_dep=dd("gv"), dtype=F32),
                      dict(kind="tm", n0=3072, n1=4096, dst=g_ztm, dst_dep=dd("gz"), dtype=F32),
                      dict(kind="tm", n0=4096, n1=4128, dst=g_gates, dst_dep=dd("gg"), dtype=F32)],
                     nscale_d=gdn_norm, norm=True, tile_T=256)
        print("launch A instructions:", P.nins)
    return nc


def na_host_prep(x, na_rpb):
    xg = x.reshape(256, 64, 1024)
    rpb = na_rpb
    xTs, biases = [], []
    for c in range(NCORES):
        R0 = 32 * c
        slot_row = np.arange(40) + R0 - 4
        if c == 0:
            slot_row[0:4] = [6, 7, 6, 7]
        if c == NCORES - 1:
            slot_row[36:40] = [248, 249, 248, 249]
        xe = xg[slot_row].reshape(TEXT, 1024)
        xTs.append(np.ascontiguousarray(xe.T))
        tab = np.full((5, 16, 640, 128), -30000.0, np.float32)
        for vi, i in enumerate([0, 1, 2, 14, 15]):
            slots = np.arange(2 * i, 2 * i + 10)
            krow = slot_row[slots]
            first = np.array([list(krow).index(r) == j for j, r in enumerate(krow)])
            kr = np.repeat(krow, 64)
            kvalid = np.repeat(first, 64)
            kc = np.tile(np.arange(64), 10)
            qr = np.repeat(R0 + 2 * i + np.arange(2), 64)
            qc = np.tile(np.arange(64), 2)
            qr0 = np.clip(qr - 4, 0, 248)
            qc0 = np.clip(qc - 8, 0, 48)
            ok = (kr[:, None] >= qr0[None, :]) & (kr[:, None] < qr0[None, :] + 8) & kvalid[:, None] \
                & (kc[:, None] >= qc0[None, :]) & (kc[:, None] < qc0[None, :] + 16)
            drow = np.clip(kr[:, None] - qr[None, :] + 7, 0, 14)
            dcol = np.clip(kc[:, None] - qc[None, :] + 15, 0, 30)
            g = rpb[:, drow, dcol]
            tab[vi] = np.where(ok[None], g, np.float32(-30000.0))
        biases.append(np.ascontiguousarray(tab.reshape(5, 16, 5, 128, 128).transpose(0, 1, 3, 2, 4)))
    return xTs, biases


def pc(v):
    return np.ascontiguousarray(np.asarray(v).reshape(-1, 128).T)


S_ALL = 16384
GCH = 64
NCH = S_ALL // GCH
GRP = 8


def gdn_consts_host():
    j = np.arange(64)
    NEG = -30000.0
    I128 = np.eye(128, dtype=np.float32)

    def pad(m):
        out = np.zeros((128, 64), np.float32)
        out[:64] = m
        return out
    tri_f = (j[:, None] <= j[None, :])
    tri_b = (j[:, None] >= j[None, :])
    us_f = (j[:, None] > j[None, :])
    us_b = (j[:, None] < j[None, :])
    negs_f = np.where(j[:, None] > j[None, :], 0, NEG)
    negiT_f = np.where(j[None, :] >= j[:, None], 0, NEG)
    negs_b = np.where(j[:, None] < j[None, :], 0, NEG)
    negiT_b = np.where(j[None, :] <= j[:, None], 0, NEG)
    blocks = [tri_f, tri_b, us_f, us_b, negs_f, negiT_f, negs_b, negiT_b]
    return np.ascontiguousarray(np.concatenate([I128] + [pad(np.asarray(b, np.float32)) for b in blocks], axis=1))


class Rot:
    def __init__(self, P, st, name, shape, dtype, n):
        self.bufs = [P.sbuf(st, "%s%d" % (name, i), shape, dtype) for i in range(n)]
        self.i = 0

    def next(self):
        b = self.bufs[self.i % len(self.bufs)]
        self.i += 1
        return b


class PsPool:
    def __init__(self, P, st, name, nbanks=8):
        self.tiles = []
        for b in range(nbanks):
            t, d = P.psum(st, "%s%d" % (name, b), [128, 512])
            self.tiles.append((t, d))
        self.i = 0

    def next(self, p, f):
        t, d = self.tiles[self.i % len(self.tiles)]
        self.i += 1
        return t[0:p, 0:f], d


def gdn_stage_prep(P, qkv_pre, conv_w, gconst_sb, qT_d, kT_d, ktm_d, vtm_d):
    gc, gc_d = gconst_sb
    with ExitStack() as st:
        cw, cw_d = P.sbuf(st, "gp_cw", [128, 3, 5], F32)
        P.dma([lambda e: e.dma_start(out=cw[:], in_=conv_w[:, :, :])], writes=[cw_d])
        onesf, onesf_d = P.sbuf(st, "gp_ones", [128, 128], F32)
        P.op("vector", lambda e: e.memset(onesf[:], 1.0), writes=[onesf_d])
        xin = Rot(P, st, "gp_x", [128, 3, 516], F32, 2)
        acc = Rot(P, st, "gp_acc", [128, 512], F32, 2)
        yb = Rot(P, st, "gp_y", [128, 512], F32, 3)
        sqb = Rot(P, st, "gp_sq", [128, 512], F32, 2)
        rsb = Rot(P, st, "gp_rs", [128, 512], F32, 2)
        ynb = Rot(P, st, "gp_yn", [128, 512], F32, 4)
        tmb = Rot(P, st, "gp_tm", [128, 4, 128], F32, 4)
        pss = [P.psum(st, "gp_ps%d" % i, [128, 512]) for i in range(2)]
        pst = [P.psum(st, "gp_pt%d" % i, [128, 512]) for i in range(4)]
        nt = S_ALL // 512
        qd, kd, ktd, vtd = Dep("qT", True), Dep("kT", True), Dep("ktm", True), Dep("vtm", True)
        ki = 0
        for ti in range(nt):
            t0 = ti * 512
            x, x_d = xin.next()
            lo, hi = max(t0 - 2, 0), min(t0 + 514, S_ALL)
            if ti == 0:
                P.op("gpsimd", lambda e: e.memset(x[:, :, 0:2], 0.0), writes=[x_d])
            if ti == nt - 1:
                P.op("gpsimd", lambda e: e.memset(x[:, :, 514:516], 0.0), writes=[x_d])
            c0 = lo - (t0 - 2)
            P.dma([lambda e: e.dma_start(out=x[:, :, c0:c0 + (hi - lo)], in_=qkv_pre[:, :, lo:hi])], writes=[x_d])
            for j in range(3):
                a, a_d = acc.next()
                P.op("vector", lambda e: e.tensor_scalar(out=a[:], in0=x[:, j, 0:512], scalar1=cw[:, j, 0:1], scalar2=None,
                                                         op0=ALU.mult), reads=[x_d, cw_d], writes=[a_d])
                for tap in range(1, 5):
                    P.op("vector", lambda e, tap=tap: e.scalar_tensor_tensor(
                        out=a[:], in0=x[:, j, tap:tap + 512], scalar=cw[:, j, tap:tap + 1], in1=a[:],
                        op0=ALU.mult, op1=ALU.add), reads=[x_d, cw_d, a_d], writes=[a_d])
                y, y_d = yb.next()
                P.op("scalar", lambda e: e.activation(out=y[:], in_=a[:], func=AF.Silu), reads=[a_d], writes=[y_d])
                if j < 2:
                    sq, sq_d = sqb.next()
                    P.op("scalar", lambda e: e.activation(out=sq[:], in_=y[:], func=AF.Square), reads=[y_d], writes=[sq_d])
                    ps, ps_d = pss[ki % 2]
                    ki += 1
                    P.op("tensor", lambda e: e.matmul(ps[:], lhsT=onesf[:], rhs=sq[:], start=True, stop=True),
                         reads=[onesf_d, sq_d], writes=[ps_d])
                    rs, rs_d = rsb.next()
                    sc = 128.0 if j == 0 else 1.0
                    P.op("scalar", lambda e: e.activation(out=rs[:], in_=ps[:], func=AF.Sqrt, scale=sc, bias=sc * EPS),
                         reads=[ps_d], writes=[rs_d])
                    P.op("vector", lambda e: e.reciprocal(out=rs[:], in_=rs[:]), reads=[rs_d], writes=[rs_d])
                    yn, yn_d = ynb.next()
                    P.op("gpsimd", lambda e: e.tensor_tensor(out=yn[:], in0=y[:], in1=rs[:], op=ALU.mult),
                         reads=[y_d, rs_d], writes=[yn_d])
                    dst, dstd = (qT_d, qd) if j == 0 else (kT_d, kd)
                    P.dma([lambda e: e.dma_start(out=dst[:, t0:t0 + 512], in_=yn[:])], reads=[yn_d], writes=[dstd])
                else:
                    yn, yn_d = y, y_d
                if j >= 1:
                    tm, tm_d = tmb.next()
                    for b in range(4):
                        pt, pt_d = pst[(ti * 8 + j * 4 + b) % 4]
                        P.op("tensor", lambda e, b=b: e.transpose(pt[:, 0:128], yn[:, b * 128:(b + 1) * 128], gc[:, 0:128]),
                             reads=[yn_d, gc_d], writes=[pt_d])
                        if b % 2 == 0:
                            P.op("scalar", lambda e, b=b: e.copy(out=tm[:, b, :], in_=pt[:, 0:128]), reads=[pt_d], writes=[tm_d])
                        else:
                            P.op("vector", lambda e, b=b: e.tensor_copy(out=tm[:, b, :], in_=pt[:, 0:128]), reads=[pt_d],
                                 writes=[tm_d])
                    dst, dstd = (ktm_d, ktd) if j == 1 else (vtm_d, vtd)
                    P.dma([lambda e: e.dma_start(out=dst[t0:t0 + 512, :].rearrange("(b p) d -> p b d", p=128), in_=tm[:])],
                          reads=[tm_d], writes=[dstd])
        P.barrier()


def gdn_stage_gates(P, st, gates_d, scal_d, gconst_sb):
    gc, gc_d = gconst_sb
    out = {}
    pers = []
    for d in range(2):
        pers.append(dict(
            g=P.sbuf(st, "gg_g%d" % d, [64, NCH], F32), beta=P.sbuf(st, "gg_beta%d" % d, [64, NCH], F32),
            eG=P.sbuf(st, "gg_eG%d" % d, [64, NCH], F32), beG=P.sbuf(st, "gg_beG%d" % d, [64, NCH], F32),
            kdec=P.sbuf(st, "gg_kdec%d" % d, [64, NCH], F32), gl=P.sbuf(st, "gg_gl%d" % d, [128, NCH], F32)))
    with ExitStack() as tmp:
        gt, gt_d = P.sbuf(tmp, "gg_gt", [64, NCH, 4], F32)
        sc, sc_d = P.sbuf(tmp, "gg_sc", [64, 4], F32)
        P.dma([lambda e: e.dma_start(out=gt[:], in_=gates_d[:, :, :])], writes=[gt_d])
        P.dma([lambda e: e.dma_start(out=sc[:], in_=scal_d[:, :])], writes=[sc_d])
        onesf, onesf_d = P.sbuf(tmp, "gg_ones", [64, 128], F32)
        P.op("vector", lambda e: e.memset(onesf[:], 1.0), writes=[onesf_d])
        t1, t1_d = P.sbuf(tmp, "gg_t1", [64, NCH], F32)
        t2, t2_d = P.sbuf(tmp, "gg_t2", [64, NCH], F32)
        t3, t3_d = P.sbuf(tmp, "gg_t3", [64, NCH], F32)
        Gs, Gs_d = P.sbuf(tmp, "gg_G", [64, NCH], F32)
        Gt, Gt_d = P.sbuf(tmp, "gg_Gt", [64, NCH], F32)
        ac, ac_d = P.sbuf(tmp, "gg_ac", [64, 1], F32)
        psG, psG_d = P.psum(tmp, "gg_psG", [128, 512])
        psT, psT_d = P.psum(tmp, "gg_psT", [128, 512])
        for d in range(2):
            (g, g_d), (beta, beta_d), (eG, eG_d) = pers[d]["g"], pers[d]["beta"], pers[d]["eG"]
            (beG, beG_d), (kdec, kdec_d), (gl, gl_d) = pers[d]["beG"], pers[d]["kdec"], pers[d]["gl"]
            P.op("vector", lambda e: e.tensor_scalar(out=t1[:], in0=gt[:, :, 2 + d], scalar1=sc[:, 2 + d:3 + d], scalar2=None,
                                                     op0=ALU.add), reads=[gt_d, sc_d], writes=[t1_d])
            P.op("scalar", lambda e: e.activation(out=t2[:], in_=t1[:], func=AF.Abs), reads=[t1_d], writes=[t2_d])
            P.op("scalar", lambda e: e.activation(out=t2[:], in_=t2[:], func=AF.Exp, scale=-1.0), reads=[t2_d], writes=[t2_d])
            P.op("scalar", lambda e: e.activation(out=t2[:], in_=t2[:], func=AF.Ln, bias=1.0), reads=[t2_d], writes=[t2_d])
            P.op("vector", lambda e: e.tensor_scalar(out=t3[:], in0=t1[:], scalar1=0.0, scalar2=None, op0=ALU.max),
                 reads=[t1_d], writes=[t3_d])
            P.op("vector", lambda e: e.tensor_tensor(out=t3[:], in0=t3[:], in1=t2[:], op=ALU.add), reads=[t3_d, t2_d],
                 writes=[t3_d])
            P.op("scalar", lambda e: e.activation(out=ac[:], in_=sc[:, d:d + 1], func=AF.Exp), reads=[sc_d], writes=[ac_d])
            P.op("vector", lambda e: e.tensor_scalar(out=g[:], in0=t3[:], scalar1=ac[:, 0:1], scalar2=-1.0, op0=ALU.mult,
                                                     op1=ALU.mult), reads=[t3_d, ac_d], writes=[g_d])
            P.op("scalar", lambda e: e.activation(out=beta[:], in_=gt[:, :, d], func=AF.Sigmoid), reads=[gt_d], writes=[beta_d])
            tri = gc[0:64, 128 + 64 * d:128 + 64 * d + 64]
            P.op("tensor", lambda e: e.matmul(psG[0:64, 0:NCH], lhsT=tri, rhs=g[:], start=True, stop=True),
                 reads=[gc_d, g_d], writes=[psG_d])
            P.op("tensor", lambda e: e.matmul(psT[:, 0:NCH], lhsT=onesf[:], rhs=g[:], start=True, stop=True),
                 reads=[onesf_d, g_d], writes=[psT_d])
            P.op("vector", lambda e: e.tensor_copy(out=Gs[:], in_=psG[0:64, 0:NCH]), reads=[psG_d], writes=[Gs_d])
            P.op("scalar", lambda e: e.activation(out=gl[:], in_=psT[:, 0:NCH], func=AF.Exp), reads=[psT_d], writes=[gl_d])
            P.op("vector", lambda e: e.tensor_tensor(out=Gt[:], in0=psT[0:64, 0:NCH], in1=Gs[:], op=ALU.subtract),
                 reads=[Gs_d], writes=[Gt_d, psT_d])
            P.op("scalar", lambda e: e.activation(out=kdec[:], in_=Gt[:], func=AF.Exp), reads=[Gt_d], writes=[kdec_d])
            P.op("scalar", lambda e: e.activation(out=eG[:], in_=Gs[:], func=AF.Exp), reads=[Gs_d], writes=[eG_d])
            P.op("vector", lambda e: e.tensor_tensor(out=beG[:], in0=beta[:], in1=eG[:], op=ALU.mult), reads=[beta_d, eG_d],
                 writes=[beG_d])
            out[d] = dict(g=(g, g_d), beta=(beta, beta_d), eG=(eG, eG_d), beG=(beG, beG_d), kdec=(kdec, kdec_d), gl=(gl, gl_d))
        P.barrier()
    return out


def gdn_stage_scan(P, G, gconst_sb, qT_d, kT_d, ktm_d, vtm_d, of_d, z_d, onorm_d, og_d, dbg=None):
    gc, gc_d = gconst_sb
    I64 = gc[0:64, 0:64]
    with ExitStack() as st:
        pp = PsPool(P, st, "gs_ps")
        S, S_d = P.sbuf(st, "gs_S", [128, 128], F32)
        onr, onr_d = P.sbuf(st, "gs_onr", [64, GRP, 128], F32)
        P.dma([lambda e: e.dma_start(out=onr[:], in_=onorm_d[:, :, :])], writes=[onr_d])
        ktg = Rot(P, st, "gs_ktg", [128, GRP * 64], F32, 2)
        qtg = Rot(P, st, "gs_qtg", [128, GRP * 64], F32, 2)
        kg = Rot(P, st, "gs_kg", [64, GRP, 128], F32, 2)
        vg = Rot(P, st, "gs_vg", [64, GRP, 128], F32, 2)
        og = Rot(P, st, "gs_og", [64, GRP, 128], F32, 2)
        ofg = Rot(P, st, "gs_ofg", [64, GRP, 128], F32, 2)
        zg = Rot(P, st, "gs_zg", [64, GRP, 128], F32, 2)
        sqg = Rot(P, st, "gs_sqg", [64, GRP, 128], F32, 1)
        ssg = Rot(P, st, "gs_ssg", [64, GRP], F32, 2)
        gtri = Rot(P, st, "gs_gtri", [64, 64], F32, 2)
        Dm = Rot(P, st, "gs_Dm", [64, 64], F32, 2)
        DTm = Rot(P, st, "gs_DTm", [64, 64], F32, 2)
        Lb = Rot(P, st, "gs_L", [64, 64], F32, 2)
        Nb = Rot(P, st, "gs_N", [64, 64], F32, 2)
        Ztb = Rot(P, st, "gs_Zt", [64, 64], F32, 2)
        Pb = Rot(P, st, "gs_P", [64, 64], F32, 4)
        Ptb = Rot(P, st, "gs_Pt", [64, 64], F32, 4)
        rwb = Rot(P, st, "gs_rw", [64, 128], F32, 2)
        vbb = Rot(P, st, "gs_vb", [64, 128], F32, 2)
        ATb = Rot(P, st, "gs_AT", [64, 64], F32, 3)
        kdb = Rot(P, st, "gs_kd", [64, 128], F32, 3)
        wTb = Rot(P, st, "gs_wT", [128, 64], F32, 3)
        ub = Rot(P, st, "gs_u", [64, 128], F32, 3)
        vnb = Rot(P, st, "gs_vn", [64, 128], F32, 2)
        tb = Rot(P, st, "gs_t", [64, 128], F32, 2)
        ofd, ogd = Dep("of", True), Dep("og", True)
        cpi = [0]

        def evac(out_ap, out_d, ps, ps_d):
            if cpi[0] % 2 == 0:
                P.op("scalar", lambda e: e.copy(out=out_ap, in_=ps), reads=[ps_d], writes=[out_d])
            else:
                P.op("vector", lambda e: e.tensor_copy(out=out_ap, in_=ps), reads=[ps_d], writes=[out_d])
            cpi[0] += 1

        for d in range(2 if dbg is None else 1):
            gd = G[d]
            (g, g_d), (beta, beta_d), (eG, eG_d) = gd["g"], gd["beta"], gd["eG"]
            (beG, beG_d), (kdec, kdec_d), (gl, gl_d) = gd["beG"], gd["kdec"], gd["gl"]
            tri = gc[0:64, 128 + 64 * d:192 + 64 * d]
            us = gc[0:64, 256 + 64 * d:320 + 64 * d]
            negs = gc[0:64, 384 + 128 * d:448 + 128 * d]
            negiT = gc[0:64, 448 + 128 * d:512 + 128 * d]
            P.op("vector", lambda e: e.memset(S[:], 0.0), writes=[S_d])
            order = list(range(NCH)) if d == 0 else list(range(NCH - 1, -1, -1))
            groups = [order[i:i + GRP] for i in range(0, NCH, GRP)]
            if dbg is not None:
                groups = groups[:dbg]
            state = {}

            def load_group(gi):
                chunks = groups[gi]
                c0 = min(chunks)
                t0 = c0 * 64
                kt, kt_d = ktg.next()
                qt, qt_d = qtg.next()
                k, k_d = kg.next()
                v, v_d = vg.next()
                P.dma([lambda e: e.dma_start(out=kt[:], in_=kT_d[:, t0:t0 + GRP * 64])], writes=[kt_d])
                P.dma([lambda e: e.dma_start(out=qt[:], in_=qT_d[:, t0:t0 + GRP * 64])], writes=[qt_d])
                P.dma([lambda e: e.dma_start(out=k[:], in_=ktm_d[t0:t0 + GRP * 64, :].rearrange("(n p) d -> p n d", p=64))],
                      writes=[k_d])
                P.dma([lambda e: e.dma_start(out=v[:], in_=vtm_d[t0:t0 + GRP * 64, :].rearrange("(n p) d -> p n d", p=64))],
                      writes=[v_d])
                r = dict(c0=c0, kt=(kt, kt_d), qt=(qt, qt_d), k=(k, k_d), v=(v, v_d), o=og.next())
                if d == 1:
                    of_, of_dd = ofg.next()
                    z_, z_dd = zg.next()
                    P.dma([lambda e: e.dma_start(out=of_[:], in_=of_d[t0:t0 + GRP * 64, :].rearrange("(n p) d -> p n d", p=64))],
                          writes=[of_dd])
                    P.dma([lambda e: e.dma_start(out=z_[:], in_=z_d[t0:t0 + GRP * 64, :].rearrange("(n p) d -> p n d", p=64))],
                          writes=[z_dd])
                    r["of"] = (of_, of_dd)
                    r["z"] = (z_, z_dd)
                state[gi] = r

            def prep(gi, n):
                r = state[gi]
                ln = n - r["c0"]
                kt, kt_d = r["kt"]
                qt, qt_d = r["qt"]
                k, k_d = r["k"]
                v, v_d = r["v"]
                ktn = kt[:, ln * 64:(ln + 1) * 64]
                qtn = qt[:, ln * 64:(ln + 1) * 64]
                gt_, gt_dd = gtri.next()
                P.op("gpsimd", lambda e: e.tensor_scalar(out=gt_[:], in0=tri, scalar1=g[:, n:n + 1], scalar2=None, op0=ALU.mult),
                     reads=[gc_d, g_d], writes=[gt_dd])
                ps1, ps1_d = pp.next(64, 64)
                P.op("tensor", lambda e: e.matmul(ps1, lhsT=gt_[:], rhs=us, start=True, stop=False), reads=[gt_dd, gc_d],
                     writes=[ps1_d])
                P.op("tensor", lambda e: e.matmul(ps1, lhsT=I64, rhs=negs, start=False, stop=True), reads=[gc_d], writes=[ps1_d])
                ps2, ps2_d = pp.next(64, 64)
                P.op("tensor", lambda e: e.matmul(ps2, lhsT=us, rhs=gt_[:], start=True, stop=False), reads=[gt_dd, gc_d],
                     writes=[ps2_d])
                P.op("tensor", lambda e: e.matmul(ps2, lhsT=I64, rhs=negiT, start=False, stop=True), reads=[gc_d], writes=[ps2_d])
                dm, dm_d = Dm.next()
                dtm, dtm_d = DTm.next()
                P.op("scalar", lambda e: e.activation(out=dm[:], in_=ps1, func=AF.Exp), reads=[ps1_d], writes=[dm_d])
                P.op("scalar", lambda e: e.activation(out=dtm[:], in_=ps2, func=AF.Exp), reads=[ps2_d], writes=[dtm_d])
                pkk, pkk_d = pp.next(64, 64)
                P.op("tensor", lambda e: e.matmul(pkk, lhsT=ktn, rhs=ktn, start=True, stop=True), reads=[kt_d], writes=[pkk_d])
                pkq, pkq_d = pp.next(64, 64)
                P.op("tensor", lambda e: e.matmul(pkq, lhsT=ktn, rhs=qtn, start=True, stop=True), reads=[kt_d, qt_d],
                     writes=[pkq_d])
                L, L_d = Lb.next()
                P.op("vector", lambda e: e.scalar_tensor_tensor(out=L[:], in0=pkk, scalar=beta[:, n:n + 1], in1=dm[:],
                                                                op0=ALU.mult, op1=ALU.mult),
                     reads=[pkk_d, beta_d, dm_d], writes=[L_d])
                AT, AT_d = ATb.next()
                P.op("vector", lambda e: e.tensor_tensor(out=AT[:], in0=pkq, in1=dtm[:], op=ALU.mult), reads=[pkq_d, dtm_d],
                     writes=[AT_d])
                pn, pn_d = pp.next(64, 64)
                P.op("tensor", lambda e: e.transpose(pn, L[:], I64), reads=[L_d, gc_d], writes=[pn_d])
                N, N_d = Nb.next()
                evac(N[:], N_d, pn, pn_d)
                Zt, Zt_d = Ztb.next()
                P.op("gpsimd", lambda e: e.tensor_tensor(out=Zt[:], in0=I64, in1=N[:], op=ALU.subtract), reads=[gc_d, N_d],
                     writes=[Zt_d])
                Pc, Pc_d, Ptc, Ptc_d = L, L_d, N, N_d
                for lvl in range(1, 6):
                    pP, pP_d = pp.next(64, 64)
                    P.op("tensor", lambda e: e.matmul(pP, lhsT=Ptc[:], rhs=Pc[:], start=True, stop=True), reads=[Ptc_d, Pc_d],
                         writes=[pP_d])
                    Pn, Pn_d = Pb.next()
                    evac(Pn[:], Pn_d, pP, pP_d)
                    if lvl < 5:
                        pPt, pPt_d = pp.next(64, 64)
                        P.op("tensor", lambda e: e.matmul(pPt, lhsT=Pc[:], rhs=Ptc[:], start=True, stop=True),
                             reads=[Ptc_d, Pc_d], writes=[pPt_d])
                        Ptn, Ptn_d = Ptb.next()
                        evac(Ptn[:], Ptn_d, pPt, pPt_d)
                    pZ, pZ_d = pp.next(64, 64)
                    P.op("tensor", lambda e: e.matmul(pZ, lhsT=Pn[:], rhs=Zt[:], start=True, stop=True), reads=[Pn_d, Zt_d],
                         writes=[pZ_d])
                    P.op("vector", lambda e: e.tensor_tensor(out=Zt[:], in0=pZ, in1=Zt[:], op=ALU.add), reads=[pZ_d, Zt_d],
                         writes=[Zt_d])
                    Pc, Pc_d = Pn, Pn_d
                    if lvl < 5:
                        Ptc, Ptc_d = Ptn, Ptn_d
                rw, rw_d = rwb.next()
                vb, vb_d = vbb.next()
                kd_, kd_d = kdb.next()
                P.op("gpsimd", lambda e: e.tensor_scalar(out=rw[:], in0=k[:, ln, :], scalar1=beG[:, n:n + 1], scalar2=None,
                                                         op0=ALU.mult), reads=[k_d, beG_d], writes=[rw_d])
                P.op("gpsimd", lambda e: e.tensor_scalar(out=vb[:], in0=v[:, ln, :], scalar1=beta[:, n:n + 1], scalar2=None,
                                                         op0=ALU.mult), reads=[v_d, beta_d], writes=[vb_d])
                P.op("gpsimd", lambda e: e.tensor_scalar(out=kd_[:], in0=k[:, ln, :], scalar1=kdec[:, n:n + 1], scalar2=None,
                                                         op0=ALU.mult), reads=[k_d, kdec_d], writes=[kd_d])
                pw, pw_d = pp.next(128, 64)
                P.op("tensor", lambda e: e.matmul(pw, lhsT=rw[:], rhs=Zt[:], start=True, stop=True), reads=[rw_d, Zt_d],
                     writes=[pw_d])
                wT, wT_d = wTb.next()
                evac(wT[:], wT_d, pw, pw_d)
                pu, pu_d = pp.next(64, 128)
                P.op("tensor", lambda e: e.matmul(pu, lhsT=Zt[:], rhs=vb[:], start=True, stop=True), reads=[Zt_d, vb_d],
                     writes=[pu_d])
                u, u_d = ub.next()
                evac(u[:], u_d, pu, pu_d)
                return dict(wT=(wT, wT_d), u=(u, u_d), AT=(AT, AT_d), kd=(kd_, kd_d), qtn=qtn, qt_d=qt_d, ln=ln)

            def scan(gi, n, pr):
                r = state[gi]
                ln = pr["ln"]
                wT, wT_d = pr["wT"]
                u, u_d = pr["u"]
                AT, AT_d = pr["AT"]
                kd_, kd_d = pr["kd"]
                o, o_d = r["o"]
                pws, pws_d = pp.next(64, 128)
                P.op("tensor", lambda e: e.matmul(pws, lhsT=wT[:], rhs=S[:], start=True, stop=True), reads=[wT_d, S_d],
                     writes=[pws_d])
                vn, vn_d = vnb.next()
                P.op("vector", lambda e: e.tensor_tensor(out=vn[:], in0=u[:], in1=pws, op=ALU.subtract), reads=[u_d, pws_d],
                     writes=[vn_d])
                pqs, pqs_d = pp.next(64, 128)
                P.op("tensor", lambda e: e.matmul(pqs, lhsT=pr["qtn"], rhs=S[:], start=True, stop=True),
                     reads=[pr["qt_d"], S_d], writes=[pqs_d])
                pkv, pkv_d = pp.next(128, 128)
                P.op("tensor", lambda e: e.matmul(pkv, lhsT=kd_[:], rhs=vn[:], start=True, stop=True), reads=[kd_d, vn_d],
                     writes=[pkv_d])
                pav, pav_d = pp.next(64, 128)
                P.op("tensor", lambda e: e.matmul(pav, lhsT=AT[:], rhs=vn[:], start=True, stop=True), reads=[AT_d, vn_d],
                     writes=[pav_d])
                P.op("vector", lambda e: e.scalar_tensor_tensor(out=S[:], in0=S[:], scalar=gl[:, n:n + 1], in1=pkv,
                                                                op0=ALU.mult, op1=ALU.add),
                     reads=[S_d, gl_d, pkv_d], writes=[S_d])
                t_, t_d = tb.next()
                P.op("scalar", lambda e: e.copy(out=t_[:], in_=pav), reads=[pav_d], writes=[t_d])
                P.op("vector", lambda e: e.scalar_tensor_tensor(out=o[:, ln, :], in0=pqs, scalar=eG[:, n:n + 1], in1=t_[:],
                                                                op0=ALU.mult, op1=ALU.add),
                     reads=[pqs_d, eG_d, t_d], writes=[o_d])

            def finish_group(gi):
                r = state.pop(gi)
                t0 = r["c0"] * 64
                o, o_d = r["o"]
                if d == 0:
                    P.dma([lambda e: e.dma_start(out=of_d[t0:t0 + GRP * 64, :].rearrange("(n p) d -> p n d", p=64), in_=o[:])],
                          reads=[o_d], writes=[ofd])
                    return
                of_, of_dd = r["of"]
                z_, z_dd = r["z"]
                sq, sq_d = sqg.next()
                ss, ss_d = ssg.next()
                P.op("gpsimd", lambda e: e.tensor_tensor(out=o[:], in0=o[:], in1=of_[:], op=ALU.add), reads=[o_d, of_dd],
                     writes=[o_d])
                P.op("scalar", lambda e: e.activation(out=sq[:], in_=o[:], func=AF.Square), reads=[o_d], writes=[sq_d])
                P.op("vector", lambda e: e.tensor_reduce(out=ss[:], in_=sq[:], axis=AX.X, op=ALU.add), reads=[sq_d],
                     writes=[ss_d])
                P.op("scalar", lambda e: e.activation(out=ss[:], in_=ss[:], func=AF.Sqrt, scale=1.0 / 128, bias=EPS),
                     reads=[ss_d], writes=[ss_d])
                P.op("vector", lambda e: e.reciprocal(out=ss[:], in_=ss[:]), reads=[ss_d], writes=[ss_d])
                P.op("scalar", lambda e: e.activation(out=z_[:], in_=z_[:], func=AF.Silu), reads=[z_dd], writes=[z_dd])
                P.op("gpsimd", lambda e: e.tensor_tensor(out=z_[:], in0=z_[:], in1=onr[:], op=ALU.mult), reads=[z_dd, onr_d],
                     writes=[z_dd])
                for c in range(GRP):
                    P.op("vector", lambda e, c=c: e.scalar_tensor_tensor(out=o[:, c, :], in0=o[:, c, :], scalar=ss[:, c:c + 1],
                                                                         in1=z_[:, c, :], op0=ALU.mult, op1=ALU.mult),
                         reads=[o_d, ss_d, z_dd], writes=[o_d])
                P.dma([lambda e: e.dma_start(out=og_d[t0:t0 + GRP * 64, :].rearrange("(n p) d -> p n d", p=64), in_=o[:])],
                      reads=[o_d], writes=[ogd])

            flat = [(gi, n) for gi, ch in enumerate(groups) for n in ch]
            load_group(0)
            pending = None
            for idx, (gi, n) in enumerate(flat):
                pr = prep(gi, n)
                if pending is not None:
                    pgi, pn, ppr = pending
                    scan(pgi, pn, ppr)
                    if pn == groups[pgi][-1]:
                        finish_group(pgi)
                if n == groups[gi][0] and gi + 1 < len(groups):
                    load_group(gi + 1)
                pending = (gi, n, pr)
            pgi, pn, ppr = pending
            scan(pgi, pn, ppr)
            finish_group(pgi)
            P.barrier()
        P.barrier()


def build_launch_b(dbg=None):
    nc = bass.Bass("TRN2", target_bir_lowering=False)
    ik = "Internal" if dbg is None else "ExternalOutput"
    qkv_pre = dram(nc, "qkv_pre", [128, 3, S_ALL], F32, "ExternalInput")
    z_tm = dram(nc, "z_tm", [S_ALL, 128], F32, "ExternalInput")
    gates = dram(nc, "gates", [64, NCH, 4], F32, "ExternalInput")
    conv_w = dram(nc, "conv_w", [128, 3, 5], F32, "ExternalInput")
    scal = dram(nc, "scal", [64, 4], F32, "ExternalInput")
    onorm = dram(nc, "onorm", [64, GRP, 128], F32, "ExternalInput")
    gconst = dram(nc, "gconst", [128, 640], F32, "ExternalInput")
    og = dram(nc, "og", [S_ALL, 128], F32, "ExternalOutput")
    qT = dram(nc, "g_qT", [128, S_ALL], F32, ik)
    kT = dram(nc, "g_kT", [128, S_ALL], F32, ik)
    ktm = dram(nc, "g_ktm", [S_ALL, 128], F32, ik)
    vtm = dram(nc, "g_vtm", [S_ALL, 128], F32, ik)
    of = dram(nc, "g_of", [S_ALL, 128], F32, ik)
    with ExitStack() as es:
        P = Prog(nc, es)
        gc = P.sbuf(es, "gconst_sb", [128, 640], F32)
        P.dma([lambda e: e.dma_start(out=gc[0][:], in_=gconst[:, :])], writes=[gc[1]])
        gdn_stage_prep(P, qkv_pre, conv_w, gc, qT, kT, ktm, vtm)
        G = gdn_stage_gates(P, es, gates, scal, gc)
        gdn_stage_scan(P, G, gc, qT, kT, ktm, vtm, of, z_tm, onorm, og, dbg=dbg)
        print("launch B instructions:", P.nins)
    return nc


def stage_mla_prep(P, C, cqT_d, ckvT_d, krT_d, pkrT_d, cos_d, ssin_d, cqn_d, ckvn_d, krr_d, T):
    with ExitStack() as st:
        xq = Rot(P, st, "mp_xq", [128, 6, 512], F32, 2)
        xkv = Rot(P, st, "mp_xkv", [128, 2, 512], F32, 2)
        xr = Rot(P, st, "mp_xr", [32, 4, 512], F32, 2)
        sq, sq_d = P.sbuf(st, "mp_sq", [128, 6, 512], BF16)
        rstd, rstd_d = P.sbuf(st, "mp_rstd", [128, 512], F32)
        oq = Rot(P, st, "mp_oq", [128, 6, 512], BF16, 2)
        okv = Rot(P, st, "mp_okv", [128, 2, 512], BF16, 2)
        okr = Rot(P, st, "mp_okr", [32, 512], BF16, 2)
        t1, t1_d = P.sbuf(st, "mp_t1", [32, 512], F32)
        t2, t2_d = P.sbuf(st, "mp_t2", [32, 512], F32)
        ps, ps_d = P.psum(st, "mp_ps", [128, 512])
        dq, dkv, dkr = Dep("cqn", True), Dep("ckvn", True), Dep("krr", True)
        for ti in range(T // 512):
            sl = slice(ti * 512, (ti + 1) * 512)
            a, a_d = xq.next()
            b, b_d = xkv.next()
            r, r_d = xr.next()
            P.dma([lambda e: e.dma_start(out=a[:], in_=fm(cqT_d)[:, :, sl])], writes=[a_d])
            P.dma([lambda e: e.dma_start(out=b[:], in_=fm(ckvT_d)[:, :, sl])], writes=[b_d])
            P.dma([lambda e: e.dma_start(out=r[:, 0, :], in_=krT_d[0:32, sl]),
                   lambda e: e.dma_start(out=r[:, 1, :], in_=pkrT_d[0:32, sl]),
                   lambda e: e.dma_start(out=r[:, 2, :], in_=cos_d[0:32, sl]),
                   lambda e: e.dma_start(out=r[:, 3, :], in_=ssin_d[0:32, sl])], writes=[r_d])
            o1, o1_d = oq.next()
            o2, o2_d = okv.next()
            o3, o3_d = okr.next()
            rms_tile(P, C, a, a_d, 6, 512, 768, sq, sq_d, ps, ps_d, rstd, rstd_d, o1, o1_d)
            P.dma([lambda e: e.dma_start(out=fm(cqn_d)[:, :, sl], in_=o1[:])], reads=[o1_d], writes=[dq])
            rms_tile(P, C, b, b_d, 2, 512, 256, sq, sq_d, ps, ps_d, rstd, rstd_d, o2, o2_d)
            P.dma([lambda e: e.dma_start(out=fm(ckvn_d)[:, :, sl], in_=o2[:])], reads=[o2_d], writes=[dkv])
            P.op("vector", lambda e: e.tensor_tensor(out=t1[:], in0=r[:, 0, :], in1=r[:, 2, :], op=ALU.mult), reads=[r_d],
                 writes=[t1_d])
            P.op("gpsimd", lambda e: e.tensor_tensor(out=t2[:], in0=r[:, 1, :], in1=r[:, 3, :], op=ALU.mult), reads=[r_d],
                 writes=[t2_d])
            P.op("vector", lambda e: e.tensor_tensor(out=o3[:], in0=t1[:], in1=t2[:], op=ALU.add), reads=[t1_d, t2_d],
                 writes=[o3_d])
            P.dma([lambda e: e.dma_start(out=krr_d[:, sl], in_=o3[:])], reads=[o3_d], writes=[dkr])
        P.barrier()


MLA_SCALE = 96 ** -0.5


def stage_mla_attention(P, C, cqn_d, ckvn_all_d, krr_all_d, wuq_d, qnorm_d, wukv_d, kvnorm_d, cosq_d, ssinq_d, sel_d, oT_d):
    NKT = S_ALL // 128
    with ExitStack() as st:
        wq, wq_deps = load_weight_bf16(P, st, wuq_d, 768, 2048, scale_d=qnorm_d, name="ma_wq", piece=1024)
        wkv, wkv_deps = load_weight_bf16(P, st, wukv_d, 256, 2048, scale_d=kvnorm_d, name="ma_wkv", piece=1024)
        cqnR = Rot(P, st, "ma_cqn", [128, 6, 512], BF16, 2)
        tabR = Rot(P, st, "ma_tab", [128, 2, 512], F32, 2)
        sel, sel_sd = P.sbuf(st, "ma_sel", [128, 256], F32)
        P.dma([lambda e: e.dma_start(out=sel[:], in_=sel_d[:, :])], writes=[sel_sd])
        KT, KT_d = P.sbuf(st, "ma_KT", [96, S_ALL], BF16)
        P.dma([lambda e: e.dma_start(out=KT[64:96, :], in_=krr_all_d[:, :])], writes=[KT_d])
        Vt, Vt_d = P.sbuf(st, "ma_V", [128, NKT, 128], BF16)
        QT, QT_d = P.sbuf(st, "ma_QT", [96, TLOC], BF16)
        oT, oT_sd = P.sbuf(st, "ma_oT", [128, 8, TLOC], BF16)
        lat = Rot(P, st, "ma_lat", [128, 2, 1024], BF16, 2)
        t1, t1_d = P.sbuf(st, "ma_t1", [96, 512], F32)
        t2, t2_d = P.sbuf(st, "ma_t2", [96, 512], F32)
        pT = Rot(P, st, "ma_pT", [128, 512], BF16, 3)
        osb = Rot(P, st, "ma_osb", [128, 512], F32, 2)
        rcp = Rot(P, st, "ma_rcp", [128, 512], F32, 2)
        ps_s = [P.psum(st, "ma_pss%d" % i, [128, 512]) for i in range(3)]
        ps_o = [P.psum(st, "ma_pso%d" % i, [128, 512]) for i in range(2)]
        ps_g = [P.psum(st, "ma_psg%d" % i, [128, 512]) for i in range(3)]
        gi = [0]
        si = [0]
        for h in range(16):
            c, po = h // 2, (h % 2) * 64
            cb = h * 128
            P.op("gpsimd", lambda e: e.memset(Vt[:], 0.0), writes=[Vt_d])
            onescol = 64 if h % 2 == 0 else 0
            P.op("gpsimd", lambda e: e.memset(Vt[:, :, onescol:onescol + 1], 1.0), writes=[Vt_d])
            for kb in range(S_ALL // 1024):
                lt, lt_d = lat.next()
                P.dma([lambda e: e.dma_start(out=lt[:], in_=fm(ckvn_all_d)[:, :, kb * 1024:(kb + 1) * 1024])], writes=[lt_d])
                for half in range(2):
                    pg, pg_d = ps_g[gi[0] % 3]
                    gi[0] += 1
                    for kc in range(2):
                        P.op("tensor", lambda e, kc=kc: e.matmul(pg[0:64, :], lhsT=wkv[:, kc, cb:cb + 64],
                                                                 rhs=lt[:, kc, half * 512:(half + 1) * 512],
                                                                 start=(kc == 0), stop=(kc == 1)),
                             reads=[wkv_deps[kc], lt_d], writes=[pg_d])
                    k0 = kb * 1024 + half * 512
                    P.op("vector", lambda e: e.tensor_copy(out=KT[0:64, k0:k0 + 512], in_=pg[0:64, :]), reads=[pg_d],
                         writes=[KT_d])
                pg, pg_d = ps_g[gi[0] % 3]
                gi[0] += 1
                for kt in range(8):
                    for kc in range(2):
                        P.op("tensor", lambda e, kc=kc, kt=kt: e.matmul(pg[:, kt * 64:(kt + 1) * 64],
                                                                        lhsT=lt[:, kc, kt * 128:(kt + 1) * 128],
                                                                        rhs=wkv[:, kc, cb + 64:cb + 128],
                                                                        start=(kc == 0), stop=(kc == 1)),
                             reads=[wkv_deps[kc], lt_d], writes=[pg_d])
                P.op("scalar", lambda e: e.copy(out=Vt[:, kb * 8:(kb + 1) * 8, po:po + 64],
                                                in_=pg[:].rearrange("p (t d) -> p t d", d=64)),
                     reads=[pg_d], writes=[Vt_d])
            for qt in range(TLOC // 512):
                qs = slice(qt * 512, (qt + 1) * 512)
                cqn, cqn_sd = cqnR.next()
                tab, tab_d = tabR.next()
                P.dma([lambda e: e.dma_start(out=cqn[:], in_=fm(cqn_d)[:, :, qs])], writes=[cqn_sd])
                P.dma([lambda e: e.dma_start(out=tab[:, 0, :], in_=cosq_d[:, qs]),
                       lambda e: e.dma_start(out=tab[:, 1, :], in_=ssinq_d[:, qs])], writes=[tab_d])
                p1, p1_d = ps_g[gi[0] % 3]
                gi[0] += 1
                for kc in range(6):
                    P.op("tensor", lambda e, kc=kc: e.matmul(p1[0:96, :], lhsT=wq[:, kc, cb:cb + 96], rhs=cqn[:, kc, :],
                                                             start=(kc == 0), stop=(kc == 5)),
                         reads=[wq_deps[kc], cqn_sd], writes=[p1_d])
                p2, p2_d = ps_g[gi[0] % 3]
                gi[0] += 1
                for kc in range(6):
                    P.op("tensor", lambda e, kc=kc: e.matmul(p2[0:96, :], lhsT=wq[:, kc, cb + 32:cb + 128], rhs=cqn[:, kc, :],
                                                             start=(kc == 0), stop=(kc == 5)),
                         reads=[wq_deps[kc], cqn_sd], writes=[p2_d])
                P.op("scalar", lambda e: e.copy(out=QT[0:64, qs], in_=p1[0:64, :]), reads=[], writes=[QT_d, p1_d])
                P.op("vector", lambda e: e.tensor_tensor(out=t1[64:96, :], in0=p1[64:96, :], in1=tab[64:96, 0, :], op=ALU.mult),
                     reads=[p1_d, tab_d], writes=[t1_d])
                P.op("vector", lambda e: e.tensor_tensor(out=t2[64:96, :], in0=p2[64:96, :], in1=tab[64:96, 1, :], op=ALU.mult),
                     reads=[p2_d, tab_d], writes=[t2_d])
                P.op("vector", lambda e: e.tensor_tensor(out=QT[64:96, qs], in0=t1[64:96, :], in1=t2[64:96, :], op=ALU.add),
                     reads=[t1_d, t2_d], writes=[QT_d])
            for qt in range(TLOC // 512):
                qs = slice(qt * 512, (qt + 1) * 512)
                po_, po_d = ps_o[(h * 4 + qt) % 2]
                LOOK = 2
                qk = {}

                def issue_qk(kt2):
                    pb, pb_d = ps_s[si[0] % 3]
                    si[0] += 1
                    P.op("tensor", lambda e: e.matmul(pb[:], lhsT=KT[0:96, kt2 * 128:(kt2 + 1) * 128], rhs=QT[0:96, qs],
                                                      start=True, stop=True), reads=[KT_d, QT_d], writes=[pb_d])
                    qk[kt2] = (pb, pb_d)

                for kt2 in range(LOOK):
                    issue_qk(kt2)
                for kt in range(NKT):
                    if kt + LOOK < NKT:
                        issue_qk(kt + LOOK)
                    pss, pss_d = qk.pop(kt)
                    p_, p_d = pT.next()
                    P.op("scalar", lambda e: e.activation(out=p_[:], in_=pss[:], func=AF.Exp, scale=MLA_SCALE),
                         reads=[pss_d], writes=[p_d])
                    P.op("tensor", lambda e: e.matmul(po_[:], lhsT=Vt[:, kt, :], rhs=p_[:], start=(kt == 0),
                                                      stop=(kt == NKT - 1)), reads=[Vt_d, p_d], writes=[po_d])
                ob, ob_d = osb.next()
                P.op("vector", lambda e: e.tensor_copy(out=ob[:], in_=po_[:]), reads=[po_d], writes=[ob_d])
                pd, pd_d = ps_g[gi[0] % 3]
                gi[0] += 1
                so = (h % 2) * 128
                P.op("tensor", lambda e: e.matmul(pd[:], lhsT=sel[:, so:so + 128], rhs=ob[:], start=True, stop=True),
                     reads=[sel_sd, ob_d], writes=[pd_d])
                rc, rc_d = rcp.next()
                P.op("vector", lambda e: e.reciprocal(out=rc[:], in_=pd[:]), reads=[pd_d], writes=[rc_d])
                P.op("vector", lambda e: e.tensor_tensor(out=oT[po:po + 64, c, qs], in0=ob[po:po + 64, :],
                                                         in1=rc[po:po + 64, :], op=ALU.mult),
                     reads=[ob_d, rc_d], writes=[oT_sd])
        P.dma([lambda e: e.dma_start(out=fm(oT_d)[:, :, :], in_=oT[:])], reads=[oT_sd], writes=[Dep("maoT", True)])
        P.barrier()


RCH = 128
RNCH = S_ALL // RCH
RGRP = 4


def ret_consts_host(h):
    f32 = np.float32
    lgs = np.log1p(-np.exp2(-5.0 - np.arange(8, dtype=f32))).astype(f32)
    lg_f, lg_b = lgs[h], lgs[7 - h]
    i = np.arange(128, dtype=f32)
    m, c = i[:, None], i[None, :]
    dmT_f = np.where(c >= m, np.exp(np.where(c >= m, c - m, 0) * lg_f), 0).astype(f32)
    dmT_b = np.where(m > c, np.exp(np.where(m > c, m - c, 0) * lg_b), 0).astype(f32)
    qdec_f = np.broadcast_to(np.exp((i + 1) * lg_f)[None, :], (128, 128)).astype(f32)
    qdec_b = np.broadcast_to(np.exp((128 - i) * lg_b)[None, :], (128, 128)).astype(f32)
    cols = np.zeros((128, 4), f32)
    cols[:, 0] = np.exp((127 - i) * lg_f)
    cols[:, 1] = np.exp(i * lg_b)
    cols[:, 2] = np.exp(128 * lg_f)
    cols[:, 3] = np.exp(128 * lg_b)
    return np.ascontiguousarray(np.concatenate([np.eye(128, dtype=f32), dmT_f, dmT_b, qdec_f, qdec_b, cols], axis=1))


def stage_retention(P, q_d, k_d, v_d, gate_d, cos_d, sin_d, rc_d, gnw_d, of_d, y_d):
    with ExitStack() as st:
        rc, rc_dd = P.sbuf(st, "rt_rc", [128, 644], F32)
        P.dma([lambda e: e.dma_start(out=rc[:], in_=rc_d[:, :])], writes=[rc_dd])
        gnw, gnw_dd = P.sbuf(st, "rt_gnw", [128, RGRP, 256], F32)
        P.dma([lambda e: e.dma_start(out=gnw[:], in_=gnw_d[:, :, :])], writes=[gnw_dd])
        ident = rc[:, 0:128]
        S, S_d = P.sbuf(st, "rt_S", [128, 256], F32)
        pp = PsPool(P, st, "rt_ps")
        qg = Rot(P, st, "rt_qg", [128, RGRP, 128], F32, 2)
        kg = Rot(P, st, "rt_kg", [128, RGRP, 128], F32, 2)
        vg = Rot(P, st, "rt_vg", [128, RGRP, 256], F32, 2)
        cg = Rot(P, st, "rt_cg", [128, RGRP, 64], F32, 2)
        sg = Rot(P, st, "rt_sg", [128, RGRP, 64], F32, 2)
        qr = Rot(P, st, "rt_qr", [128, RGRP, 128], F32, 2)
        kr = Rot(P, st, "rt_kr", [128, RGRP, 128], F32, 2)
        ta = Rot(P, st, "rt_ta", [128, RGRP, 64], F32, 2)
        tb_ = Rot(P, st, "rt_tb", [128, RGRP, 64], F32, 2)
        og = Rot(P, st, "rt_og", [128, RGRP, 256], F32, 2)
        ofg = Rot(P, st, "rt_ofg", [128, RGRP, 256], F32, 2)
        gg = Rot(P, st, "rt_gg", [128, RGRP, 256], F32, 2)
        sqg, sqg_d = P.sbuf(st, "rt_sqg", [128, RGRP, 256], F32)
        stat = Rot(P, st, "rt_stat", [128, RGRP], F32, 4)
        Qtb = Rot(P, st, "rt_Qt", [128, 128], F32, 3)
        Ktb = Rot(P, st, "rt_Kt", [128, 128], F32, 3)
        ATb = Rot(P, st, "rt_AT", [128, 128], F32, 3)
        Qdb = Rot(P, st, "rt_Qd", [128, 128], F32, 3)
        Kdb = Rot(P, st, "rt_Kd", [128, 128], F32, 3)
        ofd, yd = Dep("rof", True), Dep("ry", True)
        cpi = [0]

        def evac(out_ap, out_d, ps, ps_d):
            if cpi[0] % 2 == 0:
                P.op("scalar", lambda e: e.copy(out=out_ap, in_=ps), reads=[ps_d], writes=[out_d])
            else:
                P.op("vector", lambda e: e.tensor_copy(out=out_ap, in_=ps), reads=[ps_d], writes=[out_d])
            cpi[0] += 1

        def rows(dt, t0, w):
            return dt[t0:t0 + RGRP * 128, :].rearrange("(n p) d -> p n d", p=128)

        for d in range(2):
            dmT = rc[:, 128 + 128 * d:256 + 128 * d]
            qdec = rc[:, 384 + 128 * d:512 + 128 * d]
            kdec = rc[:, 640 + d:641 + d]
            gcol = rc[:, 642 + d:643 + d]
            P.op("vector", lambda e: e.memset(S[:], 0.0), writes=[S_d])
            order = list(range(RNCH)) if d == 0 else list(range(RNCH - 1, -1, -1))
            groups = [order[i:i + RGRP] for i in range(0, RNCH, RGRP)]
            state = {}

            def load_group(gi):
                c0 = min(groups[gi])
                t0 = c0 * 128
                q, q_dd = qg.next()
                k, k_dd = kg.next()
                v, v_dd = vg.next()
                cs, cs_dd = cg.next()
                sn, sn_dd = sg.next()
                P.dma([lambda e: e.dma_start(out=q[:], in_=rows(q_d, t0, 128))], writes=[q_dd])
                P.dma([lambda e: e.dma_start(out=k[:], in_=rows(k_d, t0, 128))], writes=[k_dd])
                P.dma([lambda e: e.dma_start(out=v[:], in_=rows(v_d, t0, 256))], writes=[v_dd])
                P.dma([lambda e: e.dma_start(out=cs[:], in_=rows(cos_d, t0, 64))], writes=[cs_dd])
                P.dma([lambda e: e.dma_start(out=sn[:], in_=rows(sin_d, t0, 64))], writes=[sn_dd])
                r = dict(c0=c0, v=(v, v_dd), o=og.next())
                if d == 1:
                    of_, of_dd = ofg.next()
                    g_, g_dd = gg.next()
                    P.dma([lambda e: e.dma_start(out=of_[:], in_=rows(of_d, t0, 256))], writes=[of_dd])
                    P.dma([lambda e: e.dma_start(out=g_[:], in_=rows(gate_d, t0, 256))], writes=[g_dd])
                    r["of"] = (of_, of_dd)
                    r["g"] = (g_, g_dd)
                outs = []
                for (x, x_dd, rot, scale) in ((q, q_dd, qr, None), (k, k_dd, kr, 128.0 ** -0.5)):
                    y, y_dd = rot.next()
                    a, a_dd = ta.next()
                    b, b_dd = tb_.next()
                    x1, x2 = x[:, :, 0:64], x[:, :, 64:128]
                    P.op("vector", lambda e: e.tensor_tensor(out=a[:], in0=x1, in1=cs[:], op=ALU.mult), reads=[x_dd, cs_dd],
                         writes=[a_dd])
                    P.op("gpsimd", lambda e: e.tensor_tensor(out=b[:], in0=x2, in1=sn[:], op=ALU.mult), reads=[x_dd, sn_dd],
                         writes=[b_dd])
                    P.op("vector", lambda e: e.tensor_tensor(out=y[:, :, 0:64], in0=a[:], in1=b[:], op=ALU.subtract),
                         reads=[a_dd, b_dd], writes=[y_dd])
                    a, a_dd = ta.next()
                    b, b_dd = tb_.next()
                    P.op("gpsimd", lambda e: e.tensor_tensor(out=a[:], in0=x2, in1=cs[:], op=ALU.mult), reads=[x_dd, cs_dd],
                         writes=[a_dd])
                    P.op("vector", lambda e: e.tensor_tensor(out=b[:], in0=x1, in1=sn[:], op=ALU.mult), reads=[x_dd, sn_dd],
                         writes=[b_dd])
                    P.op("gpsimd", lambda e: e.tensor_tensor(out=y[:, :, 64:128], in0=a[:], in1=b[:], op=ALU.add),
                         reads=[a_dd, b_dd], writes=[y_dd])
                    if scale is not None:
                        P.op("gpsimd", lambda e: e.tensor_scalar(out=y[:], in0=y[:], scalar1=scale, scalar2=None, op0=ALU.mult),
                             reads=[y_dd], writes=[y_dd])
                    outs.append((y, y_dd))
                r["q"], r["k"] = outs
                state[gi] = r

            def prep(gi, n):
                r = state[gi]
                j = n - r["c0"]
                q, q_dd = r["q"]
                k, k_dd = r["k"]
                pq, pq_d = pp.next(128, 128)
                P.op("tensor", lambda e: e.transpose(pq, q[:, j, :], ident), reads=[q_dd, rc_dd], writes=[pq_d])
                Qt, Qt_d = Qtb.next()
                evac(Qt[:], Qt_d, pq, pq_d)
                pk, pk_d = pp.next(128, 128)
                P.op("tensor", lambda e: e.transpose(pk, k[:, j, :], ident), reads=[k_dd, rc_dd], writes=[pk_d])
                Kt, Kt_d = Ktb.next()
                evac(Kt[:], Kt_d, pk, pk_d)
                pin, pin_d = pp.next(128, 128)
                P.op("tensor", lambda e: e.matmul(pin, lhsT=Kt[:], rhs=Qt[:], start=True, stop=True), reads=[Kt_d, Qt_d],
                     writes=[pin_d])
                AT, AT_d = ATb.next()
                P.op("vector", lambda e: e.tensor_tensor(out=AT[:], in0=pin, in1=dmT, op=ALU.mult), reads=[pin_d, rc_dd],
                     writes=[AT_d])
                Qd, Qd_d = Qdb.next()
                P.op("gpsimd", lambda e: e.tensor_tensor(out=Qd[:], in0=Qt[:], in1=qdec, op=ALU.mult), reads=[Qt_d, rc_dd],
                     writes=[Qd_d])
                Kd, Kd_d = Kdb.next()
                P.op("gpsimd", lambda e: e.tensor_scalar(out=Kd[:], in0=k[:, j, :], scalar1=kdec, scalar2=None, op0=ALU.mult),
                     reads=[k_dd, rc_dd], writes=[Kd_d])
                return dict(j=j, AT=(AT, AT_d), Qd=(Qd, Qd_d), Kd=(Kd, Kd_d))

            def scan(gi, n, pr):
                r = state[gi]
                j = pr["j"]
                AT, AT_d = pr["AT"]
                Qd, Qd_d = pr["Qd"]
                Kd, Kd_d = pr["Kd"]
                v, v_dd = r["v"]
                o, o_d = r["o"]
                po, po_d = pp.next(128, 256)
                P.op("tensor", lambda e: e.matmul(po, lhsT=AT[:], rhs=v[:, j, :], start=True, stop=False), reads=[AT_d, v_dd],
                     writes=[po_d])
                P.op("tensor", lambda e: e.matmul(po, lhsT=Qd[:], rhs=S[:], start=False, stop=True), reads=[Qd_d, S_d],
                     writes=[po_d])
                ps, ps_d = pp.next(128, 256)
                P.op("tensor", lambda e: e.matmul(ps, lhsT=Kd[:], rhs=v[:, j, :], start=True, stop=True), reads=[Kd_d, v_dd],
                     writes=[ps_d])
                P.op("scalar", lambda e: e.copy(out=o[:, j, :], in_=po), reads=[po_d], writes=[o_d])
                P.op("vector", lambda e: e.scalar_tensor_tensor(out=S[:], in0=S[:], scalar=gcol, in1=ps, op0=ALU.mult,
                                                                op1=ALU.add), reads=[S_d, rc_dd, ps_d], writes=[S_d])

            def finish_group(gi):
                r = state.pop(gi)
                t0 = r["c0"] * 128
                o, o_d = r["o"]
                if d == 0:
                    P.dma([lambda e: e.dma_start(out=rows(of_d, t0, 256), in_=o[:])], reads=[o_d], writes=[ofd])
                    return
                of_, of_dd = r["of"]
                g_, g_dd = r["g"]
                mu, mu_d = stat.next()
                var, var_d = stat.next()
                P.op("gpsimd", lambda e: e.tensor_tensor(out=o[:], in0=o[:], in1=of_[:], op=ALU.add), reads=[o_d, of_dd],
                     writes=[o_d])
                P.op("vector", lambda e: e.tensor_reduce(out=mu[:], in_=o[:], axis=AX.X, op=ALU.add), reads=[o_d], writes=[mu_d])
                P.op("vector", lambda e: e.tensor_scalar(out=mu[:], in0=mu[:], scalar1=1.0 / 256, scalar2=None, op0=ALU.mult),
                     reads=[mu_d], writes=[mu_d])
                for c in range(RGRP):
                    P.op("vector", lambda e, c=c: e.tensor_scalar(out=o[:, c, :], in0=o[:, c, :], scalar1=mu[:, c:c + 1],
                                                                  scalar2=None, op0=ALU.subtract),
                         reads=[o_d, mu_d], writes=[o_d])
                P.op("scalar", lambda e: e.activation(out=sqg[:], in_=o[:], func=AF.Square), reads=[o_d], writes=[sqg_d])
                P.op("vector", lambda e: e.tensor_reduce(out=var[:], in_=sqg[:], axis=AX.X, op=ALU.add), reads=[sqg_d],
                     writes=[var_d])
                P.op("scalar", lambda e: e.activation(out=var[:], in_=var[:], func=AF.Sqrt, scale=1.0 / 256, bias=EPS),
                     reads=[var_d], writes=[var_d])
                P.op("vector", lambda e: e.reciprocal(out=var[:], in_=var[:]), reads=[var_d], writes=[var_d])
                P.op("scalar", lambda e: e.activation(out=g_[:], in_=g_[:], func=AF.Silu), reads=[g_dd], writes=[g_dd])
                P.op("gpsimd", lambda e: e.tensor_tensor(out=g_[:], in0=g_[:], in1=gnw[:], op=ALU.mult), reads=[g_dd, gnw_dd],
                     writes=[g_dd])
                for c in range(RGRP):
                    P.op("vector", lambda e, c=c: e.scalar_tensor_tensor(out=o[:, c, :], in0=o[:, c, :], scalar=var[:, c:c + 1],
                                                                         in1=g_[:, c, :], op0=ALU.mult, op1=ALU.mult),
                         reads=[o_d, var_d, g_dd], writes=[o_d])
                P.dma([lambda e: e.dma_start(out=rows(y_d, t0, 256), in_=o[:])], reads=[o_d], writes=[yd])

            flat = [(gi, n) for gi, ch in enumerate(groups) for n in ch]
            load_group(0)
            pending = None
            for idx, (gi, n) in enumerate(flat):
                pr = prep(gi, n)
                if pending is not None:
                    pgi, pn, ppr = pending
                    scan(pgi, pn, ppr)
                    if pn == groups[pgi][-1]:
                        finish_group(pgi)
                if n == groups[gi][0] and gi + 1 < len(groups):
                    load_group(gi + 1)
                pending = (gi, n, pr)
            pgi, pn, ppr = pending
            scan(pgi, pn, ppr)
            finish_group(pgi)
            P.barrier()
        P.barrier()


def stage_final_norm(P, C, hT_d, w_d, out_d, T):
    with ExitStack() as st:
        w, w_dd = P.sbuf(st, "fn_w", [128, 8], F32)
        P.dma([lambda e: e.dma_start(out=w[:], in_=w_d[:, :])], writes=[w_dd])
        xs = Rot(P, st, "fn_x", [128, 8, 512], F32, 2)
        os_ = Rot(P, st, "fn_o", [128, 8, 512], F32, 2)
        sq, sq_d = P.sbuf(st, "fn_sq", [128, 8, 512], BF16)
        rstd, rstd_d = P.sbuf(st, "fn_rstd", [128, 512], F32)
        ps, ps_d = P.psum(st, "fn_ps", [128, 512])
        od = Dep("fnout", True)
        for ti in range(T // 512):
            sl = slice(ti * 512, (ti + 1) * 512)
            x, x_d = xs.next()
            o, o_d = os_.next()
            P.dma([lambda e: e.dma_start(out=x[:], in_=fm(hT_d)[:, :, sl])], writes=[x_d])
            P.op("scalar", lambda e: e.activation(out=sq[:], in_=x[:], func=AF.Square), reads=[x_d], writes=[sq_d])
            for k in range(8):
                P.op("tensor", lambda e, k=k: e.matmul(ps[:], lhsT=C.ones_bf[:], rhs=sq[:, k, :], start=(k == 0), stop=(k == 7)),
                     reads=[sq_d, C.ones_bf_d], writes=[ps_d])
            P.op("scalar", lambda e: e.activation(out=rstd[:], in_=ps[:], func=AF.Sqrt, scale=1.0 / 1024, bias=EPS),
                 reads=[ps_d], writes=[rstd_d])
            P.op("vector", lambda e: e.reciprocal(out=rstd[:], in_=rstd[:]), reads=[rstd_d], writes=[rstd_d])
            for k in range(8):
                P.op("vector", lambda e, k=k: e.scalar_tensor_tensor(out=o[:, k, :], in0=x[:, k, :], scalar=w[:, k:k + 1],
                                                                     in1=rstd[:], op0=ALU.mult, op1=ALU.mult),
                     reads=[x_d, w_dd, rstd_d], writes=[o_d])
            P.dma([lambda e: e.dma_start(out=fm(out_d)[:, :, sl], in_=o[:])], reads=[o_d], writes=[od])
        P.barrier()


def build_launch_c():
    nc = bass.Bass("TRN2", target_bir_lowering=False)
    ei = lambda n, s, dt=F32: dram(nc, n, s, dt, "ExternalInput")
    eo = lambda n, s, dt=F32: dram(nc, n, s, dt, "ExternalOutput")
    h1T = ei("h1T", [1024, TLOC])
    ogT = ei("ogT", [1024, TLOC])
    g_wo = ei("g_wo", [1024, 1024])
    mlp_norm = ei("mlp_norm", [128, 8])
    w1 = ei("w1", [1024, 4096])
    w2 = ei("w2", [4096, 1024])
    mla_norm = ei("mla_norm", [128, 8])
    mla_w_in = ei("mla_w_in", [1024, 1280])
    cos_kr = ei("cos_kr", [32, TLOC])
    ssin_kr = ei("ssin_kr", [32, TLOC])
    h2T = eo("h2T", [1024, TLOC])
    cqn = eo("cqn", [768, TLOC], BF16)
    ckvn = eo("ckvn", [256, TLOC], BF16)
    krr = eo("krr", [32, TLOC], BF16)
    hmT = dram(nc, "c_hmT", [1024, TLOC], F32)
    aT = dram(nc, "c_aT", [4096, TLOC], BF16)
    cqT = dram(nc, "c_cqT", [768, TLOC], F32)
    ckvT = dram(nc, "c_ckvT", [256, TLOC], F32)
    krT = dram(nc, "c_krT", [128, TLOC], F32)
    pkrT = dram(nc, "c_pkrT", [128, TLOC], F32)
    with ExitStack() as es:
        P = Prog(nc, es)
        C = Consts(P, es)
        dd = lambda n: Dep(n, dram=True)
        stage_linear(P, C, "gwo", ogT, 1024, TLOC, g_wo, 1024,
                     [dict(kind="fm", n0=0, n1=1024, dst=hmT, dst_dep=dd("hmT"), epi="resadd", res=h1T, dtype=F32)])
        stage_mlp(P, C, "mlp1", hmT, TLOC, w1, w2, mlp_norm, aT, h2T, dd("h2T"))
        stage_linear(P, C, "min", h2T, 1024, TLOC, mla_w_in, 1280,
                     [dict(kind="fm", n0=0, n1=768, dst=cqT, dst_dep=dd("cq"), dtype=F32),
                      dict(kind="fm", n0=768, n1=1024, dst=ckvT, dst_dep=dd("ckv"), dtype=F32),
                      dict(kind="fm", n0=1024, n1=1152, dst=krT, dst_dep=dd("kr"), dtype=F32),
                      dict(kind="fm", n0=1152, n1=1280, dst=pkrT, dst_dep=dd("pkr"), dtype=F32)],
                     nscale_d=mla_norm, norm=True)
        stage_mla_prep(P, C, cqT, ckvT, krT, pkrT, cos_kr, ssin_kr, cqn, ckvn, krr, TLOC)
        print("launch C instructions:", P.nins)
    return nc


def build_launch_d():
    nc = bass.Bass("TRN2", target_bir_lowering=False)
    ei = lambda n, s, dt=F32: dram(nc, n, s, dt, "ExternalInput")
    eo = lambda n, s, dt=F32: dram(nc, n, s, dt, "ExternalOutput")
    h2T = ei("h2T", [1024, TLOC])
    cqn = ei("cqn", [768, TLOC], BF16)
    ckvn_all = ei("ckvn_all", [256, S_ALL], BF16)
    krr_all = ei("krr_all", [32, S_ALL], BF16)
    wuq = ei("wuq", [768, 2048])
    qnorm = ei("qnorm", [128, 6])
    wukv = ei("wukv", [256, 2048])
    kvnorm = ei("kvnorm", [128, 2])
    cosq = ei("cosq", [128, TLOC])
    ssinq = ei("ssinq", [128, TLOC])
    sel = ei("sel", [128, 256])
    m_wo = ei("m_wo", [1024, 1024])
    mlp_norm = ei("mlp_norm", [128, 8])
    w1 = ei("w1", [1024, 4096])
    w2 = ei("w2", [4096, 1024])
    ret_norm = ei("ret_norm", [128, 8])
    ret_w_in = ei("ret_w_in", [1024, 6144])
    h3T = eo("h3T", [1024, TLOC])
    r_q = eo("r_q", [TLOC, 1024])
    r_k = eo("r_k", [TLOC, 1024])
    r_v = eo("r_v", [TLOC, 2048])
    r_g = eo("r_g", [TLOC, 2048])
    oT = dram(nc, "d_oT", [1024, TLOC], BF16)
    hmT = dram(nc, "d_hmT", [1024, TLOC], F32)
    aT = dram(nc, "d_aT", [4096, TLOC], BF16)
    with ExitStack() as es:
        P = Prog(nc, es)
        C = Consts(P, es)
        dd = lambda n: Dep(n, dram=True)
        stage_mla_attention(P, C, cqn, ckvn_all, krr_all, wuq, qnorm, wukv, kvnorm, cosq, ssinq, sel, oT)
        stage_linear(P, C, "mwo", oT, 1024, TLOC, m_wo, 1024,
                     [dict(kind="fm", n0=0, n1=1024, dst=hmT, dst_dep=dd("hmT"), epi="resadd", res=h2T, dtype=F32)],
                     src_dtype=BF16)
        stage_mlp(P, C, "mlp2", hmT, TLOC, w1, w2, mlp_norm, aT, h3T, dd("h3T"))
        stage_linear(P, C, "rin", h3T, 1024, TLOC, ret_w_in, 6144,
                     [dict(kind="tm", n0=0, n1=1024, dst=r_q, dst_dep=dd("rq"), dtype=F32),
                      dict(kind="tm", n0=1024, n1=2048, dst=r_k, dst_dep=dd("rk"), dtype=F32),
                      dict(kind="tm", n0=2048, n1=4096, dst=r_v, dst_dep=dd("rv"), dtype=F32),
                      dict(kind="tm", n0=4096, n1=6144, dst=r_g, dst_dep=dd("rg"), dtype=F32)],
                     nscale_d=ret_norm, norm=True, tile_T=256)
        print("launch D instructions:", P.nins)
    return nc


def build_launch_e():
    nc = bass.Bass("TRN2", target_bir_lowering=False)
    ei = lambda n, s, dt=F32: dram(nc, n, s, dt, "ExternalInput")
    q = ei("q_tm", [S_ALL, 128])
    k = ei("k_tm", [S_ALL, 128])
    v = ei("v_tm", [S_ALL, 256])
    g = ei("gate_tm", [S_ALL, 256])
    cos = ei("cos_tm", [S_ALL, 64])
    sin = ei("sin_tm", [S_ALL, 64])
    rc = ei("rconst", [128, 644])
    gnw = ei("gnw", [128, RGRP, 256])
    y = dram(nc, "y_tm", [S_ALL, 256], F32, "ExternalOutput")
    of = dram(nc, "e_of", [S_ALL, 256], F32)
    with ExitStack() as es:
        P = Prog(nc, es)
        stage_retention(P, q, k, v, g, cos, sin, rc, gnw, of, y)
        print("launch E instructions:", P.nins)
    return nc


def build_launch_f():
    nc = bass.Bass("TRN2", target_bir_lowering=False)
    ei = lambda n, s, dt=F32: dram(nc, n, s, dt, "ExternalInput")
    yT = ei("yT", [2048, TLOC])
    h3T = ei("h3T", [1024, TLOC])
    r_wo = ei("r_wo", [2048, 1024])
    mlp_norm = ei("mlp_norm", [128, 8])
    w1 = ei("w1", [1024, 4096])
    w2 = ei("w2", [4096, 1024])
    fnorm = ei("fnorm", [128, 8])
    outT = dram(nc, "outT", [1024, TLOC], F32, "ExternalOutput")
    hmT = dram(nc, "f_hmT", [1024, TLOC], F32)
    h4T = dram(nc, "f_h4T", [1024, TLOC], F32)
    aT = dram(nc, "f_aT", [4096, TLOC], BF16)
    with ExitStack() as es:
        P = Prog(nc, es)
        C = Consts(P, es)
        dd = lambda n: Dep(n, dram=True)
        stage_linear(P, C, "rwo", yT, 2048, TLOC, r_wo, 1024,
                     [dict(kind="fm", n0=0, n1=1024, dst=hmT, dst_dep=dd("hmT"), epi="resadd", res=h3T, dtype=F32)],
                     tile_T=256)
        stage_mlp(P, C, "mlp3", hmT, TLOC, w1, w2, mlp_norm, aT, h4T, dd("h4T"))
        stage_final_norm(P, C, h4T, fnorm, outT, TLOC)
        print("launch F instructions:", P.nins)
    return nc


def _run(nc, in_maps):
    res = run_bass_kernel_spmd(nc, in_maps, core_ids=list(range(NCORES)))
    return res.results


def gdn_in_maps(qkvT_full, z_tm_full, gates_full, inp):
    gcs = gdn_consts_host()
    conv = inp['gdn_conv'][0]
    maps = []
    for hd in range(8):
        qkv_pre = np.stack([qkvT_full[j * 1024 + hd * 128: j * 1024 + (hd + 1) * 128] for j in range(3)], axis=1)
        ztm = np.ascontiguousarray(z_tm_full[:, hd * 128:(hd + 1) * 128])
        gt = gates_full[:, [hd, 8 + hd, 16 + hd, 24 + hd]]
        gt = np.ascontiguousarray(gt.reshape(NCH, 64, 4).transpose(1, 0, 2))
        cw = np.stack([conv[j * 1024 + hd * 128: j * 1024 + (hd + 1) * 128] for j in range(3)], axis=1)
        sc = np.array([inp['gdn_a_log_f'][0][hd], inp['gdn_a_log_b'][0][hd], inp['gdn_dt_bias_f'][0][hd],
                       inp['gdn_dt_bias_b'][0][hd]], np.float32)
        sc = np.ascontiguousarray(np.broadcast_to(sc[None], (64, 4)))
        on = np.ascontiguousarray(np.broadcast_to(inp['gdn_o_norm'][0][None, None], (64, GRP, 128)))
        maps.append(dict(qkv_pre=np.ascontiguousarray(qkv_pre), z_tm=ztm, gates=gt, conv_w=np.ascontiguousarray(cw),
                         scal=sc, onorm=on, gconst=gcs))
    return maps


def rope_tables(pos, half):
    inv = (1.0 / (10000.0 ** (np.arange(half, dtype=np.float32) / half))).astype(np.float32)
    ang = pos.astype(np.float32)[:, None] * inv[None, :]
    return np.cos(ang).astype(np.float32), np.sin(ang).astype(np.float32)


def kernel(**inp):
    inp = {k: np.asarray(v) for k, v in inp.items()}
    f32 = np.float32
    cores = range(NCORES)
    xTs, biases = na_host_prep(inp['x'][0], inp['na_rpb'][0])
    ra = _run(build_launch_a(), [dict(
        xT=xTs[c], na_norm=pc(inp['na_norm'][0]), w_qkv=inp['na_w_qkv'][0], w_o=inp['na_w_o'][0], bias=biases[c],
        mlp_norm=pc(inp['mlp_norm'][0]), w1=inp['mlp_w1'][0], w2=inp['mlp_w2'][0],
        gdn_norm=pc(inp['gdn_norm'][0]), gdn_w_in=inp['gdn_w_in'][0]) for c in cores])
    qkvT = np.concatenate([ra[c]["g_qkvT"] for c in cores], axis=1)
    ztm = np.concatenate([ra[c]["g_ztm"] for c in cores], axis=0)
    gates = np.concatenate([ra[c]["g_gates"] for c in cores], axis=0)
    rb = _run(build_launch_b(), gdn_in_maps(qkvT, ztm, gates, inp))
    og = np.stack([rb[c]["og"] for c in cores], axis=1).reshape(S_ALL, 1024)
    pos = np.arange(S_ALL)
    c16, s16 = rope_tables(pos, 16)
    cos32 = np.concatenate([c16, c16], axis=1).T
    ssin32 = np.concatenate([-s16, s16], axis=1).T
    w_in = inp['mla_w_in'][0]
    kr_w = w_in[:, 1024:1056]
    pkr_w = np.concatenate([kr_w[:, 16:32], kr_w[:, 0:16]], axis=1)
    zpad = np.zeros((1024, 96), f32)
    w_in_ext = np.ascontiguousarray(np.concatenate([w_in[:, :1024], kr_w, zpad, pkr_w, zpad], axis=1))
    rc_ = _run(build_launch_c(), [dict(
        h1T=ra[c]["h1T"], ogT=np.ascontiguousarray(og[c * TLOC:(c + 1) * TLOC].T), g_wo=inp['gdn_w_o'][0],
        mlp_norm=pc(inp['mlp_norm'][1]), w1=inp['mlp_w1'][1], w2=inp['mlp_w2'][1],
        mla_norm=pc(inp['mla_norm'][0]), mla_w_in=w_in_ext,
        cos_kr=np.ascontiguousarray(cos32[:, c * TLOC:(c + 1) * TLOC]),
        ssin_kr=np.ascontiguousarray(ssin32[:, c * TLOC:(c + 1) * TLOC])) for c in cores])
    ckvn_all = np.ascontiguousarray(np.concatenate([rc_[c]["ckvn"] for c in cores], axis=1))
    krr_all = np.ascontiguousarray(np.concatenate([rc_[c]["krr"] for c in cores], axis=1))
    wuq = inp['mla_w_uq'][0].reshape(768, 16, 96)
    wuq_ext = np.ascontiguousarray(np.concatenate(
        [wuq, wuq[:, :, 80:96], wuq[:, :, 64:80]], axis=2).reshape(768, 2048))
    cosq = np.zeros((128, S_ALL), f32)
    ssinq = np.zeros((128, S_ALL), f32)
    cosq[64:96] = cos32
    ssinq[64:96] = ssin32
    sel = np.zeros((128, 256), f32)
    sel[64, 0:128] = 1.0
    sel[0, 128:256] = 1.0
    rd = _run(build_launch_d(), [dict(
        h2T=rc_[c]["h2T"], cqn=rc_[c]["cqn"], ckvn_all=ckvn_all, krr_all=krr_all, wuq=wuq_ext,
        qnorm=pc(inp['mla_q_norm'][0]), wukv=inp['mla_w_ukv'][0], kvnorm=pc(inp['mla_kv_norm'][0]),
        cosq=np.ascontiguousarray(cosq[:, c * TLOC:(c + 1) * TLOC]),
        ssinq=np.ascontiguousarray(ssinq[:, c * TLOC:(c + 1) * TLOC]), sel=sel, m_wo=inp['mla_w_o'][0],
        mlp_norm=pc(inp['mlp_norm'][2]), w1=inp['mlp_w1'][2], w2=inp['mlp_w2'][2],
        ret_norm=pc(inp['ret_norm'][0]), ret_w_in=inp['ret_w_in'][0]) for c in cores])
    r_q = np.concatenate([rd[c]["r_q"] for c in cores], axis=0)
    r_k = np.concatenate([rd[c]["r_k"] for c in cores], axis=0)
    r_v = np.concatenate([rd[c]["r_v"] for c in cores], axis=0)
    r_g = np.concatenate([rd[c]["r_g"] for c in cores], axis=0)
    c64, s64 = rope_tables(pos, 64)
    gn = inp['ret_gn'][0]
    re_ = _run(build_launch_e(), [dict(
        q_tm=np.ascontiguousarray(r_q[:, h * 128:(h + 1) * 128]), k_tm=np.ascontiguousarray(r_k[:, h * 128:(h + 1) * 128]),
        v_tm=np.ascontiguousarray(r_v[:, h * 256:(h + 1) * 256]), gate_tm=np.ascontiguousarray(r_g[:, h * 256:(h + 1) * 256]),
        cos_tm=c64, sin_tm=s64, rconst=ret_consts_host(h),
        gnw=np.ascontiguousarray(np.broadcast_to(gn[h * 256:(h + 1) * 256][None, None], (128, RGRP, 256)))) for h in cores])
    y = np.stack([re_[h]["y_tm"] for h in cores], axis=1).reshape(S_ALL, 2048)
    rf = _run(build_launch_f(), [dict(
        yT=np.ascontiguousarray(y[c * TLOC:(c + 1) * TLOC].T), h3T=rd[c]["h3T"], r_wo=inp['ret_w_o'][0],
        mlp_norm=pc(inp['mlp_norm'][3]), w1=inp['mlp_w1'][3], w2=inp['mlp_w2'][3],
        fnorm=pc(inp['final_norm'])) for c in cores])
    out = np.concatenate([rf[c]["outT"].T for c in cores], axis=0)
    return np.ascontiguousarray(out.reshape(1, S_ALL, 1024).astype(f32))
```

```python
import numpy as np
from contextlib import ExitStack
import concourse.bass as bass
import concourse.mybir as mybir
from concourse.bass_utils import run_bass_kernel_spmd

F32 = mybir.dt.float32
BF16 = mybir.dt.bfloat16
AF = mybir.ActivationFunctionType
ALU = mybir.AluOpType
AX = mybir.AxisListType

NCORES = 8
EPS = 1e-6


class Dep:
    __slots__ = ("name", "w", "r", "dsem", "dram")

    def __init__(self, name="", dram=False):
        self.name = name
        self.w = []
        self.r = []
        self.dsem = None
        self.dram = dram


class Prog:
    ENG = ("tensor", "vector", "scalar", "gpsimd", "sync")

    def __init__(self, nc, es, ndsem=90):
        self.nc = nc
        self.es = es
        self.eng = {"tensor": nc.tensor, "vector": nc.vector, "scalar": nc.scalar,
                    "gpsimd": nc.gpsimd, "sync": nc.sync}
        self.sem = {}
        self.cnt = {}
        self.seen = {e: {} for e in self.ENG}
        for e in self.ENG:
            self.sem[e] = es.enter_context(nc.semaphore("s_" + e))
            self.cnt[e] = 0
        self.free_dsem = []
        for i in range(ndsem):
            k = "d%d" % i
            self.sem[k] = es.enter_context(nc.semaphore(k))
            self.cnt[k] = 0
            self.free_dsem.append(k)
        self.stage_dsem = []
        self.nins = 0

    def sbuf(self, st, name, shape, dtype):
        t = st.enter_context(self.nc.sbuf_tensor(name, list(shape), dtype))
        return t, Dep(name)

    def psum(self, st, name, shape, dtype=F32):
        t = st.enter_context(self.nc.psum_tensor(name, list(shape), dtype))
        return t, Dep(name)

    def _dsem(self, d):
        if d.dsem is None:
            d.dsem = self.free_dsem.pop()
            self.stage_dsem.append((d, d.dsem))
        return d.dsem

    def _collect(self, eng, reads, writes, same_engine_sync=True):
        need = {}
        for d in reads:
            for (k, v) in d.w:
                if need.get(k, 0) < v:
                    need[k] = v
        for d in writes:
            if d.dram:
                continue
            for (k, v) in d.w:
                if need.get(k, 0) < v:
                    need[k] = v
            for (k, v) in d.r:
                if need.get(k, 0) < v:
                    need[k] = v
        seen = self.seen[eng]
        e = self.eng[eng]
        for k, v in need.items():
            if k == eng and not same_engine_sync:
                continue
            if seen.get(k, 0) >= v:
                continue
            seen[k] = v
            e.wait_ge(self.sem[k], v)
            self.nins += 1

    def _finish(self, comp, reads, writes):
        for d in writes:
            if d.dram:
                d.w = [c for c in d.w if c[0] != comp[0]] + [comp]
                continue
            d.w = [comp]
            d.r = []
        for d in reads:
            if d in writes:
                continue
            d.r = [c for c in d.r if c[0] != comp[0]] + [comp]

    def op(self, eng, fn, reads=(), writes=()):
        self._collect(eng, reads, writes, same_engine_sync=(eng != "tensor"))
        self.cnt[eng] += 1
        comp = (eng, self.cnt[eng])
        ins = fn(self.eng[eng])
        ins.then_inc(self.sem[eng], 1)
        self.nins += 1
        self._finish(comp, reads, writes)
        return comp

    def dma(self, fns, reads=(), writes=(), semdep=None, queue="sync"):
        if semdep is None:
            cand = [d for d in list(writes) + list(reads) if not d.dram]
            semdep = cand[0]
        key = self._dsem(semdep)
        self._collect(queue, reads, writes, same_engine_sync=False)
        e = self.eng[queue]
        for f in fns:
            f(e).then_inc(self.sem[key], 16)
            self.nins += 1
        self.cnt[key] += 16 * len(fns)
        comp = (key, self.cnt[key])
        self._finish(comp, reads, writes)
        return comp

    def barrier(self):
        for en in self.ENG:
            e = self.eng[en]
            seen = self.seen[en]
            for k, v in self.cnt.items():
                if k == en or v == 0:
                    continue
                if seen.get(k, 0) >= v:
                    continue
                seen[k] = v
                e.wait_ge(self.sem[k], v)
                self.nins += 1
        for (d, k) in self.stage_dsem:
            d.dsem = None
            self.free_dsem.append(k)
        self.stage_dsem = []


def dram(nc, name, shape, dtype, kind="Internal"):
    return nc.dram_tensor(name, list(shape), dtype, kind=kind).ap()


def fm(ap, p=128):
    return ap.rearrange("(c p) t -> p c t", p=p)


def load_weight_bf16(P, st, W_d, K, N, scale_d=None, name="w", piece=2048):
    KC = K // 128
    wb, wb_d = P.sbuf(st, name + "_b", [128, KC, N], BF16)
    deps = [Dep(name + "_b%d" % k) for k in range(KC)]
    sc = None
    if scale_d is not None:
        sc, sc_d = P.sbuf(st, name + "_sc", [128, KC], F32)
        P.dma([lambda e: e.dma_start(out=sc[:], in_=scale_d[:, :])], writes=[sc_d])
    stg = [P.sbuf(st, name + "_stg%d" % i, [128, piece], F32) for i in range(2)]
    it = 0
    for k in range(KC):
        for n0 in range(0, N, piece):
            n1 = min(N, n0 + piece)
            s, s_d = stg[it % 2]
            P.dma([lambda e, s=s, k=k, n0=n0, n1=n1: e.dma_start(out=s[:, 0:n1 - n0], in_=W_d[k * 128:(k + 1) * 128, n0:n1])],
                  writes=[s_d])
            eng = "vector" if it % 2 == 0 else "gpsimd"
            if sc is not None:
                P.op(eng, lambda e, s=s, k=k, n0=n0, n1=n1: e.tensor_scalar(
                    out=wb[:, k, n0:n1], in0=s[:, 0:n1 - n0], scalar1=sc[:, k:k + 1], scalar2=None, op0=ALU.mult),
                    reads=[s_d, sc_d], writes=[deps[k]])
            else:
                P.op(eng, lambda e, s=s, k=k, n0=n0, n1=n1: e.tensor_copy(out=wb[:, k, n0:n1], in_=s[:, 0:n1 - n0]),
                     reads=[s_d], writes=[deps[k]])
            it += 1
    return wb, deps


class Consts:
    def __init__(self, P, st):
        self.ones_bf, self.ones_bf_d = P.sbuf(st, "c_ones_bf", [128, 128], BF16)
        P.op("vector", lambda e: e.memset(self.ones_bf[:], 1.0), writes=[self.ones_bf_d])
        self.ones_f, self.ones_f_d = P.sbuf(st, "c_ones_f", [128, 128], F32)
        P.op("vector", lambda e: e.memset(self.ones_f[:], 1.0), writes=[self.ones_f_d])


def rms_tile(P, C, xt, xt_d, KC, T, D, sq, sq_d, ps, ps_d, rstd, rstd_d, xn, xn_d, eps=EPS):
    P.op("scalar", lambda e: e.activation(out=sq[:, 0:KC, 0:T], in_=xt[:, 0:KC, 0:T], func=AF.Square),
         reads=[xt_d], writes=[sq_d])
    for k in range(KC):
        P.op("tensor", lambda e, k=k: e.matmul(ps[:, 0:T], lhsT=C.ones_bf[:], rhs=sq[:, k, 0:T],
                                                  start=(k == 0), stop=(k == KC - 1)),
             reads=[sq_d, C.ones_bf_d], writes=[ps_d])
    P.op("scalar", lambda e: e.activation(out=rstd[:, 0:T], in_=ps[:, 0:T], func=AF.Sqrt, scale=1.0 / D, bias=eps),
         reads=[ps_d], writes=[rstd_d])
    P.op("vector", lambda e: e.reciprocal(out=rstd[:, 0:T], in_=rstd[:, 0:T]), reads=[rstd_d], writes=[rstd_d])
    for k in range(KC):
        eng = "vector" if k % 2 == 0 else "gpsimd"
        P.op(eng, lambda e, k=k: e.tensor_tensor(out=xn[:, k, 0:T], in0=xt[:, k, 0:T], in1=rstd[:, 0:T], op=ALU.mult),
             reads=[xt_d, rstd_d], writes=[xn_d])


def stage_linear(P, C, name, src_d, K, T, W_d, N, sinks, nscale_d=None, norm=False, src_dtype=F32,
                 tile_T=512, D_norm=None):
    nc = P.nc
    KC = K // 128
    with ExitStack() as st:
        wb, wdeps = load_weight_bf16(P, st, W_d, K, N, scale_d=nscale_d, name=name + "_w")
        xts = [P.sbuf(st, name + "_xt%d" % i, [128, KC, tile_T], src_dtype) for i in range(2)]
        need_cast = norm or (src_dtype != BF16)
        if need_cast:
            xns = [P.sbuf(st, name + "_xn%d" % i, [128, KC, tile_T], BF16) for i in range(2)]
        if norm:
            sq, sq_d = P.sbuf(st, name + "_sq", [128, KC, tile_T], BF16)
            rstd, rstd_d = P.sbuf(st, name + "_rstd", [128, tile_T], F32)
            ps_ss, ps_ss_d = P.psum(st, name + "_psss", [128, 512])
        pss = [P.psum(st, name + "_ps%d" % i, [128, 512]) for i in range(4)]
        psi = [0]
        outs = []
        for si, s in enumerate(sinks):
            ncs = (s["n1"] - s["n0"]) // 128
            if s["kind"] == "fm":
                o = [P.sbuf(st, name + "_o%d_%d" % (si, i), [128, ncs, tile_T], s["dtype"]) for i in range(2)]
                r = None
                if s.get("epi") == "resadd":
                    r = [P.sbuf(st, name + "_r%d_%d" % (si, i), [128, ncs, tile_T], F32) for i in range(2)]
                tmp = None
                if s.get("epi") == "relu2":
                    tmp = [P.sbuf(st, name + "_t%d_%d" % (si, i), [128, tile_T], F32) for i in range(2)]
                outs.append((o, r, tmp))
            elif s.get("aug"):
                o = [P.sbuf(st, name + "_o%d_%d" % (si, i), [128, (s["n1"] - s["n0"]) // 64, 65], s["dtype"])
                     for i in range(2)]
                for (oo, oo_d) in o:
                    P.op("vector", lambda e, oo=oo: e.memset(oo[:], 1.0), writes=[oo_d])
                outs.append((o, None, None))
            else:
                o = [P.sbuf(st, name + "_o%d_%d" % (si, i), [128, s["n1"] - s["n0"]], s["dtype"]) for i in range(2)]
                outs.append((o, None, None))
        ntiles = T // tile_T
        srcv = fm(src_d)

        def load(ti):
            xt, xt_d = xts[ti % 2]
            P.dma([lambda e: e.dma_start(out=xt[:], in_=srcv[:, :, ti * tile_T:(ti + 1) * tile_T])], writes=[xt_d])
            for si, s in enumerate(sinks):
                if s.get("epi") == "resadd":
                    r, r_d = outs[si][1][ti % 2]
                    P.dma([lambda e, r=r, s=s: e.dma_start(
                        out=r[:], in_=fm(s["res"])[:, :, ti * tile_T:(ti + 1) * tile_T])], writes=[r_d])

        load(0)
        cp = 0
        tmcount = 0
        for ti in range(ntiles):
            if ti + 1 < ntiles:
                load(ti + 1)
            xt, xt_d = xts[ti % 2]
            if norm:
                xn, xn_d = xns[ti % 2]
                rms_tile(P, C, xt, xt_d, KC, tile_T, D_norm or K, sq, sq_d, ps_ss, ps_ss_d, rstd, rstd_d, xn, xn_d)
            elif need_cast:
                xn, xn_d = xns[ti % 2]
                for k in range(KC):
                    eng = ("vector", "gpsimd")[k % 2]
                    P.op(eng, lambda e, k=k: e.tensor_copy(out=xn[:, k, :], in_=xt[:, k, :]), reads=[xt_d], writes=[xn_d])
            else:
                xn, xn_d = xt, xt_d
            for si, s in enumerate(sinks):
                n0, n1 = s["n0"], s["n1"]
                if s["kind"] == "fm":
                    (ob, rb, tb) = outs[si]
                    o, o_d = ob[ti % 2]
                    ncs = (n1 - n0) // 128
                    for c in range(ncs):
                        ps, ps_d = pss[psi[0] % 4]
                        psi[0] += 1
                        for k in range(KC):
                            P.op("tensor", lambda e, k=k, c=c, ps=ps: e.matmul(
                                ps[:, 0:tile_T], lhsT=wb[:, k, n0 + c * 128:n0 + (c + 1) * 128], rhs=xn[:, k, :],
                                start=(k == 0), stop=(k == KC - 1)), reads=[wdeps[k], xn_d], writes=[ps_d])
                        epi = s.get("epi", "copy")
                        if epi == "copy":
                            if cp % 2 == 0:
                                P.op("scalar", lambda e, c=c, ps=ps: e.copy(out=o[:, c, :], in_=ps[:, 0:tile_T]),
                                     reads=[ps_d], writes=[o_d])
                            else:
                                P.op("vector", lambda e, c=c, ps=ps: e.tensor_copy(out=o[:, c, :], in_=ps[:, 0:tile_T]),
                                     reads=[ps_d], writes=[o_d])
                            cp += 1
                        elif epi == "relu2":
                            t, t_d = tb[c % 2]
                            P.op("scalar", lambda e, ps=ps, t=t: e.activation(out=t[:], in_=ps[:, 0:tile_T], func=AF.Relu),
                                 reads=[ps_d], writes=[t_d])
                            eng = ("vector", "gpsimd")[c % 2]
                            P.op(eng, lambda e, c=c, t=t: e.tensor_tensor(out=o[:, c, :], in0=t[:], in1=t[:], op=ALU.mult),
                                 reads=[t_d], writes=[o_d])
                        elif epi == "resadd":
                            r, r_d = rb[ti % 2]
                            P.op("vector", lambda e, c=c, ps=ps, r=r: e.tensor_tensor(
                                out=o[:, c, :], in0=ps[:, 0:tile_T], in1=r[:, c, :], op=ALU.add),
                                reads=[ps_d, r_d], writes=[o_d])
                    P.dma([lambda e, o=o: e.dma_start(out=fm(s["dst"])[:, :, ti * tile_T:(ti + 1) * tile_T], in_=o[:])],
                          reads=[o_d], writes=[s["dst_dep"]])
                else:
                    (ob, _, _) = outs[si]
                    for tk in range(tile_T // 128):
                        o, o_d = ob[tmcount % 2]
                        tmcount += 1
                        for nb in range(n0, n1, 512):
                            ne = min(n1, nb + 512)
                            ps, ps_d = pss[psi[0] % 4]
                            psi[0] += 1
                            for k in range(KC):
                                P.op("tensor", lambda e, k=k, ps=ps, nb=nb, ne=ne, tk=tk: e.matmul(
                                    ps[:, 0:ne - nb], lhsT=xn[:, k, tk * 128:(tk + 1) * 128], rhs=wb[:, k, nb:ne],
                                    start=(k == 0), stop=(k == KC - 1)), reads=[wdeps[k], xn_d], writes=[ps_d])
                            if s.get("aug"):
                                oview = o[:, (nb - n0) // 64:(ne - n0) // 64, 0:64]
                                pview = ps[:, 0:ne - nb].rearrange("p (h d) -> p h d", d=64)
                            else:
                                oview = o[:, nb - n0:ne - n0]
                                pview = ps[:, 0:ne - nb]
                            if cp % 2 == 0:
                                P.op("scalar", lambda e: e.copy(out=oview, in_=pview), reads=[ps_d], writes=[o_d])
                            else:
                                P.op("vector", lambda e: e.tensor_copy(out=oview, in_=pview), reads=[ps_d], writes=[o_d])
                            cp += 1
                        r0 = ti * tile_T + tk * 128
                        oflat = o[:].rearrange("p h d -> p (h d)") if s.get("aug") else o[:]
                        P.dma([lambda e: e.dma_start(out=s["dst"][r0:r0 + 128, :], in_=oflat)],
                              reads=[o_d], writes=[s["dst_dep"]])
        P.barrier()


TLOC = 2048
HALO = 256
TEXT = TLOC + 2 * HALO
NA_H = 16
NA_DH = 64


def na_variant(i):
    return {0: 0, 1: 1, 14: 3, 15: 4}.get(i, 2)


def stage_na_attention(P, C, QT_d, KT_d, V_d, bias_d, oT_d, oT_dep):
    with ExitStack() as st:
        ident, ident_d = P.sbuf(st, "na_ident", [128, 128], BF16)
        idf, idf_d = P.sbuf(st, "na_idf", [128, 128], F32)
        P.op("gpsimd", lambda e: e.memset(idf[:], 1.0), writes=[idf_d])
        P.op("gpsimd", lambda e: e.affine_select(out=idf[:], in_=idf[:], pattern=[[-1, 128]], compare_op=ALU.is_equal,
                                                  fill=0.0, base=0, channel_multiplier=1), reads=[idf_d], writes=[idf_d])
        P.op("vector", lambda e: e.tensor_copy(out=ident[:], in_=idf[:]), reads=[idf_d], writes=[ident_d])
        kts = [P.sbuf(st, "na_kt%d" % i, [128, 8, 640], BF16) for i in range(2)]
        qt, qt_d = P.sbuf(st, "na_qt", [128, 8, TLOC], BF16)
        vts = [P.sbuf(st, "na_vt%d" % i, [128, 5, 16, 65], BF16) for i in range(2)]
        bts = [P.sbuf(st, "na_bt%d" % i, [128, 5, 128], F32) for i in range(3)]
        sbs = [P.sbuf(st, "na_sb%d" % i, [128, 5, 128], F32) for i in range(2)]
        pts = [P.sbuf(st, "na_pt%d" % i, [128, 5, 128], BF16) for i in range(2)]
        rcs = [P.sbuf(st, "na_rc%d" % i, [128, 1], F32) for i in range(2)]
        otm = [P.sbuf(st, "na_otm%d" % i, [128, 16, 64], BF16) for i in range(2)]
        oT, oT_sd = P.sbuf(st, "na_oT", [128, 8, TLOC], BF16)
        ps_s = [P.psum(st, "na_pss%d" % i, [128, 8, 128]) for i in range(2)]
        ps_o = [P.psum(st, "na_pso%d" % i, [128, 512]) for i in range(2)]
        ps_t = [P.psum(st, "na_pst%d" % i, [128, 8, 128], BF16) for i in range(1)]
        KTv, QTv = fm(KT_d), fm(QT_d)
        Vv = V_d.rearrange("(t p) f -> p t f", p=128)
        nblk = TLOC // 128
        P.dma([lambda e: e.dma_start(out=qt[:], in_=QTv[:, :, HALO:HALO + TLOC])], writes=[qt_d])

        def load_blk(i):
            kt, kt_d = kts[i % 2]
            vt, vt_d = vts[i % 2]
            P.dma([lambda e: e.dma_start(out=kt[:], in_=KTv[:, :, 128 * i:128 * i + 640])], writes=[kt_d])
            P.dma([lambda e: e.dma_start(out=vt[:].rearrange("p t h d -> p t (h d)"), in_=Vv[:, i:i + 5, :])],
                  writes=[vt_d])

        def load_bias(it):
            i, h = divmod(it, NA_H)
            bt, bt_d = bts[it % 3]
            P.dma([lambda e: e.dma_start(out=bt[:], in_=bias_d[na_variant(i), h])], writes=[bt_d])

        load_blk(0)
        load_bias(0)
        load_bias(1)
        it = 0
        for i in range(nblk):
            if i + 1 < nblk:
                load_blk(i + 1)
            kt, kt_d = kts[i % 2]
            vt, vt_d = vts[i % 2]
            ot, ot_d = otm[i % 2]
            for h in range(NA_H):
                if it + 2 < nblk * NA_H:
                    load_bias(it + 2)
                bt, bt_d = bts[it % 3]
                pss, pss_d = ps_s[it % 2]
                pso, pso_d = ps_o[it % 2]
                sb, sb_d = sbs[it % 2]
                pt, pt_d = pts[it % 2]
                rc, rc_d = rcs[it % 2]
                c, po = h // 2, (h % 2) * 64
                for t in range(5):
                    P.op("tensor", lambda e, t=t: e.matmul(pss[:, t, :], lhsT=kt[po:po + 64, c, t * 128:(t + 1) * 128],
                                                           rhs=qt[po:po + 64, c, 128 * i:128 * i + 128], start=True, stop=True),
                         reads=[kt_d, qt_d], writes=[pss_d])
                P.op("vector", lambda e: e.scalar_tensor_tensor(out=sb[:], in0=pss[:, 0:5, :], scalar=NA_DH ** -0.5,
                                                                in1=bt[:], op0=ALU.mult, op1=ALU.add),
                     reads=[pss_d, bt_d], writes=[sb_d])
                P.op("scalar", lambda e: e.activation(out=pt[:], in_=sb[:], func=AF.Exp), reads=[sb_d], writes=[pt_d])
                for t in range(5):
                    P.op("tensor", lambda e, t=t: e.matmul(pso[:, 0:65], lhsT=pt[:, t, :], rhs=vt[:, t, h, :],
                                                           start=(t == 0), stop=(t == 4)),
                         reads=[pt_d, vt_d], writes=[pso_d])
                P.op("vector", lambda e: e.reciprocal(out=rc[:], in_=pso[:, 64:65]), reads=[pso_d], writes=[rc_d])
                P.op("vector", lambda e: e.tensor_scalar(out=ot[:, h, :], in0=pso[:, 0:64], scalar1=rc[:, 0:1],
                                                         scalar2=None, op0=ALU.mult),
                     reads=[pso_d, rc_d], writes=[ot_d])
                it += 1
            pst, pst_d = ps_t[0]
            for cc in range(8):
                P.op("tensor", lambda e, cc=cc: e.transpose(pst[:, cc, :], ot[:, 2 * cc:2 * cc + 2, :], ident[:]),
                     reads=[ot_d, ident_d], writes=[pst_d])
            P.op("scalar", lambda e: e.copy(out=oT[:, :, 128 * i:128 * i + 128], in_=pst[:]), reads=[pst_d],
                 writes=[oT_sd])
        P.dma([lambda e: e.dma_start(out=fm(oT_d)[:, :, :], in_=oT[:])], reads=[oT_sd], writes=[oT_dep])
        P.barrier()


def stage_mlp(P, C, name, hT_d, T, w1_d, w2_d, nscale_d, aT_d, out_d, out_dep, tile_T=512):
    aT_dep = Dep(name + "_aT", dram=True)
    stage_linear(P, C, name + "a", hT_d, 1024, T, w1_d, 4096,
                 [dict(kind="fm", n0=0, n1=4096, dst=aT_d, dst_dep=aT_dep, epi="relu2", dtype=BF16)],
                 nscale_d=nscale_d, norm=True, tile_T=tile_T)
    stage_linear(P, C, name + "b", aT_d, 4096, T, w2_d, 1024,
                 [dict(kind="fm", n0=0, n1=1024, dst=out_d, dst_dep=out_dep, epi="resadd", res=hT_d, dtype=F32)],
                 src_dtype=BF16, tile_T=256)


def build_launch_a():
    nc = bass.Bass("TRN2", target_bir_lowering=False)
    xT = dram(nc, "xT", [1024, TEXT], F32, "ExternalInput")
    na_norm = dram(nc, "na_norm", [128, 8], F32, "ExternalInput")
    w_qkv = dram(nc, "w_qkv", [1024, 3072], F32, "ExternalInput")
    w_o = dram(nc, "w_o", [1024, 1024], F32, "ExternalInput")
    bias = dram(nc, "bias", [5, 16, 128, 5, 128], F32, "ExternalInput")
    mlp_norm = dram(nc, "mlp_norm", [128, 8], F32, "ExternalInput")
    w1 = dram(nc, "w1", [1024, 4096], F32, "ExternalInput")
    w2 = dram(nc, "w2", [4096, 1024], F32, "ExternalInput")
    h1T = dram(nc, "h1T", [1024, TLOC], F32, "ExternalOutput")
    gdn_norm = dram(nc, "gdn_norm", [128, 8], F32, "ExternalInput")
    gdn_w_in = dram(nc, "gdn_w_in", [1024, 4128], F32, "ExternalInput")
    g_qkvT = dram(nc, "g_qkvT", [3072, TLOC], F32, "ExternalOutput")
    g_ztm = dram(nc, "g_ztm", [TLOC, 1024], F32, "ExternalOutput")
    g_gates = dram(nc, "g_gates", [TLOC, 32], F32, "ExternalOutput")
    QT = dram(nc, "QT", [1024, TEXT], BF16)
    KT = dram(nc, "KT", [1024, TEXT], BF16)
    V = dram(nc, "V", [TEXT, 16 * 65], BF16)
    oT = dram(nc, "oT", [1024, TLOC], BF16)
    hmT = dram(nc, "hmT", [1024, TLOC], F32)
    aT = dram(nc, "aT", [4096, TLOC], BF16)
    with ExitStack() as es:
        P = Prog(nc, es)
        C = Consts(P, es)
        dd = lambda n: Dep(n, dram=True)
        stage_linear(P, C, "qkv", xT, 1024, TEXT, w_qkv, 3072,
                     [dict(kind="fm", n0=0, n1=1024, dst=QT, dst_dep=dd("QT"), dtype=BF16),
                      dict(kind="fm", n0=1024, n1=2048, dst=KT, dst_dep=dd("KT"), dtype=BF16),
                      dict(kind="tm", n0=2048, n1=3072, dst=V, dst_dep=dd("V"), dtype=BF16, aug=True)],
                     nscale_d=na_norm, norm=True)
        stage_na_attention(P, C, QT, KT, V, bias, oT, dd("oT"))
        stage_linear(P, C, "wo", oT, 1024, TLOC, w_o, 1024,
                     [dict(kind="fm", n0=0, n1=1024, dst=hmT, dst_dep=dd("hmT"), epi="resadd",
                           res=xT[:, HALO:HALO + TLOC], dtype=F32)], src_dtype=BF16)
        stage_mlp(P, C, "mlp0", hmT, TLOC, w1, w2, mlp_norm, aT, h1T, dd("h1T"))
        stage_linear(P, C, "gin", h1T, 1024, TLOC, gdn_w_in, 4128,
                     [dict(kind="fm", n0=0, n1=1024, dst=g_qkvT[0:1024, :], dst_dep=dd("gq"), dtype=F32),
                      dict(kind="fm", n0=1024, n1=2048, dst=g_qkvT[1024:2048, :], dst_dep=dd("gk"), dtype=F32),
                      dict(kind="fm", n0=2048, n1=3072, dst=g_qkvT[2048:3072, :], dst_dep=dd("gv"), dtype=F32),
                      dict(kind="tm", n0=3072, n1=4096, dst=g_ztm, dst_dep=dd("gz"), dtype=F32),
                      dict(kind="tm", n0=4096, n1=4128, dst=g_gates, dst_dep=dd("gg"), dtype=F32)],
                     nscale_d=gdn_norm, norm=True, tile_T=256)
        print("launch A instructions:", P.nins)
    return nc


def na_host_prep(x, na_rpb):
    xg = x.reshape(256, 64, 1024)
    rpb = na_rpb
    xTs, biases = [], []
    for c in range(NCORES):
        R0 = 32 * c
        slot_row = np.arange(40) + R0 - 4
        if c == 0:
            slot_row[0:4] = [6, 7, 6, 7]
        if c == NCORES - 1:
            slot_row[36:40] = [248, 249, 248, 249]
        xe = xg[slot_row].reshape(TEXT, 1024)
        xTs.append(np.ascontiguousarray(xe.T))
        tab = np.full((5, 16, 640, 128), -30000.0, np.float32)
        for vi, i in enumerate([0, 1, 2, 14, 15]):
            slots = np.arange(2 * i, 2 * i + 10)
            krow = slot_row[slots]
            first = np.array([list(krow).index(r) == j for j, r in enumerate(krow)])
            kr = np.repeat(krow, 64)
            kvalid = np.repeat(first, 64)
            kc = np.tile(np.arange(64), 10)
            qr = np.repeat(R0 + 2 * i + np.arange(2), 64)
            qc = np.tile(np.arange(64), 2)
            qr0 = np.clip(qr - 4, 0, 248)
            qc0 = np.clip(qc - 8, 0, 48)
            ok = (kr[:, None] >= qr0[None, :]) & (kr[:, None] < qr0[None, :] + 8) & kvalid[:, None] \
                & (kc[:, None] >= qc0[None, :]) & (kc[:, None] < qc0[None, :] + 16)
            drow = np.clip(kr[:, None] - qr[None, :] + 7, 0, 14)
            dcol = np.clip(kc[:, None] - qc[None, :] + 15, 0, 30)
            g = rpb[:, drow, dcol]
            tab[vi] = np.where(ok[None], g, np.float32(-30000.0))
        biases.append(np.ascontiguousarray(tab.reshape(5, 16, 5, 128, 128).transpose(0, 1, 3, 2, 4)))
    return xTs, biases


def pc(v):
    return np.ascontiguousarray(np.asarray(v).reshape(-1, 128).T)


S_ALL = 16384
GCH = 64
NCH = S_ALL // GCH
GRP = 8


def gdn_consts_host():
    j = np.arange(64)
    NEG = -30000.0
    I128 = np.eye(128, dtype=np.float32)

    def pad(m):
        out = np.zeros((128, 64), np.float32)
        out[:64] = m
        return out
    tri_f = (j[:, None] <= j[None, :])
    tri_b = (j[:, None] >= j[None, :])
    us_f = (j[:, None] > j[None, :])
    us_b = (j[:, None] < j[None, :])
    negs_f = np.where(j[:, None] > j[None, :], 0, NEG)
    negiT_f = np.where(j[None, :] >= j[:, None], 0, NEG)
    negs_b = np.where(j[:, None] < j[None, :], 0, NEG)
    negiT_b = np.where(j[None, :] <= j[:, None], 0, NEG)
    blocks = [tri_f, tri_b, us_f, us_b, negs_f, negiT_f, negs_b, negiT_b]
    return np.ascontiguousarray(np.concatenate([I128] + [pad(np.asarray(b, np.float32)) for b in blocks], axis=1))


class Rot:
    def __init__(self, P, st, name, shape, dtype, n):
        self.bufs = [P.sbuf(st, "%s%d" % (name, i), shape, dtype) for i in range(n)]
        self.i = 0

    def next(self):
        b = self.bufs[self.i % len(self.bufs)]
        self.i += 1
        return b


class PsPool:
    def __init__(self, P, st, name, nbanks=8):
        self.tiles = []
        for b in range(nbanks):
            t, d = P.psum(st, "%s%d" % (name, b), [128, 512])
            self.tiles.append((t, d))
        self.i = 0

    def next(self, p, f):
        t, d = self.tiles[self.i % len(self.tiles)]
        self.i += 1
        return t[0:p, 0:f], d


def gdn_stage_prep(P, qkv_pre, conv_w, gconst_sb, qT_d, kT_d, ktm_d, vtm_d):
    gc, gc_d = gconst_sb
    with ExitStack() as st:
        cw, cw_d = P.sbuf(st, "gp_cw", [128, 3, 5], F32)
        P.dma([lambda e: e.dma_start(out=cw[:], in_=conv_w[:, :, :])], writes=[cw_d])
        onesf, onesf_d = P.sbuf(st, "gp_ones", [128, 128], F32)
        P.op("vector", lambda e: e.memset(onesf[:], 1.0), writes=[onesf_d])
        xin = Rot(P, st, "gp_x", [128, 3, 516], F32, 2)
        acc = Rot(P, st, "gp_acc", [128, 512], F32, 2)
        yb = Rot(P, st, "gp_y", [128, 512], F32, 3)
        sqb = Rot(P, st, "gp_sq", [128, 512], F32, 2)
        rsb = Rot(P, st, "gp_rs", [128, 512], F32, 2)
        ynb = Rot(P, st, "gp_yn", [128, 512], F32, 4)
        tmb = Rot(P, st, "gp_tm", [128, 4, 128], F32, 4)
        pss = [P.psum(st, "gp_ps%d" % i, [128, 512]) for i in range(2)]
        pst = [P.psum(st, "gp_pt%d" % i, [128, 512]) for i in range(4)]
        nt = S_ALL // 512
        qd, kd, ktd, vtd = Dep("qT", True), Dep("kT", True), Dep("ktm", True), Dep("vtm", True)
        ki = 0
        for ti in range(nt):
            t0 = ti * 512
            x, x_d = xin.next()
            lo, hi = max(t0 - 2, 0), min(t0 + 514, S_ALL)
            if ti == 0:
                P.op("gpsimd", lambda e: e.memset(x[:, :, 0:2], 0.0), writes=[x_d])
            if ti == nt - 1:
                P.op("gpsimd", lambda e: e.memset(x[:, :, 514:516], 0.0), writes=[x_d])
            c0 = lo - (t0 - 2)
            P.dma([lambda e: e.dma_start(out=x[:, :, c0:c0 + (hi - lo)], in_=qkv_pre[:, :, lo:hi])], writes=[x_d])
            for j in range(3):
                a, a_d = acc.next()
                P.op("vector", lambda e: e.tensor_scalar(out=a[:], in0=x[:, j, 0:512], scalar1=cw[:, j, 0:1], scalar2=None,
                                                         op0=ALU.mult), reads=[x_d, cw_d], writes=[a_d])
                for tap in range(1, 5):
                    P.op("vector", lambda e, tap=tap: e.scalar_tensor_tensor(
                        out=a[:], in0=x[:, j, tap:tap + 512], scalar=cw[:, j, tap:tap + 1], in1=a[:],
                        op0=ALU.mult, op1=ALU.add), reads=[x_d, cw_d, a_d], writes=[a_d])
                y, y_d = yb.next()
                P.op("scalar", lambda e: e.activation(out=y[:], in_=a[:], func=AF.Silu), reads=[a_d], writes=[y_d])
                if j < 2:
                    sq, sq_d = sqb.next()
                    P.op("scalar", lambda e: e.activation(out=sq[:], in_=y[:], func=AF.Square), reads=[y_d], writes=[sq_d])
                    ps, ps_d = pss[ki % 2]
                    ki += 1
                    P.op("tensor", lambda e: e.matmul(ps[:], lhsT=onesf[:], rhs=sq[:], start=True, stop=True),
                         reads=[onesf_d, sq_d], writes=[ps_d])
                    rs, rs_d = rsb.next()
                    sc = 128.0 if j == 0 else 1.0
                    P.op("scalar", lambda e: e.activation(out=rs[:], in_=ps[:], func=AF.Sqrt, scale=sc, bias=sc * EPS),
                         reads=[ps_d], writes=[rs_d])
                    P.op("vector", lambda e: e.reciprocal(out=rs[:], in_=rs[:]), reads=[rs_d], writes=[rs_d])
                    yn, yn_d = ynb.next()
                    P.op("gpsimd", lambda e: e.tensor_tensor(out=yn[:], in0=y[:], in1=rs[:], op=ALU.mult),
                         reads=[y_d, rs_d], writes=[yn_d])
                    dst, dstd = (qT_d, qd) if j == 0 else (kT_d, kd)
                    P.dma([lambda e: e.dma_start(out=dst[:, t0:t0 + 512], in_=yn[:])], reads=[yn_d], writes=[dstd])
                else:
                    yn, yn_d = y, y_d
                if j >= 1:
                    tm, tm_d = tmb.next()
                    for b in range(4):
                        pt, pt_d = pst[(ti * 8 + j * 4 + b) % 4]
                        P.op("tensor", lambda e, b=b: e.transpose(pt[:, 0:128], yn[:, b * 128:(b + 1) * 128], gc[:, 0:128]),
                             reads=[yn_d, gc_d], writes=[pt_d])
                        if b % 2 == 0:
                            P.op("scalar", lambda e, b=b: e.copy(out=tm[:, b, :], in_=pt[:, 0:128]), reads=[pt_d], writes=[tm_d])
                        else:
                            P.op("vector", lambda e, b=b: e.tensor_copy(out=tm[:, b, :], in_=pt[:, 0:128]), reads=[pt_d],
                                 writes=[tm_d])
                    dst, dstd = (ktm_d, ktd) if j == 1 else (vtm_d, vtd)
                    P.dma([lambda e: e.dma_start(out=dst[t0:t0 + 512, :].rearrange("(b p) d -> p b d", p=128), in_=tm[:])],
                          reads=[tm_d], writes=[dstd])
        P.barrier()


def gdn_stage_gates(P, st, gates_d, scal_d, gconst_sb):
    gc, gc_d = gconst_sb
    out = {}
    pers = []
    for d in range(2):
        pers.append(dict(
            g=P.sbuf(st, "gg_g%d" % d, [64, NCH], F32), beta=P.sbuf(st, "gg_beta%d" % d, [64, NCH], F32),
            eG=P.sbuf(st, "gg_eG%d" % d, [64, NCH], F32), beG=P.sbuf(st, "gg_beG%d" % d, [64, NCH], F32),
            kdec=P.sbuf(st, "gg_kdec%d" % d, [64, NCH], F32), gl=P.sbuf(st, "gg_gl%d" % d, [128, NCH], F32)))
    with ExitStack() as tmp:
        gt, gt_d = P.sbuf(tmp, "gg_gt", [64, NCH, 4], F32)
        sc, sc_d = P.sbuf(tmp, "gg_sc", [64, 4], F32)
        P.dma([lambda e: e.dma_start(out=gt[:], in_=gates_d[:, :, :])], writes=[gt_d])
        P.dma([lambda e: e.dma_start(out=sc[:], in_=scal_d[:, :])], writes=[sc_d])
        onesf, onesf_d = P.sbuf(tmp, "gg_ones", [64, 128], F32)
        P.op("vector", lambda e: e.memset(onesf[:], 1.0), writes=[onesf_d])
        t1, t1_d = P.sbuf(tmp, "gg_t1", [64, NCH], F32)
        t2, t2_d = P.sbuf(tmp, "gg_t2", [64, NCH], F32)
        t3, t3_d = P.sbuf(tmp, "gg_t3", [64, NCH], F32)
        Gs, Gs_d = P.sbuf(tmp, "gg_G", [64, NCH], F32)
        Gt, Gt_d = P.sbuf(tmp, "gg_Gt", [64, NCH], F32)
        ac, ac_d = P.sbuf(tmp, "gg_ac", [64, 1], F32)
        psG, psG_d = P.psum(tmp, "gg_psG", [128, 512])
        psT, psT_d = P.psum(tmp, "gg_psT", [128, 512])
        for d in range(2):
            (g, g_d), (beta, beta_d), (eG, eG_d) = pers[d]["g"], pers[d]["beta"], pers[d]["eG"]
            (beG, beG_d), (kdec, kdec_d), (gl, gl_d) = pers[d]["beG"], pers[d]["kdec"], pers[d]["gl"]
            P.op("vector", lambda e: e.tensor_scalar(out=t1[:], in0=gt[:, :, 2 + d], scalar1=sc[:, 2 + d:3 + d], scalar2=None,
                                                     op0=ALU.add), reads=[gt_d, sc_d], writes=[t1_d])
            P.op("scalar", lambda e: e.activation(out=t2[:], in_=t1[:], func=AF.Abs), reads=[t1_d], writes=[t2_d])
            P.op("scalar", lambda e: e.activation(out=t2[:], in_=t2[:], func=AF.Exp, scale=-1.0), reads=[t2_d], writes=[t2_d])
            P.op("scalar", lambda e: e.activation(out=t2[:], in_=t2[:], func=AF.Ln, bias=1.0), reads=[t2_d], writes=[t2_d])
            P.op("vector", lambda e: e.tensor_scalar(out=t3[:], in0=t1[:], scalar1=0.0, scalar2=None, op0=ALU.max),
                 reads=[t1_d], writes=[t3_d])
            P.op("vector", lambda e: e.tensor_tensor(out=t3[:], in0=t3[:], in1=t2[:], op=ALU.add), reads=[t3_d, t2_d],
                 writes=[t3_d])
            P.op("scalar", lambda e: e.activation(out=ac[:], in_=sc[:, d:d + 1], func=AF.Exp), reads=[sc_d], writes=[ac_d])
            P.op("vector", lambda e: e.tensor_scalar(out=g[:], in0=t3[:], scalar1=ac[:, 0:1], scalar2=-1.0, op0=ALU.mult,
                                                     op1=ALU.mult), reads=[t3_d, ac_d], writes=[g_d])
            P.op("scalar", lambda e: e.activation(out=beta[:], in_=gt[:, :, d], func=AF.Sigmoid), reads=[gt_d], writes=[beta_d])
            tri = gc[0:64, 128 + 64 * d:128 + 64 * d + 64]
            P.op("tensor", lambda e: e.matmul(psG[0:64, 0:NCH], lhsT=tri, rhs=g[:], start=True, stop=True),
                 reads=[gc_d, g_d], writes=[psG_d])
            P.op("tensor", lambda e: e.matmul(psT[:, 0:NCH], lhsT=onesf[:], rhs=g[:], start=True, stop=True),
                 reads=[onesf_d, g_d], writes=[psT_d])
            P.op("vector", lambda e: e.tensor_copy(out=Gs[:], in_=psG[0:64, 0:NCH]), reads=[psG_d], writes=[Gs_d])
            P.op("scalar", lambda e: e.activation(out=gl[:], in_=psT[:, 0:NCH], func=AF.Exp), reads=[psT_d], writes=[gl_d])
            P.op("vector", lambda e: e.tensor_tensor(out=Gt[:], in0=psT[0:64, 0:NCH], in1=Gs[:], op=ALU.subtract),
                 reads=[Gs_d], writes=[Gt_d, psT_d])
            P.op("scalar", lambda e: e.activation(out=kdec[:], in_=Gt[:], func=AF.Exp), reads=[Gt_d], writes=[kdec_d])
            P.op("scalar", lambda e: e.activation(out=eG[:], in_=Gs[:], func=AF.Exp), reads=[Gs_d], writes=[eG_d])
            P.op("vector", lambda e: e.tensor_tensor(out=beG[:], in0=beta[:], in1=eG[:], op=ALU.mult), reads=[beta_d, eG_d],
                 writes=[beG_d])
            out[d] = dict(g=(g, g_d), beta=(beta, beta_d), eG=(eG, eG_d), beG=(beG, beG_d), kdec=(kdec, kdec_d), gl=(gl, gl_d))
        P.barrier()
    return out


def gdn_stage_scan(P, G, gconst_sb, qT_d, kT_d, ktm_d, vtm_d, of_d, z_d, onorm_d, og_d, dbg=None):
    gc, gc_d = gconst_sb
    I64 = gc[0:64, 0:64]
    with ExitStack() as st:
        pp = PsPool(P, st, "gs_ps")
        S, S_d = P.sbuf(st, "gs_S", [128, 128], F32)
        onr, onr_d = P.sbuf(st, "gs_onr", [64, GRP, 128], F32)
        P.dma([lambda e: e.dma_start(out=onr[:], in_=onorm_d[:, :, :])], writes=[onr_d])
        ktg = Rot(P, st, "gs_ktg", [128, GRP * 64], F32, 2)
        qtg = Rot(P, st, "gs_qtg", [128, GRP * 64], F32, 2)
        kg = Rot(P, st, "gs_kg", [64, GRP, 128], F32, 2)
        vg = Rot(P, st, "gs_vg", [64, GRP, 128], F32, 2)
        og = Rot(P, st, "gs_og", [64, GRP, 128], F32, 2)
        ofg = Rot(P, st, "gs_ofg", [64, GRP, 128], F32, 2)
        zg = Rot(P, st, "gs_zg", [64, GRP, 128], F32, 2)
        sqg = Rot(P, st, "gs_sqg", [64, GRP, 128], F32, 1)
        ssg = Rot(P, st, "gs_ssg", [64, GRP], F32, 2)
        gtri = Rot(P, st, "gs_gtri", [64, 64], F32, 4)
        Dm = Rot(P, st, "gs_Dm", [64, 64], F32, 4)
        DTm = Rot(P, st, "gs_DTm", [64, 64], F32, 4)
        Lb = Rot(P, st, "gs_L", [64, 64], F32, 4)
        Nb = Rot(P, st, "gs_N", [64, 64], F32, 4)
        Ztb = Rot(P, st, "gs_Zt", [64, 64], F32, 4)
        Pb = Rot(P, st, "gs_P", [64, 64], F32, 8)
        Ptb = Rot(P, st, "gs_Pt", [64, 64], F32, 8)
        rwb = Rot(P, st, "gs_rw", [64, 128], F32, 4)
        vbb = Rot(P, st, "gs_vb", [64, 128], F32, 4)
        ATb = Rot(P, st, "gs_AT", [64, 64], F32, 6)
        kdb = Rot(P, st, "gs_kd", [64, 128], F32, 6)
        wTb = Rot(P, st, "gs_wT", [128, 64], F32, 6)
        ub = Rot(P, st, "gs_u", [64, 128], F32, 6)
        vnb = Rot(P, st, "gs_vn", [64, 128], F32, 2)
        tb = Rot(P, st, "gs_t", [64, 128], F32, 2)
        ofd, ogd = Dep("of", True), Dep("og", True)
        cpi = [0]

        def evac(out_ap, out_d, ps, ps_d):
            if cpi[0] % 2 == 0:
                P.op("scalar", lambda e: e.copy(out=out_ap, in_=ps), reads=[ps_d], writes=[out_d])
            else:
                P.op("vector", lambda e: e.tensor_copy(out=out_ap, in_=ps), reads=[ps_d], writes=[out_d])
            cpi[0] += 1

        for d in range(2 if dbg is None else 1):
            gd = G[d]
            (g, g_d), (beta, beta_d), (eG, eG_d) = gd["g"], gd["beta"], gd["eG"]
            (beG, beG_d), (kdec, kdec_d), (gl, gl_d) = gd["beG"], gd["kdec"], gd["gl"]
            tri = gc[0:64, 128 + 64 * d:192 + 64 * d]
            us = gc[0:64, 256 + 64 * d:320 + 64 * d]
            negs = gc[0:64, 384 + 128 * d:448 + 128 * d]
            negiT = gc[0:64, 448 + 128 * d:512 + 128 * d]
            P.op("vector", lambda e: e.memset(S[:], 0.0), writes=[S_d])
            order = list(range(NCH)) if d == 0 else list(range(NCH - 1, -1, -1))
            groups = [order[i:i + GRP] for i in range(0, NCH, GRP)]
            if dbg is not None:
                groups = groups[:dbg]
            state = {}

            def load_group(gi):
                chunks = groups[gi]
                c0 = min(chunks)
                t0 = c0 * 64
                kt, kt_d = ktg.next()
                qt, qt_d = qtg.next()
                k, k_d = kg.next()
                v, v_d = vg.next()
                P.dma([lambda e: e.dma_start(out=kt[:], in_=kT_d[:, t0:t0 + GRP * 64])], writes=[kt_d])
                P.dma([lambda e: e.dma_start(out=qt[:], in_=qT_d[:, t0:t0 + GRP * 64])], writes=[qt_d])
                P.dma([lambda e: e.dma_start(out=k[:], in_=ktm_d[t0:t0 + GRP * 64, :].rearrange("(n p) d -> p n d", p=64))],
                      writes=[k_d])
                P.dma([lambda e: e.dma_start(out=v[:], in_=vtm_d[t0:t0 + GRP * 64, :].rearrange("(n p) d -> p n d", p=64))],
                      writes=[v_d])
                r = dict(c0=c0, kt=(kt, kt_d), qt=(qt, qt_d), k=(k, k_d), v=(v, v_d), o=og.next())
                if d == 1:
                    of_, of_dd = ofg.next()
                    z_, z_dd = zg.next()
                    P.dma([lambda e: e.dma_start(out=of_[:], in_=of_d[t0:t0 + GRP * 64, :].rearrange("(n p) d -> p n d", p=64))],
                          writes=[of_dd])
                    P.dma([lambda e: e.dma_start(out=z_[:], in_=z_d[t0:t0 + GRP * 64, :].rearrange("(n p) d -> p n d", p=64))],
                          writes=[z_dd])
                    r["of"] = (of_, of_dd)
                    r["z"] = (z_, z_dd)
                state[gi] = r

            def prep(gi, ns):
                r = state[gi]
                kt, kt_d = r["kt"]
                qt, qt_d = r["qt"]
                k, k_d = r["k"]
                v, v_d = r["v"]
                cs = []
                for n in ns:
                    ln = n - r["c0"]
                    cs.append(dict(n=n, ln=ln, ktn=kt[:, ln * 64:(ln + 1) * 64], qtn=qt[:, ln * 64:(ln + 1) * 64]))
                for c in cs:
                    n = c["n"]
                    gt_, gt_dd = gtri.next()
                    P.op("gpsimd", lambda e: e.tensor_scalar(out=gt_[:], in0=tri, scalar1=g[:, n:n + 1], scalar2=None,
                                                             op0=ALU.mult), reads=[gc_d, g_d], writes=[gt_dd])
                    c["gt"] = (gt_, gt_dd)
                for c in cs:
                    gt_, gt_dd = c["gt"]
                    ps1, ps1_d = pp.next(64, 64)
                    P.op("tensor", lambda e: e.matmul(ps1, lhsT=gt_[:], rhs=us, start=True, stop=False), reads=[gt_dd, gc_d],
                         writes=[ps1_d])
                    P.op("tensor", lambda e: e.matmul(ps1, lhsT=I64, rhs=negs, start=False, stop=True), reads=[gc_d],
                         writes=[ps1_d])
                    ps2, ps2_d = pp.next(64, 64)
                    P.op("tensor", lambda e: e.matmul(ps2, lhsT=us, rhs=gt_[:], start=True, stop=False), reads=[gt_dd, gc_d],
                         writes=[ps2_d])
                    P.op("tensor", lambda e: e.matmul(ps2, lhsT=I64, rhs=negiT, start=False, stop=True), reads=[gc_d],
                         writes=[ps2_d])
                    c["ps1"], c["ps2"] = (ps1, ps1_d), (ps2, ps2_d)
                for c in cs:
                    ps1, ps1_d = c["ps1"]
                    ps2, ps2_d = c["ps2"]
                    dm, dm_d = Dm.next()
                    dtm, dtm_d = DTm.next()
                    P.op("scalar", lambda e: e.activation(out=dm[:], in_=ps1, func=AF.Exp), reads=[ps1_d], writes=[dm_d])
                    P.op("scalar", lambda e: e.activation(out=dtm[:], in_=ps2, func=AF.Exp), reads=[ps2_d], writes=[dtm_d])
                    c["dm"], c["dtm"] = (dm, dm_d), (dtm, dtm_d)
                for c in cs:
                    ktn, qtn = c["ktn"], c["qtn"]
                    pkk, pkk_d = pp.next(64, 64)
                    P.op("tensor", lambda e: e.matmul(pkk, lhsT=ktn, rhs=ktn, start=True, stop=True), reads=[kt_d],
                         writes=[pkk_d])
                    pkq, pkq_d = pp.next(64, 64)
                    P.op("tensor", lambda e: e.matmul(pkq, lhsT=ktn, rhs=qtn, start=True, stop=True), reads=[kt_d, qt_d],
                         writes=[pkq_d])
                    c["pkk"], c["pkq"] = (pkk, pkk_d), (pkq, pkq_d)
                for c in cs:
                    n = c["n"]
                    pkk, pkk_d = c["pkk"]
                    pkq, pkq_d = c["pkq"]
                    dm, dm_d = c["dm"]
                    dtm, dtm_d = c["dtm"]
                    L, L_d = Lb.next()
                    P.op("vector", lambda e: e.scalar_tensor_tensor(out=L[:], in0=pkk, scalar=beta[:, n:n + 1], in1=dm[:],
                                                                    op0=ALU.mult, op1=ALU.mult),
                         reads=[pkk_d, beta_d, dm_d], writes=[L_d])
                    AT, AT_d = ATb.next()
                    P.op("vector", lambda e: e.tensor_tensor(out=AT[:], in0=pkq, in1=dtm[:], op=ALU.mult),
                         reads=[pkq_d, dtm_d], writes=[AT_d])
                    c["L"], c["AT"] = (L, L_d), (AT, AT_d)
                for c in cs:
                    L, L_d = c["L"]
                    pn, pn_d = pp.next(64, 64)
                    P.op("tensor", lambda e: e.transpose(pn, L[:], I64), reads=[L_d, gc_d], writes=[pn_d])
                    c["pn"] = (pn, pn_d)
                for c in cs:
                    pn, pn_d = c["pn"]
                    N, N_d = Nb.next()
                    evac(N[:], N_d, pn, pn_d)
                    c["N"] = (N, N_d)
                for c in cs:
                    N, N_d = c["N"]
                    Zt, Zt_d = Ztb.next()
                    P.op("gpsimd", lambda e: e.tensor_tensor(out=Zt[:], in0=I64, in1=N[:], op=ALU.subtract), reads=[gc_d, N_d],
                         writes=[Zt_d])
                    c["Zt"] = (Zt, Zt_d)
                    c["P"], c["Pt"] = c["L"], c["N"]
                for lvl in range(1, 6):
                    for c in cs:
                        Pc, Pc_d = c["P"]
                        Ptc, Ptc_d = c["Pt"]
                        pP, pP_d = pp.next(64, 64)
                        P.op("tensor", lambda e: e.matmul(pP, lhsT=Ptc[:], rhs=Pc[:], start=True, stop=True),
                             reads=[Ptc_d, Pc_d], writes=[pP_d])
                        c["pP"] = (pP, pP_d)
                        if lvl < 5:
                            pPt, pPt_d = pp.next(64, 64)
                            P.op("tensor", lambda e: e.matmul(pPt, lhsT=Pc[:], rhs=Ptc[:], start=True, stop=True),
                                 reads=[Ptc_d, Pc_d], writes=[pPt_d])
                            c["pPt"] = (pPt, pPt_d)
                    for c in cs:
                        pP, pP_d = c["pP"]
                        Pn, Pn_d = Pb.next()
                        evac(Pn[:], Pn_d, pP, pP_d)
                        c["P"] = (Pn, Pn_d)
                        if lvl < 5:
                            pPt, pPt_d = c["pPt"]
                            Ptn, Ptn_d = Ptb.next()
                            evac(Ptn[:], Ptn_d, pPt, pPt_d)
                            c["Pt"] = (Ptn, Ptn_d)
                    for c in cs:
                        Pn, Pn_d = c["P"]
                        Zt, Zt_d = c["Zt"]
                        pZ, pZ_d = pp.next(64, 64)
                        P.op("tensor", lambda e: e.matmul(pZ, lhsT=Pn[:], rhs=Zt[:], start=True, stop=True),
                             reads=[Pn_d, Zt_d], writes=[pZ_d])
                        c["pZ"] = (pZ, pZ_d)
                    for c in cs:
                        pZ, pZ_d = c["pZ"]
                        Zt, Zt_d = c["Zt"]
                        P.op("vector", lambda e: e.tensor_tensor(out=Zt[:], in0=pZ, in1=Zt[:], op=ALU.add),
                             reads=[pZ_d, Zt_d], writes=[Zt_d])
                for c in cs:
                    n, ln = c["n"], c["ln"]
                    rw, rw_d = rwb.next()
                    vb, vb_d = vbb.next()
                    kd_, kd_d = kdb.next()
                    P.op("gpsimd", lambda e: e.tensor_scalar(out=rw[:], in0=k[:, ln, :], scalar1=beG[:, n:n + 1], scalar2=None,
                                                             op0=ALU.mult), reads=[k_d, beG_d], writes=[rw_d])
                    P.op("gpsimd", lambda e: e.tensor_scalar(out=vb[:], in0=v[:, ln, :], scalar1=beta[:, n:n + 1], scalar2=None,
                                                             op0=ALU.mult), reads=[v_d, beta_d], writes=[vb_d])
                    P.op("gpsimd", lambda e: e.tensor_scalar(out=kd_[:], in0=k[:, ln, :], scalar1=kdec[:, n:n + 1],
                                                             scalar2=None, op0=ALU.mult), reads=[k_d, kdec_d], writes=[kd_d])
                    c["rw"], c["vb"], c["kd"] = (rw, rw_d), (vb, vb_d), (kd_, kd_d)
                for c in cs:
                    rw, rw_d = c["rw"]
                    vb, vb_d = c["vb"]
                    Zt, Zt_d = c["Zt"]
                    pw, pw_d = pp.next(128, 64)
                    P.op("tensor", lambda e: e.matmul(pw, lhsT=rw[:], rhs=Zt[:], start=True, stop=True), reads=[rw_d, Zt_d],
                         writes=[pw_d])
                    pu, pu_d = pp.next(64, 128)
                    P.op("tensor", lambda e: e.matmul(pu, lhsT=Zt[:], rhs=vb[:], start=True, stop=True), reads=[Zt_d, vb_d],
                         writes=[pu_d])
                    c["pw"], c["pu"] = (pw, pw_d), (pu, pu_d)
                outs = []
                for c in cs:
                    pw, pw_d = c["pw"]
                    pu, pu_d = c["pu"]
                    wT, wT_d = wTb.next()
                    evac(wT[:], wT_d, pw, pw_d)
                    u, u_d = ub.next()
                    evac(u[:], u_d, pu, pu_d)
                    outs.append(dict(wT=(wT, wT_d), u=(u, u_d), AT=c["AT"], kd=c["kd"], qtn=c["qtn"], qt_d=qt_d, ln=c["ln"]))
                return outs

            def scan(gi, n, pr):
                r = state[gi]
                ln = pr["ln"]
                wT, wT_d = pr["wT"]
                u, u_d = pr["u"]
                AT, AT_d = pr["AT"]
                kd_, kd_d = pr["kd"]
                o, o_d = r["o"]
                pws, pws_d = pp.next(64, 128)
                P.op("tensor", lambda e: e.matmul(pws, lhsT=wT[:], rhs=S[:], start=True, stop=True), reads=[wT_d, S_d],
                     writes=[pws_d])
                vn, vn_d = vnb.next()
                P.op("vector", lambda e: e.tensor_tensor(out=vn[:], in0=u[:], in1=pws, op=ALU.subtract), reads=[u_d, pws_d],
                     writes=[vn_d])
                pqs, pqs_d = pp.next(64, 128)
                P.op("tensor", lambda e: e.matmul(pqs, lhsT=pr["qtn"], rhs=S[:], start=True, stop=True),
                     reads=[pr["qt_d"], S_d], writes=[pqs_d])
                pkv, pkv_d = pp.next(128, 128)
                P.op("tensor", lambda e: e.matmul(pkv, lhsT=kd_[:], rhs=vn[:], start=True, stop=True), reads=[kd_d, vn_d],
                     writes=[pkv_d])
                pav, pav_d = pp.next(64, 128)
                P.op("tensor", lambda e: e.matmul(pav, lhsT=AT[:], rhs=vn[:], start=True, stop=True), reads=[AT_d, vn_d],
                     writes=[pav_d])
                P.op("vector", lambda e: e.scalar_tensor_tensor(out=S[:], in0=S[:], scalar=gl[:, n:n + 1], in1=pkv,
                                                                op0=ALU.mult, op1=ALU.add),
                     reads=[S_d, gl_d, pkv_d], writes=[S_d])
                t_, t_d = tb.next()
                P.op("scalar", lambda e: e.copy(out=t_[:], in_=pav), reads=[pav_d], writes=[t_d])
                P.op("vector", lambda e: e.scalar_tensor_tensor(out=o[:, ln, :], in0=pqs, scalar=eG[:, n:n + 1], in1=t_[:],
                                                                op0=ALU.mult, op1=ALU.add),
                     reads=[pqs_d, eG_d, t_d], writes=[o_d])

            def finish_group(gi):
                r = state.pop(gi)
                t0 = r["c0"] * 64
                o, o_d = r["o"]
                if d == 0:
                    P.dma([lambda e: e.dma_start(out=of_d[t0:t0 + GRP * 64, :].rearrange("(n p) d -> p n d", p=64), in_=o[:])],
                          reads=[o_d], writes=[ofd])
                    return
                of_, of_dd = r["of"]
                z_, z_dd = r["z"]
                sq, sq_d = sqg.next()
                ss, ss_d = ssg.next()
                P.op("gpsimd", lambda e: e.tensor_tensor(out=o[:], in0=o[:], in1=of_[:], op=ALU.add), reads=[o_d, of_dd],
                     writes=[o_d])
                P.op("scalar", lambda e: e.activation(out=sq[:], in_=o[:], func=AF.Square), reads=[o_d], writes=[sq_d])
                P.op("vector", lambda e: e.tensor_reduce(out=ss[:], in_=sq[:], axis=AX.X, op=ALU.add), reads=[sq_d],
                     writes=[ss_d])
                P.op("scalar", lambda e: e.activation(out=ss[:], in_=ss[:], func=AF.Sqrt, scale=1.0 / 128, bias=EPS),
                     reads=[ss_d], writes=[ss_d])
                P.op("vector", lambda e: e.reciprocal(out=ss[:], in_=ss[:]), reads=[ss_d], writes=[ss_d])
                P.op("scalar", lambda e: e.activation(out=z_[:], in_=z_[:], func=AF.Silu), reads=[z_dd], writes=[z_dd])
                P.op("gpsimd", lambda e: e.tensor_tensor(out=z_[:], in0=z_[:], in1=onr[:], op=ALU.mult), reads=[z_dd, onr_d],
                     writes=[z_dd])
                for c in range(GRP):
                    P.op("vector", lambda e, c=c: e.scalar_tensor_tensor(out=o[:, c, :], in0=o[:, c, :], scalar=ss[:, c:c + 1],
                                                                         in1=z_[:, c, :], op0=ALU.mult, op1=ALU.mult),
                         reads=[o_d, ss_d, z_dd], writes=[o_d])
                P.dma([lambda e: e.dma_start(out=og_d[t0:t0 + GRP * 64, :].rearrange("(n p) d -> p n d", p=64), in_=o[:])],
                      reads=[o_d], writes=[ogd])

            pairs = [(gi, ch[j:j + 2]) for gi, ch in enumerate(groups) for j in range(0, len(ch), 2)]
            load_group(0)
            pending = None
            for (gi, ns) in pairs:
                prs = prep(gi, ns)
                if pending is not None:
                    pgi, pns, pprs = pending
                    for pn, ppr in zip(pns, pprs):
                        scan(pgi, pn, ppr)
                    if pns[-1] == groups[pgi][-1]:
                        finish_group(pgi)
                if ns[0] == groups[gi][0] and gi + 1 < len(groups):
                    load_group(gi + 1)
                pending = (gi, ns, prs)
            pgi, pns, pprs = pending
            for pn, ppr in zip(pns, pprs):
                scan(pgi, pn, ppr)
            finish_group(pgi)
            P.barrier()
        P.barrier()


def build_launch_b(dbg=None):
    nc = bass.Bass("TRN2", target_bir_lowering=False)
    ik = "Internal" if dbg is None else "ExternalOutput"
    qkv_pre = dram(nc, "qkv_pre", [128, 3, S_ALL], F32, "ExternalInput")
    z_tm = dram(nc, "z_tm", [S_ALL, 128], F32, "ExternalInput")
    gates = dram(nc, "gates", [64, NCH, 4], F32, "ExternalInput")
    conv_w = dram(nc, "conv_w", [128, 3, 5], F32, "ExternalInput")
    scal = dram(nc, "scal", [64, 4], F32, "ExternalInput")
    onorm = dram(nc, "onorm", [64, GRP, 128], F32, "ExternalInput")
    gconst = dram(nc, "gconst", [128, 640], F32, "ExternalInput")
    og = dram(nc, "og", [S_ALL, 128], F32, "ExternalOutput")
    qT = dram(nc, "g_qT", [128, S_ALL], F32, ik)
    kT = dram(nc, "g_kT", [128, S_ALL], F32, ik)
    ktm = dram(nc, "g_ktm", [S_ALL, 128], F32, ik)
    vtm = dram(nc, "g_vtm", [S_ALL, 128], F32, ik)
    of = dram(nc, "g_of", [S_ALL, 128], F32, ik)
    with ExitStack() as es:
        P = Prog(nc, es)
        gc = P.sbuf(es, "gconst_sb", [128, 640], F32)
        P.dma([lambda e: e.dma_start(out=gc[0][:], in_=gconst[:, :])], writes=[gc[1]])
        gdn_stage_prep(P, qkv_pre, conv_w, gc, qT, kT, ktm, vtm)
        G = gdn_stage_gates(P, es, gates, scal, gc)
        gdn_stage_scan(P, G, gc, qT, kT, ktm, vtm, of, z_tm, onorm, og, dbg=dbg)
        print("launch B instructions:", P.nins)
    return nc


def stage_mla_prep(P, C, cqT_d, ckvT_d, krT_d, pkrT_d, cos_d, ssin_d, cqn_d, ckvn_d, krr_d, T):
    with ExitStack() as st:
        xq = Rot(P, st, "mp_xq", [128, 6, 512], F32, 2)
        xkv = Rot(P, st, "mp_xkv", [128, 2, 512], F32, 2)
        xr = Rot(P, st, "mp_xr", [32, 4, 512], F32, 2)
        sq, sq_d = P.sbuf(st, "mp_sq", [128, 6, 512], BF16)
        rstd, rstd_d = P.sbuf(st, "mp_rstd", [128, 512], F32)
        oq = Rot(P, st, "mp_oq", [128, 6, 512], BF16, 2)
        okv = Rot(P, st, "mp_okv", [128, 2, 512], BF16, 2)
        okr = Rot(P, st, "mp_okr", [32, 512], BF16, 2)
        t1, t1_d = P.sbuf(st, "mp_t1", [32, 512], F32)
        t2, t2_d = P.sbuf(st, "mp_t2", [32, 512], F32)
        ps, ps_d = P.psum(st, "mp_ps", [128, 512])
        dq, dkv, dkr = Dep("cqn", True), Dep("ckvn", True), Dep("krr", True)
        for ti in range(T // 512):
            sl = slice(ti * 512, (ti + 1) * 512)
            a, a_d = xq.next()
            b, b_d = xkv.next()
            r, r_d = xr.next()
            P.dma([lambda e: e.dma_start(out=a[:], in_=fm(cqT_d)[:, :, sl])], writes=[a_d])
            P.dma([lambda e: e.dma_start(out=b[:], in_=fm(ckvT_d)[:, :, sl])], writes=[b_d])
            P.dma([lambda e: e.dma_start(out=r[:, 0, :], in_=krT_d[0:32, sl]),
                   lambda e: e.dma_start(out=r[:, 1, :], in_=pkrT_d[0:32, sl]),
                   lambda e: e.dma_start(out=r[:, 2, :], in_=cos_d[0:32, sl]),
                   lambda e: e.dma_start(out=r[:, 3, :], in_=ssin_d[0:32, sl])], writes=[r_d])
            o1, o1_d = oq.next()
            o2, o2_d = okv.next()
            o3, o3_d = okr.next()
            rms_tile(P, C, a, a_d, 6, 512, 768, sq, sq_d, ps, ps_d, rstd, rstd_d, o1, o1_d)
            P.dma([lambda e: e.dma_start(out=fm(cqn_d)[:, :, sl], in_=o1[:])], reads=[o1_d], writes=[dq])
            rms_tile(P, C, b, b_d, 2, 512, 256, sq, sq_d, ps, ps_d, rstd, rstd_d, o2, o2_d)
            P.dma([lambda e: e.dma_start(out=fm(ckvn_d)[:, :, sl], in_=o2[:])], reads=[o2_d], writes=[dkv])
            P.op("vector", lambda e: e.tensor_tensor(out=t1[:], in0=r[:, 0, :], in1=r[:, 2, :], op=ALU.mult), reads=[r_d],
                 writes=[t1_d])
            P.op("gpsimd", lambda e: e.tensor_tensor(out=t2[:], in0=r[:, 1, :], in1=r[:, 3, :], op=ALU.mult), reads=[r_d],
                 writes=[t2_d])
            P.op("vector", lambda e: e.tensor_tensor(out=o3[:], in0=t1[:], in1=t2[:], op=ALU.add), reads=[t1_d, t2_d],
                 writes=[o3_d])
            P.dma([lambda e: e.dma_start(out=krr_d[:, sl], in_=o3[:])], reads=[o3_d], writes=[dkr])
        P.barrier()


MLA_SCALE = 96 ** -0.5


def stage_mla_attention(P, C, cqn_d, ckvn_all_d, krr_all_d, wuq_d, qnorm_d, wukv_d, kvnorm_d, cosq_d, ssinq_d, sel_d, oT_d):
    NKT = S_ALL // 128
    with ExitStack() as st:
        wq, wq_deps = load_weight_bf16(P, st, wuq_d, 768, 2048, scale_d=qnorm_d, name="ma_wq", piece=1024)
        wkv, wkv_deps = load_weight_bf16(P, st, wukv_d, 256, 2048, scale_d=kvnorm_d, name="ma_wkv", piece=1024)
        cqnR = Rot(P, st, "ma_cqn", [128, 6, 512], BF16, 2)
        tabR = Rot(P, st, "ma_tab", [128, 2, 512], F32, 2)
        sel, sel_sd = P.sbuf(st, "ma_sel", [128, 256], F32)
        P.dma([lambda e: e.dma_start(out=sel[:], in_=sel_d[:, :])], writes=[sel_sd])
        KT, KT_d = P.sbuf(st, "ma_KT", [96, S_ALL], BF16)
        P.dma([lambda e: e.dma_start(out=KT[64:96, :], in_=krr_all_d[:, :])], writes=[KT_d])
        Vt, Vt_d = P.sbuf(st, "ma_V", [128, NKT, 128], BF16)
        QT, QT_d = P.sbuf(st, "ma_QT", [96, TLOC], BF16)
        oT, oT_sd = P.sbuf(st, "ma_oT", [128, 8, TLOC], BF16)
        lat = Rot(P, st, "ma_lat", [128, 2, 1024], BF16, 2)
        t1, t1_d = P.sbuf(st, "ma_t1", [96, 512], F32)
        t2, t2_d = P.sbuf(st, "ma_t2", [96, 512], F32)
        pT = Rot(P, st, "ma_pT", [128, 512], BF16, 3)
        osb = Rot(P, st, "ma_osb", [128, 512], F32, 2)
        rcp = Rot(P, st, "ma_rcp", [128, 512], F32, 2)
        ps_s = [P.psum(st, "ma_pss%d" % i, [128, 512]) for i in range(3)]
        ps_o = [P.psum(st, "ma_pso%d" % i, [128, 512]) for i in range(2)]
        ps_g = [P.psum(st, "ma_psg%d" % i, [128, 512]) for i in range(3)]
        gi = [0]
        si = [0]
        for h in range(16):
            c, po = h // 2, (h % 2) * 64
            cb = h * 128
            P.op("gpsimd", lambda e: e.memset(Vt[:], 0.0), writes=[Vt_d])
            onescol = 64 if h % 2 == 0 else 0
            P.op("gpsimd", lambda e: e.memset(Vt[:, :, onescol:onescol + 1], 1.0), writes=[Vt_d])
            for kb in range(S_ALL // 1024):
                lt, lt_d = lat.next()
                P.dma([lambda e: e.dma_start(out=lt[:], in_=fm(ckvn_all_d)[:, :, kb * 1024:(kb + 1) * 1024])], writes=[lt_d])
                for half in range(2):
                    pg, pg_d = ps_g[gi[0] % 3]
                    gi[0] += 1
                    for kc in range(2):
                        P.op("tensor", lambda e, kc=kc: e.matmul(pg[0:64, :], lhsT=wkv[:, kc, cb:cb + 64],
                                                                 rhs=lt[:, kc, half * 512:(half + 1) * 512],
                                                                 start=(kc == 0), stop=(kc == 1)),
                             reads=[wkv_deps[kc], lt_d], writes=[pg_d])
                    k0 = kb * 1024 + half * 512
                    P.op("vector", lambda e: e.tensor_copy(out=KT[0:64, k0:k0 + 512], in_=pg[0:64, :]), reads=[pg_d],
                         writes=[KT_d])
                pg, pg_d = ps_g[gi[0] % 3]
                gi[0] += 1
                for kt in range(8):
                    for kc in range(2):
                        P.op("tensor", lambda e, kc=kc, kt=kt: e.matmul(pg[:, kt * 64:(kt + 1) * 64],
                                                                        lhsT=lt[:, kc, kt * 128:(kt + 1) * 128],
                                                                        rhs=wkv[:, kc, cb + 64:cb + 128],
                                                                        start=(kc == 0), stop=(kc == 1)),
                             reads=[wkv_deps[kc], lt_d], writes=[pg_d])
                P.op("scalar", lambda e: e.copy(out=Vt[:, kb * 8:(kb + 1) * 8, po:po + 64],
                                                in_=pg[:].rearrange("p (t d) -> p t d", d=64)),
                     reads=[pg_d], writes=[Vt_d])
            for qt in range(TLOC // 512):
                qs = slice(qt * 512, (qt + 1) * 512)
                cqn, cqn_sd = cqnR.next()
                tab, tab_d = tabR.next()
                P.dma([lambda e: e.dma_start(out=cqn[:], in_=fm(cqn_d)[:, :, qs])], writes=[cqn_sd])
                P.dma([lambda e: e.dma_start(out=tab[:, 0, :], in_=cosq_d[:, qs]),
                       lambda e: e.dma_start(out=tab[:, 1, :], in_=ssinq_d[:, qs])], writes=[tab_d])
                p1, p1_d = ps_g[gi[0] % 3]
                gi[0] += 1
                for kc in range(6):
                    P.op("tensor", lambda e, kc=kc: e.matmul(p1[0:96, :], lhsT=wq[:, kc, cb:cb + 96], rhs=cqn[:, kc, :],
                                                             start=(kc == 0), stop=(kc == 5)),
                         reads=[wq_deps[kc], cqn_sd], writes=[p1_d])
                p2, p2_d = ps_g[gi[0] % 3]
                gi[0] += 1
                for kc in range(6):
                    P.op("tensor", lambda e, kc=kc: e.matmul(p2[0:96, :], lhsT=wq[:, kc, cb + 32:cb + 128], rhs=cqn[:, kc, :],
                                                             start=(kc == 0), stop=(kc == 5)),
                         reads=[wq_deps[kc], cqn_sd], writes=[p2_d])
                P.op("scalar", lambda e: e.copy(out=QT[0:64, qs], in_=p1[0:64, :]), reads=[], writes=[QT_d, p1_d])
                P.op("vector", lambda e: e.tensor_tensor(out=t1[64:96, :], in0=p1[64:96, :], in1=tab[64:96, 0, :], op=ALU.mult),
                     reads=[p1_d, tab_d], writes=[t1_d])
                P.op("vector", lambda e: e.tensor_tensor(out=t2[64:96, :], in0=p2[64:96, :], in1=tab[64:96, 1, :], op=ALU.mult),
                     reads=[p2_d, tab_d], writes=[t2_d])
                P.op("vector", lambda e: e.tensor_tensor(out=QT[64:96, qs], in0=t1[64:96, :], in1=t2[64:96, :], op=ALU.add),
                     reads=[t1_d, t2_d], writes=[QT_d])
            for qt in range(TLOC // 512):
                qs = slice(qt * 512, (qt + 1) * 512)
                po_, po_d = ps_o[(h * 4 + qt) % 2]
                LOOK = 2
                qk = {}

                def issue_qk(kt2):
                    pb, pb_d = ps_s[si[0] % 3]
                    si[0] += 1
                    P.op("tensor", lambda e: e.matmul(pb[:], lhsT=KT[0:96, kt2 * 128:(kt2 + 1) * 128], rhs=QT[0:96, qs],
                                                      start=True, stop=True), reads=[KT_d, QT_d], writes=[pb_d])
                    qk[kt2] = (pb, pb_d)

                for kt2 in range(LOOK):
                    issue_qk(kt2)
                for kt in range(NKT):
                    if kt + LOOK < NKT:
                        issue_qk(kt + LOOK)
                    pss, pss_d = qk.pop(kt)
                    p_, p_d = pT.next()
                    P.op("scalar", lambda e: e.activation(out=p_[:], in_=pss[:], func=AF.Exp, scale=MLA_SCALE),
                         reads=[pss_d], writes=[p_d])
                    P.op("tensor", lambda e: e.matmul(po_[:], lhsT=Vt[:, kt, :], rhs=p_[:], start=(kt == 0),
                                                      stop=(kt == NKT - 1)), reads=[Vt_d, p_d], writes=[po_d])
                ob, ob_d = osb.next()
                P.op("vector", lambda e: e.tensor_copy(out=ob[:], in_=po_[:]), reads=[po_d], writes=[ob_d])
                pd, pd_d = ps_g[gi[0] % 3]
                gi[0] += 1
                so = (h % 2) * 128
                P.op("tensor", lambda e: e.matmul(pd[:], lhsT=sel[:, so:so + 128], rhs=ob[:], start=True, stop=True),
                     reads=[sel_sd, ob_d], writes=[pd_d])
                rc, rc_d = rcp.next()
                P.op("vector", lambda e: e.reciprocal(out=rc[:], in_=pd[:]), reads=[pd_d], writes=[rc_d])
                P.op("vector", lambda e: e.tensor_tensor(out=oT[po:po + 64, c, qs], in0=ob[po:po + 64, :],
                                                         in1=rc[po:po + 64, :], op=ALU.mult),
                     reads=[ob_d, rc_d], writes=[oT_sd])
        P.dma([lambda e: e.dma_start(out=fm(oT_d)[:, :, :], in_=oT[:])], reads=[oT_sd], writes=[Dep("maoT", True)])
        P.barrier()


RCH = 128
RNCH = S_ALL // RCH
RGRP = 4


def ret_consts_host(h):
    f32 = np.float32
    lgs = np.log1p(-np.exp2(-5.0 - np.arange(8, dtype=f32))).astype(f32)
    lg_f, lg_b = lgs[h], lgs[7 - h]
    i = np.arange(128, dtype=f32)
    m, c = i[:, None], i[None, :]
    dmT_f = np.where(c >= m, np.exp(np.where(c >= m, c - m, 0) * lg_f), 0).astype(f32)
    dmT_b = np.where(m > c, np.exp(np.where(m > c, m - c, 0) * lg_b), 0).astype(f32)
    qdec_f = np.broadcast_to(np.exp((i + 1) * lg_f)[None, :], (128, 128)).astype(f32)
    qdec_b = np.broadcast_to(np.exp((128 - i) * lg_b)[None, :], (128, 128)).astype(f32)
    cols = np.zeros((128, 4), f32)
    cols[:, 0] = np.exp((127 - i) * lg_f)
    cols[:, 1] = np.exp(i * lg_b)
    cols[:, 2] = np.exp(128 * lg_f)
    cols[:, 3] = np.exp(128 * lg_b)
    return np.ascontiguousarray(np.concatenate([np.eye(128, dtype=f32), dmT_f, dmT_b, qdec_f, qdec_b, cols], axis=1))


def stage_retention(P, q_d, k_d, v_d, gate_d, cos_d, sin_d, rc_d, gnw_d, of_d, y_d):
    with ExitStack() as st:
        rc, rc_dd = P.sbuf(st, "rt_rc", [128, 644], F32)
        P.dma([lambda e: e.dma_start(out=rc[:], in_=rc_d[:, :])], writes=[rc_dd])
        gnw, gnw_dd = P.sbuf(st, "rt_gnw", [128, RGRP, 256], F32)
        P.dma([lambda e: e.dma_start(out=gnw[:], in_=gnw_d[:, :, :])], writes=[gnw_dd])
        ident = rc[:, 0:128]
        S, S_d = P.sbuf(st, "rt_S", [128, 256], F32)
        pp = PsPool(P, st, "rt_ps")
        qg = Rot(P, st, "rt_qg", [128, RGRP, 128], F32, 2)
        kg = Rot(P, st, "rt_kg", [128, RGRP, 128], F32, 2)
        vg = Rot(P, st, "rt_vg", [128, RGRP, 256], F32, 2)
        cg = Rot(P, st, "rt_cg", [128, RGRP, 64], F32, 2)
        sg = Rot(P, st, "rt_sg", [128, RGRP, 64], F32, 2)
        qr = Rot(P, st, "rt_qr", [128, RGRP, 128], F32, 2)
        kr = Rot(P, st, "rt_kr", [128, RGRP, 128], F32, 2)
        ta = Rot(P, st, "rt_ta", [128, RGRP, 64], F32, 2)
        tb_ = Rot(P, st, "rt_tb", [128, RGRP, 64], F32, 2)
        og = Rot(P, st, "rt_og", [128, RGRP, 256], F32, 2)
        ofg = Rot(P, st, "rt_ofg", [128, RGRP, 256], F32, 2)
        gg = Rot(P, st, "rt_gg", [128, RGRP, 256], F32, 2)
        sqg, sqg_d = P.sbuf(st, "rt_sqg", [128, RGRP, 256], F32)
        stat = Rot(P, st, "rt_stat", [128, RGRP], F32, 4)
        Qtb = Rot(P, st, "rt_Qt", [128, 128], F32, 3)
        Ktb = Rot(P, st, "rt_Kt", [128, 128], F32, 3)
        ATb = Rot(P, st, "rt_AT", [128, 128], F32, 3)
        Qdb = Rot(P, st, "rt_Qd", [128, 128], F32, 3)
        Kdb = Rot(P, st, "rt_Kd", [128, 128], F32, 3)
        ofd, yd = Dep("rof", True), Dep("ry", True)
        cpi = [0]

        def evac(out_ap, out_d, ps, ps_d):
            if cpi[0] % 2 == 0:
                P.op("scalar", lambda e: e.copy(out=out_ap, in_=ps), reads=[ps_d], writes=[out_d])
            else:
                P.op("vector", lambda e: e.tensor_copy(out=out_ap, in_=ps), reads=[ps_d], writes=[out_d])
            cpi[0] += 1

        def rows(dt, t0, w):
            return dt[t0:t0 + RGRP * 128, :].rearrange("(n p) d -> p n d", p=128)

        for d in range(2):
            dmT = rc[:, 128 + 128 * d:256 + 128 * d]
            qdec = rc[:, 384 + 128 * d:512 + 128 * d]
            kdec = rc[:, 640 + d:641 + d]
            gcol = rc[:, 642 + d:643 + d]
            P.op("vector", lambda e: e.memset(S[:], 0.0), writes=[S_d])
            order = list(range(RNCH)) if d == 0 else list(range(RNCH - 1, -1, -1))
            groups = [order[i:i + RGRP] for i in range(0, RNCH, RGRP)]
            state = {}

            def load_group(gi):
                c0 = min(groups[gi])
                t0 = c0 * 128
                q, q_dd = qg.next()
                k, k_dd = kg.next()
                v, v_dd = vg.next()
                cs, cs_dd = cg.next()
                sn, sn_dd = sg.next()
                P.dma([lambda e: e.dma_start(out=q[:], in_=rows(q_d, t0, 128))], writes=[q_dd])
                P.dma([lambda e: e.dma_start(out=k[:], in_=rows(k_d, t0, 128))], writes=[k_dd])
                P.dma([lambda e: e.dma_start(out=v[:], in_=rows(v_d, t0, 256))], writes=[v_dd])
                P.dma([lambda e: e.dma_start(out=cs[:], in_=rows(cos_d, t0, 64))], writes=[cs_dd])
                P.dma([lambda e: e.dma_start(out=sn[:], in_=rows(sin_d, t0, 64))], writes=[sn_dd])
                r = dict(c0=c0, v=(v, v_dd), o=og.next())
                if d == 1:
                    of_, of_dd = ofg.next()
                    g_, g_dd = gg.next()
                    P.dma([lambda e: e.dma_start(out=of_[:], in_=rows(of_d, t0, 256))], writes=[of_dd])
                    P.dma([lambda e: e.dma_start(out=g_[:], in_=rows(gate_d, t0, 256))], writes=[g_dd])
                    r["of"] = (of_, of_dd)
                    r["g"] = (g_, g_dd)
                outs = []
                for (x, x_dd, rot, scale) in ((q, q_dd, qr, None), (k, k_dd, kr, 128.0 ** -0.5)):
                    y, y_dd = rot.next()
                    a, a_dd = ta.next()
                    b, b_dd = tb_.next()
                    x1, x2 = x[:, :, 0:64], x[:, :, 64:128]
                    P.op("vector", lambda e: e.tensor_tensor(out=a[:], in0=x1, in1=cs[:], op=ALU.mult), reads=[x_dd, cs_dd],
                         writes=[a_dd])
                    P.op("gpsimd", lambda e: e.tensor_tensor(out=b[:], in0=x2, in1=sn[:], op=ALU.mult), reads=[x_dd, sn_dd],
                         writes=[b_dd])
                    P.op("vector", lambda e: e.tensor_tensor(out=y[:, :, 0:64], in0=a[:], in1=b[:], op=ALU.subtract),
                         reads=[a_dd, b_dd], writes=[y_dd])
                    a, a_dd = ta.next()
                    b, b_dd = tb_.next()
                    P.op("gpsimd", lambda e: e.tensor_tensor(out=a[:], in0=x2, in1=cs[:], op=ALU.mult), reads=[x_dd, cs_dd],
                         writes=[a_dd])
                    P.op("vector", lambda e: e.tensor_tensor(out=b[:], in0=x1, in1=sn[:], op=ALU.mult), reads=[x_dd, sn_dd],
                         writes=[b_dd])
                    P.op("gpsimd", lambda e: e.tensor_tensor(out=y[:, :, 64:128], in0=a[:], in1=b[:], op=ALU.add),
                         reads=[a_dd, b_dd], writes=[y_dd])
                    if scale is not None:
                        P.op("gpsimd", lambda e: e.tensor_scalar(out=y[:], in0=y[:], scalar1=scale, scalar2=None, op0=ALU.mult),
                             reads=[y_dd], writes=[y_dd])
                    outs.append((y, y_dd))
                r["q"], r["k"] = outs
                state[gi] = r

            def prep(gi, n):
                r = state[gi]
                j = n - r["c0"]
                q, q_dd = r["q"]
                k, k_dd = r["k"]
                pq, pq_d = pp.next(128, 128)
                P.op("tensor", lambda e: e.transpose(pq, q[:, j, :], ident), reads=[q_dd, rc_dd], writes=[pq_d])
                Qt, Qt_d = Qtb.next()
                evac(Qt[:], Qt_d, pq, pq_d)
                pk, pk_d = pp.next(128, 128)
                P.op("tensor", lambda e: e.transpose(pk, k[:, j, :], ident), reads=[k_dd, rc_dd], writes=[pk_d])
                Kt, Kt_d = Ktb.next()
                evac(Kt[:], Kt_d, pk, pk_d)
                pin, pin_d = pp.next(128, 128)
                P.op("tensor", lambda e: e.matmul(pin, lhsT=Kt[:], rhs=Qt[:], start=True, stop=True), reads=[Kt_d, Qt_d],
                     writes=[pin_d])
                AT, AT_d = ATb.next()
                P.op("vector", lambda e: e.tensor_tensor(out=AT[:], in0=pin, in1=dmT, op=ALU.mult), reads=[pin_d, rc_dd],
                     writes=[AT_d])
                Qd, Qd_d = Qdb.next()
                P.op("gpsimd", lambda e: e.tensor_tensor(out=Qd[:], in0=Qt[:], in1=qdec, op=ALU.mult), reads=[Qt_d, rc_dd],
                     writes=[Qd_d])
                Kd, Kd_d = Kdb.next()
                P.op("gpsimd", lambda e: e.tensor_scalar(out=Kd[:], in0=k[:, j, :], scalar1=kdec, scalar2=None, op0=ALU.mult),
                     reads=[k_dd, rc_dd], writes=[Kd_d])
                return dict(j=j, AT=(AT, AT_d), Qd=(Qd, Qd_d), Kd=(Kd, Kd_d))

            def scan(gi, n, pr):
                r = state[gi]
                j = pr["j"]
                AT, AT_d = pr["AT"]
                Qd, Qd_d = pr["Qd"]
                Kd, Kd_d = pr["Kd"]
                v, v_dd = r["v"]
                o, o_d = r["o"]
                po, po_d = pp.next(128, 256)
                P.op("tensor", lambda e: e.matmul(po, lhsT=AT[:], rhs=v[:, j, :], start=True, stop=False), reads=[AT_d, v_dd],
                     writes=[po_d])
                P.op("tensor", lambda e: e.matmul(po, lhsT=Qd[:], rhs=S[:], start=False, stop=True), reads=[Qd_d, S_d],
                     writes=[po_d])
                ps, ps_d = pp.next(128, 256)
                P.op("tensor", lambda e: e.matmul(ps, lhsT=Kd[:], rhs=v[:, j, :], start=True, stop=True), reads=[Kd_d, v_dd],
                     writes=[ps_d])
                P.op("scalar", lambda e: e.copy(out=o[:, j, :], in_=po), reads=[po_d], writes=[o_d])
                P.op("vector", lambda e: e.scalar_tensor_tensor(out=S[:], in0=S[:], scalar=gcol, in1=ps, op0=ALU.mult,
                                                                op1=ALU.add), reads=[S_d, rc_dd, ps_d], writes=[S_d])

            def finish_group(gi):
                r = state.pop(gi)
                t0 = r["c0"] * 128
                o, o_d = r["o"]
                if d == 0:
                    P.dma([lambda e: e.dma_start(out=rows(of_d, t0, 256), in_=o[:])], reads=[o_d], writes=[ofd])
                    return
                of_, of_dd = r["of"]
                g_, g_dd = r["g"]
                mu, mu_d = stat.next()
                var, var_d = stat.next()
                P.op("gpsimd", lambda e: e.tensor_tensor(out=o[:], in0=o[:], in1=of_[:], op=ALU.add), reads=[o_d, of_dd],
                     writes=[o_d])
                P.op("vector", lambda e: e.tensor_reduce(out=mu[:], in_=o[:], axis=AX.X, op=ALU.add), reads=[o_d], writes=[mu_d])
                P.op("vector", lambda e: e.tensor_scalar(out=mu[:], in0=mu[:], scalar1=1.0 / 256, scalar2=None, op0=ALU.mult),
                     reads=[mu_d], writes=[mu_d])
                for c in range(RGRP):
                    P.op("vector", lambda e, c=c: e.tensor_scalar(out=o[:, c, :], in0=o[:, c, :], scalar1=mu[:, c:c + 1],
                                                                  scalar2=None, op0=ALU.subtract),
                         reads=[o_d, mu_d], writes=[o_d])
                P.op("scalar", lambda e: e.activation(out=sqg[:], in_=o[:], func=AF.Square), reads=[o_d], writes=[sqg_d])
                P.op("vector", lambda e: e.tensor_reduce(out=var[:], in_=sqg[:], axis=AX.X, op=ALU.add), reads=[sqg_d],
                     writes=[var_d])
                P.op("scalar", lambda e: e.activation(out=var[:], in_=var[:], func=AF.Sqrt, scale=1.0 / 256, bias=EPS),
                     reads=[var_d], writes=[var_d])
                P.op("vector", lambda e: e.reciprocal(out=var[:], in_=var[:]), reads=[var_d], writes=[var_d])
                P.op("scalar", lambda e: e.activation(out=g_[:], in_=g_[:], func=AF.Silu), reads=[g_dd], writes=[g_dd])
                P.op("gpsimd", lambda e: e.tensor_tensor(out=g_[:], in0=g_[:], in1=gnw[:], op=ALU.mult), reads=[g_dd, gnw_dd],
                     writes=[g_dd])
                for c in range(RGRP):
                    P.op("vector", lambda e, c=c: e.scalar_tensor_tensor(out=o[:, c, :], in0=o[:, c, :], scalar=var[:, c:c + 1],
                                                                         in1=g_[:, c, :], op0=ALU.mult, op1=ALU.mult),
                         reads=[o_d, var_d, g_dd], writes=[o_d])
                P.dma([lambda e: e.dma_start(out=rows(y_d, t0, 256), in_=o[:])], reads=[o_d], writes=[yd])

            flat = [(gi, n) for gi, ch in enumerate(groups) for n in ch]
            load_group(0)
            pending = None
            for idx, (gi, n) in enumerate(flat):
                pr = prep(gi, n)
                if pending is not None:
                    pgi, pn, ppr = pending
                    scan(pgi, pn, ppr)
                    if pn == groups[pgi][-1]:
                        finish_group(pgi)
                if n == groups[gi][0] and gi + 1 < len(groups):
                    load_group(gi + 1)
                pending = (gi, n, pr)
            pgi, pn, ppr = pending
            scan(pgi, pn, ppr)
            finish_group(pgi)
            P.barrier()
        P.barrier()


def stage_final_norm(P, C, hT_d, w_d, out_d, T):
    with ExitStack() as st:
        w, w_dd = P.sbuf(st, "fn_w", [128, 8], F32)
        P.dma([lambda e: e.dma_start(out=w[:], in_=w_d[:, :])], writes=[w_dd])
        xs = Rot(P, st, "fn_x", [128, 8, 512], F32, 2)
        os_ = Rot(P, st, "fn_o", [128, 8, 512], F32, 2)
        sq, sq_d = P.sbuf(st, "fn_sq", [128, 8, 512], BF16)
        rstd, rstd_d = P.sbuf(st, "fn_rstd", [128, 512], F32)
        ps, ps_d = P.psum(st, "fn_ps", [128, 512])
        od = Dep("fnout", True)
        for ti in range(T // 512):
            sl = slice(ti * 512, (ti + 1) * 512)
            x, x_d = xs.next()
            o, o_d = os_.next()
            P.dma([lambda e: e.dma_start(out=x[:], in_=fm(hT_d)[:, :, sl])], writes=[x_d])
            P.op("scalar", lambda e: e.activation(out=sq[:], in_=x[:], func=AF.Square), reads=[x_d], writes=[sq_d])
            for k in range(8):
                P.op("tensor", lambda e, k=k: e.matmul(ps[:], lhsT=C.ones_bf[:], rhs=sq[:, k, :], start=(k == 0), stop=(k == 7)),
                     reads=[sq_d, C.ones_bf_d], writes=[ps_d])
            P.op("scalar", lambda e: e.activation(out=rstd[:], in_=ps[:], func=AF.Sqrt, scale=1.0 / 1024, bias=EPS),
                 reads=[ps_d], writes=[rstd_d])
            P.op("vector", lambda e: e.reciprocal(out=rstd[:], in_=rstd[:]), reads=[rstd_d], writes=[rstd_d])
            for k in range(8):
                P.op("vector", lambda e, k=k: e.scalar_tensor_tensor(out=o[:, k, :], in0=x[:, k, :], scalar=w[:, k:k + 1],
                                                                     in1=rstd[:], op0=ALU.mult, op1=ALU.mult),
                     reads=[x_d, w_dd, rstd_d], writes=[o_d])
            P.dma([lambda e: e.dma_start(out=fm(out_d)[:, :, sl], in_=o[:])], reads=[o_d], writes=[od])
        P.barrier()


def build_launch_c():
    nc = bass.Bass("TRN2", target_bir_lowering=False)
    ei = lambda n, s, dt=F32: dram(nc, n, s, dt, "ExternalInput")
    eo = lambda n, s, dt=F32: dram(nc, n, s, dt, "ExternalOutput")
    h1T = ei("h1T", [1024, TLOC])
    ogT = ei("ogT", [1024, TLOC])
    g_wo = ei("g_wo", [1024, 1024])
    mlp_norm = ei("mlp_norm", [128, 8])
    w1 = ei("w1", [1024, 4096])
    w2 = ei("w2", [4096, 1024])
    mla_norm = ei("mla_norm", [128, 8])
    mla_w_in = ei("mla_w_in", [1024, 1280])
    cos_kr = ei("cos_kr", [32, TLOC])
    ssin_kr = ei("ssin_kr", [32, TLOC])
    h2T = eo("h2T", [1024, TLOC])
    cqn = eo("cqn", [768, TLOC], BF16)
    ckvn = eo("ckvn", [256, TLOC], BF16)
    krr = eo("krr", [32, TLOC], BF16)
    hmT = dram(nc, "c_hmT", [1024, TLOC], F32)
    aT = dram(nc, "c_aT", [4096, TLOC], BF16)
    cqT = dram(nc, "c_cqT", [768, TLOC], F32)
    ckvT = dram(nc, "c_ckvT", [256, TLOC], F32)
    krT = dram(nc, "c_krT", [128, TLOC], F32)
    pkrT = dram(nc, "c_pkrT", [128, TLOC], F32)
    with ExitStack() as es:
        P = Prog(nc, es)
        C = Consts(P, es)
        dd = lambda n: Dep(n, dram=True)
        stage_linear(P, C, "gwo", ogT, 1024, TLOC, g_wo, 1024,
                     [dict(kind="fm", n0=0, n1=1024, dst=hmT, dst_dep=dd("hmT"), epi="resadd", res=h1T, dtype=F32)])
        stage_mlp(P, C, "mlp1", hmT, TLOC, w1, w2, mlp_norm, aT, h2T, dd("h2T"))
        stage_linear(P, C, "min", h2T, 1024, TLOC, mla_w_in, 1280,
                     [dict(kind="fm", n0=0, n1=768, dst=cqT, dst_dep=dd("cq"), dtype=F32),
                      dict(kind="fm", n0=768, n1=1024, dst=ckvT, dst_dep=dd("ckv"), dtype=F32),
                      dict(kind="fm", n0=1024, n1=1152, dst=krT, dst_dep=dd("kr"), dtype=F32),
                      dict(kind="fm", n0=1152, n1=1280, dst=pkrT, dst_dep=dd("pkr"), dtype=F32)],
                     nscale_d=mla_norm, norm=True)
        stage_mla_prep(P, C, cqT, ckvT, krT, pkrT, cos_kr, ssin_kr, cqn, ckvn, krr, TLOC)
        print("launch C instructions:", P.nins)
    return nc


def build_launch_d():
    nc = bass.Bass("TRN2", target_bir_lowering=False)
    ei = lambda n, s, dt=F32: dram(nc, n, s, dt, "ExternalInput")
    eo = lambda n, s, dt=F32: dram(nc, n, s, dt, "ExternalOutput")
    h2T = ei("h2T", [1024, TLOC])
    cqn = ei("cqn", [768, TLOC], BF16)
    ckvn_all = ei("ckvn_all", [256, S_ALL], BF16)
    krr_all = ei("krr_all", [32, S_ALL], BF16)
    wuq = ei("wuq", [768, 2048])
    qnorm = ei("qnorm", [128, 6])
    wukv = ei("wukv", [256, 2048])
    kvnorm = ei("kvnorm", [128, 2])
    cosq = ei("cosq", [128, TLOC])
    ssinq = ei("ssinq", [128, TLOC])
    sel = ei("sel", [128, 256])
    m_wo = ei("m_wo", [1024, 1024])
    mlp_norm = ei("mlp_norm", [128, 8])
    w1 = ei("w1", [1024, 4096])
    w2 = ei("w2", [4096, 1024])
    ret_norm = ei("ret_norm", [128, 8])
    ret_w_in = ei("ret_w_in", [1024, 6144])
    h3T = eo("h3T", [1024, TLOC])
    r_q = eo("r_q", [TLOC, 1024])
    r_k = eo("r_k", [TLOC, 1024])
    r_v = eo("r_v", [TLOC, 2048])
    r_g = eo("r_g", [TLOC, 2048])
    oT = dram(nc, "d_oT", [1024, TLOC], BF16)
    hmT = dram(nc, "d_hmT", [1024, TLOC], F32)
    aT = dram(nc, "d_aT", [4096, TLOC], BF16)
    with ExitStack() as es:
        P = Prog(nc, es)
        C = Consts(P, es)
        dd = lambda n: Dep(n, dram=True)
        stage_mla_attention(P, C, cqn, ckvn_all, krr_all, wuq, qnorm, wukv, kvnorm, cosq, ssinq, sel, oT)
        stage_linear(P, C, "mwo", oT, 1024, TLOC, m_wo, 1024,
                     [dict(kind="fm", n0=0, n1=1024, dst=hmT, dst_dep=dd("hmT"), epi="resadd", res=h2T, dtype=F32)],
                     src_dtype=BF16)
        stage_mlp(P, C, "mlp2", hmT, TLOC, w1, w2, mlp_norm, aT, h3T, dd("h3T"))
        stage_linear(P, C, "rin", h3T, 1024, TLOC, ret_w_in, 6144,
                     [dict(kind="tm", n0=0, n1=1024, dst=r_q, dst_dep=dd("rq"), dtype=F32),
                      dict(kind="tm", n0=1024, n1=2048, dst=r_k, dst_dep=dd("rk"), dtype=F32),
                      dict(kind="tm", n0=2048, n1=4096, dst=r_v, dst_dep=dd("rv"), dtype=F32),
                      dict(kind="tm", n0=4096, n1=6144, dst=r_g, dst_dep=dd("rg"), dtype=F32)],
                     nscale_d=ret_norm, norm=True, tile_T=256)
        print("launch D instructions:", P.nins)
    return nc


def build_launch_e():
    nc = bass.Bass("TRN2", target_bir_lowering=False)
    ei = lambda n, s, dt=F32: dram(nc, n, s, dt, "ExternalInput")
    q = ei("q_tm", [S_ALL, 128])
    k = ei("k_tm", [S_ALL, 128])
    v = ei("v_tm", [S_ALL, 256])
    g = ei("gate_tm", [S_ALL, 256])
    cos = ei("cos_tm", [S_ALL, 64])
    sin = ei("sin_tm", [S_ALL, 64])
    rc = ei("rconst", [128, 644])
    gnw = ei("gnw", [128, RGRP, 256])
    y = dram(nc, "y_tm", [S_ALL, 256], F32, "ExternalOutput")
    of = dram(nc, "e_of", [S_ALL, 256], F32)
    with ExitStack() as es:
        P = Prog(nc, es)
        stage_retention(P, q, k, v, g, cos, sin, rc, gnw, of, y)
        print("launch E instructions:", P.nins)
    return nc


def build_launch_f():
    nc = bass.Bass("TRN2", target_bir_lowering=False)
    ei = lambda n, s, dt=F32: dram(nc, n, s, dt, "ExternalInput")
    yT = ei("yT", [2048, TLOC])
    h3T = ei("h3T", [1024, TLOC])
    r_wo = ei("r_wo", [2048, 1024])
    mlp_norm = ei("mlp_norm", [128, 8])
    w1 = ei("w1", [1024, 4096])
    w2 = ei("w2", [4096, 1024])
    fnorm = ei("fnorm", [128, 8])
    outT = dram(nc, "outT", [1024, TLOC], F32, "ExternalOutput")
    hmT = dram(nc, "f_hmT", [1024, TLOC], F32)
    h4T = dram(nc, "f_h4T", [1024, TLOC], F32)
    aT = dram(nc, "f_aT", [4096, TLOC], BF16)
    with ExitStack() as es:
        P = Prog(nc, es)
        C = Consts(P, es)
        dd = lambda n: Dep(n, dram=True)
        stage_linear(P, C, "rwo", yT, 2048, TLOC, r_wo, 1024,
                     [dict(kind="fm", n0=0, n1=1024, dst=hmT, dst_dep=dd("hmT"), epi="resadd", res=h3T, dtype=F32)],
                     tile_T=256)
        stage_mlp(P, C, "mlp3", hmT, TLOC, w1, w2, mlp_norm, aT, h4T, dd("h4T"))
        stage_final_norm(P, C, h4T, fnorm, outT, TLOC)
        print("launch F instructions:", P.nins)
    return nc


def _run(nc, in_maps):
    res = run_bass_kernel_spmd(nc, in_maps, core_ids=list(range(NCORES)))
    return res.results


def gdn_in_maps(qkvT_full, z_tm_full, gates_full, inp):
    gcs = gdn_consts_host()
    conv = inp['gdn_conv'][0]
    maps = []
    for hd in range(8):
        qkv_pre = np.stack([qkvT_full[j * 1024 + hd * 128: j * 1024 + (hd + 1) * 128] for j in range(3)], axis=1)
        ztm = np.ascontiguousarray(z_tm_full[:, hd * 128:(hd + 1) * 128])
        gt = gates_full[:, [hd, 8 + hd, 16 + hd, 24 + hd]]
        gt = np.ascontiguousarray(gt.reshape(NCH, 64, 4).transpose(1, 0, 2))
        cw = np.stack([conv[j * 1024 + hd * 128: j * 1024 + (hd + 1) * 128] for j in range(3)], axis=1)
        sc = np.array([inp['gdn_a_log_f'][0][hd], inp['gdn_a_log_b'][0][hd], inp['gdn_dt_bias_f'][0][hd],
                       inp['gdn_dt_bias_b'][0][hd]], np.float32)
        sc = np.ascontiguousarray(np.broadcast_to(sc[None], (64, 4)))
        on = np.ascontiguousarray(np.broadcast_to(inp['gdn_o_norm'][0][None, None], (64, GRP, 128)))
        maps.append(dict(qkv_pre=np.ascontiguousarray(qkv_pre), z_tm=ztm, gates=gt, conv_w=np.ascontiguousarray(cw),
                         scal=sc, onorm=on, gconst=gcs))
    return maps


def rope_tables(pos, half):
    inv = (1.0 / (10000.0 ** (np.arange(half, dtype=np.float32) / half))).astype(np.float32)
    ang = pos.astype(np.float32)[:, None] * inv[None, :]
    return np.cos(ang).astype(np.float32), np.sin(ang).astype(np.float32)


def kernel(**inp):
    inp = {k: np.asarray(v) for k, v in inp.items()}
    f32 = np.float32
    cores = range(NCORES)
    xTs, biases = na_host_prep(inp['x'][0], inp['na_rpb'][0])
    ra = _run(build_launch_a(), [dict(
        xT=xTs[c], na_norm=pc(inp['na_norm'][0]), w_qkv=inp['na_w_qkv'][0], w_o=inp['na_w_o'][0], bias=biases[c],
        mlp_norm=pc(inp['mlp_norm'][0]), w1=inp['mlp_w1'][0], w2=inp['mlp_w2'][0],
        gdn_norm=pc(inp['gdn_norm'][0]), gdn_w_in=inp['gdn_w_in'][0]) for c in cores])
    qkvT = np.concatenate([ra[c]["g_qkvT"] for c in cores], axis=1)
    ztm = np.concatenate([ra[c]["g_ztm"] for c in cores], axis=0)
    gates = np.concatenate([ra[c]["g_gates"] for c in cores], axis=0)
    rb = _run(build_launch_b(), gdn_in_maps(qkvT, ztm, gates, inp))
    og = np.stack([rb[c]["og"] for c in cores], axis=1).reshape(S_ALL, 1024)
    pos = np.arange(S_ALL)
    c16, s16 = rope_tables(pos, 16)
    cos32 = np.concatenate([c16, c16], axis=1).T
    ssin32 = np.concatenate([-s16, s16], axis=1).T
    w_in = inp['mla_w_in'][0]
    kr_w = w_in[:, 1024:1056]
    pkr_w = np.concatenate([kr_w[:, 16:32], kr_w[:, 0:16]], axis=1)
    zpad = np.zeros((1024, 96), f32)
    w_in_ext = np.ascontiguousarray(np.concatenate([w_in[:, :1024], kr_w, zpad, pkr_w, zpad], axis=1))
    rc_ = _run(build_launch_c(), [dict(
        h1T=ra[c]["h1T"], ogT=np.ascontiguousarray(og[c * TLOC:(c + 1) * TLOC].T), g_wo=inp['gdn_w_o'][0],
        mlp_norm=pc(inp['mlp_norm'][1]), w1=inp['mlp_w1'][1], w2=inp['mlp_w2'][1],
        mla_norm=pc(inp['mla_norm'][0]), mla_w_in=w_in_ext,
        cos_kr=np.ascontiguousarray(cos32[:, c * TLOC:(c + 1) * TLOC]),
        ssin_kr=np.ascontiguousarray(ssin32[:, c * TLOC:(c + 1) * TLOC])) for c in cores])
    ckvn_all = np.ascontiguousarray(np.concatenate([rc_[c]["ckvn"] for c in cores], axis=1))
    krr_all = np.ascontiguousarray(np.concatenate([rc_[c]["krr"] for c in cores], axis=1))
    wuq = inp['mla_w_uq'][0].reshape(768, 16, 96)
    wuq_ext = np.ascontiguousarray(np.concatenate(
        [wuq, wuq[:, :, 80:96], wuq[:, :, 64:80]], axis=2).reshape(768, 2048))
    cosq = np.zeros((128, S_ALL), f32)
    ssinq = np.zeros((128, S_ALL), f32)
    cosq[64:96] = cos32
    ssinq[64:96] = ssin32
    sel = np.zeros((128, 256), f32)
    sel[64, 0:128] = 1.0
    sel[0, 128:256] = 1.0
    rd = _run(build_launch_d(), [dict(
        h2T=rc_[c]["h2T"], cqn=rc_[c]["cqn"], ckvn_all=ckvn_all, krr_all=krr_all, wuq=wuq_ext,
        qnorm=pc(inp['mla_q_norm'][0]), wukv=inp['mla_w_ukv'][0], kvnorm=pc(inp['mla_kv_norm'][0]),
        cosq=np.ascontiguousarray(cosq[:, c * TLOC:(c + 1) * TLOC]),
        ssinq=np.ascontiguousarray(ssinq[:, c * TLOC:(c + 1) * TLOC]), sel=sel, m_wo=inp['mla_w_o'][0],
        mlp_norm=pc(inp['mlp_norm'][2]), w1=inp['mlp_w1'][2], w2=inp['mlp_w2'][2],
        ret_norm=pc(inp['ret_norm'][0]), ret_w_in=inp['ret_w_in'][0]) for c in cores])
    r_q = np.concatenate([rd[c]["r_q"] for c in cores], axis=0)
    r_k = np.concatenate([rd[c]["r_k"] for c in cores], axis=0)
    r_v = np.concatenate([rd[c]["r_v"] for c in cores], axis=0)
    r_g = np.concatenate([rd[c]["r_g"] for c in cores], axis=0)
    c64, s64 = rope_tables(pos, 64)
    gn = inp['ret_gn'][0]
    re_ = _run(build_launch_e(), [dict(
        q_tm=np.ascontiguousarray(r_q[:, h * 128:(h + 1) * 128]), k_tm=np.ascontiguousarray(r_k[:, h * 128:(h + 1) * 128]),
        v_tm=np.ascontiguousarray(r_v[:, h * 256:(h + 1) * 256]), gate_tm=np.ascontiguousarray(r_g[:, h * 256:(h + 1) * 256]),
        cos_tm=c64, sin_tm=s64, rconst=ret_consts_host(h),
        gnw=np.ascontiguousarray(np.broadcast_to(gn[h * 256:(h + 1) * 256][None, None], (128, RGRP, 256)))) for h in cores])
    y = np.stack([re_[h]["y_tm"] for h in cores], axis=1).reshape(S_ALL, 2048)
    rf = _run(build_launch_f(), [dict(
        yT=np.ascontiguousarray(y[c * TLOC:(c + 1) * TLOC].T), h3T=rd[c]["h3T"], r_wo=inp['ret_w_o'][0],
        mlp_norm=pc(inp['mlp_norm'][3]), w1=inp['mlp_w1'][3], w2=inp['mlp_w2'][3],
        fnorm=pc(inp['final_norm'])) for c in cores])
    out = np.concatenate([rf[c]["outT"].T for c in cores], axis=0)
    return np.ascontiguousarray(out.reshape(1, S_ALL, 1024).astype(f32))
```
